# Optimizing a Trainium2 kernel written in Bass

```python
import jax
import jax.numpy as jnp
from jax import lax
import numpy as np

D_MODEL = 2048
BATCH = 4
SEQ = 4096
DEPTH = 4

N_MIXERS = 4
MEM_LEN = 256
HEAD_DIM = 128
MEM_HEADS = 4
MEM_WIDTH = MEM_HEADS * HEAD_DIM
MIX_WIDTH = D_MODEL - MEM_WIDTH
QB = 128
D_FF = -(-(8 * D_MODEL) // (3 * 256)) * 256
EPS = 1e-6
NEG_INF = -1e30

A_HEADS = MIX_WIDTH // HEAD_DIM
A_KV = 2
A_CMP_LEN = 32
A_CMP_STRIDE = 16
A_CMP_HID = 2 * HEAD_DIM
A_SEL_LEN = 64
A_TOP_N = 16
A_WINDOW = 512
A_SEL_QC = 64
A_FORCE = 1e6
A_EXCLUDE = -1e9

B_HEADS = MIX_WIDTH // HEAD_DIM

C_HEADS = 4
C_HEAD_DIM = MIX_WIDTH // C_HEADS
C_CONV = 4
C_CHUNK = 64

D_HEAD_DIM = 64
D_HEADS = MIX_WIDTH // D_HEAD_DIM
D_KV = D_HEADS // 8
D_WINDOW = 128

W_IN_A = A_HEADS * HEAD_DIM + 6 * A_KV * HEAD_DIM + 3 * A_HEADS + MEM_WIDTH
W_IN_B = 3 * MIX_WIDTH + MEM_WIDTH
W_IN_C = 4 * MIX_WIDTH + 2 * C_HEADS + MEM_WIDTH
W_IN_D = D_HEADS * D_HEAD_DIM + 2 * D_KV * D_HEAD_DIM + MEM_WIDTH

kernel_name = 'hybrid_nsa_stickbreak_mlstm_swa_decoder'


def rms_norm(x, g):
    xf = x.astype(jnp.float32)
    y = xf * lax.rsqrt(jnp.mean(xf * xf, axis=-1, keepdims=True) + EPS)
    return (y * g.astype(jnp.float32)).astype(x.dtype)


def split_cols(u, sizes):
    idx, acc = [], 0
    for s in sizes[:-1]:
        acc += s
        idx.append(acc)
    return jnp.split(u, idx, axis=-1)


def banded_attention(q, k, v, window, sinks=None):
    B, S, H, dh = q.shape
    G = k.shape[2]
    hpg = H // G
    nb = S // QB
    kw = window + QB
    kp = jnp.pad(k, ((0, 0), (window, 0), (0, 0), (0, 0)))
    vp = jnp.pad(v, ((0, 0), (window, 0), (0, 0), (0, 0)))
    qb = q.reshape(B, nb, QB, G, hpg, dh).transpose(1, 0, 2, 3, 4, 5)
    scale = dh ** -0.5

    def block(args):
        i, qi = args
        start = i * QB
        ki = lax.dynamic_slice_in_dim(kp, start, kw, axis=1)
        vi = lax.dynamic_slice_in_dim(vp, start, kw, axis=1)
        s = jnp.einsum('bqghd,bkgd->bghqk', qi, ki).astype(jnp.float32) * scale
        qpos = start + jnp.arange(QB)
        kpos = start - window + jnp.arange(kw)
        rel = qpos[:, None] - kpos[None, :]
        mask = (rel >= 0) & (rel < window) & (kpos[None, :] >= 0)
        s = jnp.where(mask, s, NEG_INF)
        if sinks is None:
            p = jax.nn.softmax(s, axis=-1)
        else:
            sk = sinks.astype(jnp.float32).reshape(G, hpg)[None, :, :, None, None]
            m = jnp.maximum(s.max(axis=-1, keepdims=True), sk)
            e = jnp.exp(s - m)
            p = e / (e.sum(axis=-1, keepdims=True) + jnp.exp(sk - m))
        return jnp.einsum('bghqk,bkgd->bqghd', p.astype(vi.dtype), vi)

    out = lax.map(block, (jnp.arange(nb), qb))
    return out.transpose(1, 0, 2, 3, 4, 5).reshape(B, S, H, dh)


def compress_blocks(blocks, pe, w1, w2):
    z = blocks + pe[:, None, :]
    z = jnp.moveaxis(z, 3, 2)
    z = z.reshape(z.shape[:3] + (-1,))
    return jax.nn.gelu(z @ w1) @ w2


def nsa_mixer(u, pe_k, w1_k, w2_k, pe_v, w1_v, w2_v):
    B, S, _ = u.shape
    H, G, dh = A_HEADS, A_KV, HEAD_DIM
    hpg = H // G
    scale = dh ** -0.5
    q, kv, gate = split_cols(u, [H * dh, 6 * G * dh, 3 * H])
    q = q.reshape(B, S, H, dh)
    kv = kv.reshape(B, S, 6, G, dh)
    k_cmp, v_cmp, k_slc, v_slc, k_win, v_win = [kv[:, :, j] for j in range(6)]
    qg = q.reshape(B, S, G, hpg, dh)
    pos = jnp.arange(S)

    n_cmp = (S - A_CMP_LEN) // A_CMP_STRIDE + 1
    starts = jnp.arange(n_cmp) * A_CMP_STRIDE
    blk_idx = starts[:, None] + jnp.arange(A_CMP_LEN)[None, :]
    kc = compress_blocks(k_cmp[:, blk_idx], pe_k, w1_k, w2_k)
    vc = compress_blocks(v_cmp[:, blk_idx], pe_v, w1_v, w2_v)
    s = jnp.einsum('bsghd,bngd->bghsn', qg, kc).astype(jnp.float32) * scale
    cmask = (starts[None, :] + A_CMP_LEN - 1) <= pos[:, None]
    p_cmp = jax.nn.softmax(jnp.where(cmask, s, NEG_INF), axis=-1) * cmask
    o_cmp = jnp.einsum('bghsn,bngd->bsghd', p_cmp.astype(vc.dtype), vc)

    n_slc = S // A_SEL_LEN
    blk = jnp.arange(n_slc)
    overlap = ((starts[:, None] < (blk[None, :] + 1) * A_SEL_LEN)
               & (starts[:, None] + A_CMP_LEN > blk[None, :] * A_SEL_LEN)).astype(jnp.float32)
    imp = jnp.einsum('bghsn,nj->bgsj', p_cmp, overlap)
    cur = pos // A_SEL_LEN
    valid = blk[None, :] <= cur[:, None]
    forced = (blk[None, :] == 0) | (blk[None, :] == cur[:, None]) | (blk[None, :] == cur[:, None] - 1)
    score = jnp.where(valid, jnp.where(forced, A_FORCE, imp), A_EXCLUDE)
    n_sel = min(A_TOP_N, n_slc)
    sel_val, sel_idx = lax.top_k(score, n_sel)
    sel_ok = sel_val > 0.5 * A_EXCLUDE

    kb = k_slc.reshape(B, n_slc, A_SEL_LEN, G, dh).transpose(0, 3, 1, 2, 4)
    vb = v_slc.reshape(B, n_slc, A_SEL_LEN, G, dh).transpose(0, 3, 1, 2, 4)
    nqc = S // A_SEL_QC

    def to_chunks(a):
        a = a.reshape(a.shape[:2] + (nqc, A_SEL_QC) + a.shape[3:])
        return jnp.moveaxis(a, 2, 0)

    q_sel = to_chunks(qg.transpose(0, 2, 1, 3, 4))
    idx_c = to_chunks(sel_idx)
    ok_c = to_chunks(sel_ok)
    bi = jnp.arange(B)[:, None, None, None]
    gi = jnp.arange(G)[None, :, None, None]

    def sel_block(args):
        c, qc, ic, okc = args
        kg = kb[bi, gi, ic]
        vg = vb[bi, gi, ic]
        sc = jnp.einsum('bgqhd,bgqnld->bgqhnl', qc, kg).astype(jnp.float32) * scale
        qpos = c * A_SEL_QC + jnp.arange(A_SEL_QC)
        kpos = ic[..., None] * A_SEL_LEN + jnp.arange(A_SEL_LEN)
        m = okc[..., None] & (kpos <= qpos[None, None, :, None, None])
        sc = jnp.where(m[:, :, :, None], sc, NEG_INF)
        sh = sc.shape
        p = jax.nn.softmax(sc.reshape(sh[:4] + (-1,)), axis=-1).reshape(sh)
        return jnp.einsum('bgqhnl,bgqnld->bgqhd', p.astype(vg.dtype), vg)

    o_slc = lax.map(sel_block, (jnp.arange(nqc), q_sel, idx_c, ok_c))
    o_slc = jnp.moveaxis(o_slc, 0, 2).reshape(B, G, S, hpg, dh).transpose(0, 2, 1, 3, 4)

    o_win = banded_attention(q, k_win, v_win, A_WINDOW).reshape(B, S, G, hpg, dh)

    g = jax.nn.sigmoid(gate.astype(jnp.float32)).reshape(B, S, G, hpg, 3)
    o = g[..., 0:1] * o_cmp + g[..., 1:2] * o_slc + g[..., 2:3] * o_win
    return o.reshape(B, S, H * dh).astype(u.dtype)


def stick_breaking_attention(q, k, v):
    B, S, H, dh = q.shape
    nb = S // QB
    scale = dh ** -0.5
    qb = q.reshape(B, nb, QB, H, dh).transpose(1, 0, 2, 3, 4)
    kpos = jnp.arange(S)

    def block(args):
        i, qi = args
        z = jnp.einsum('bqhd,bkhd->bhqk', qi, k).astype(jnp.float32) * scale
        qpos = i * QB + jnp.arange(QB)
        causal = kpos[None, :] < qpos[:, None]
        log_keep = jnp.where(causal, jax.nn.log_sigmoid(-z), 0.0)
        after = lax.cumsum(log_keep, axis=3, reverse=True) - log_keep
        a = jnp.where(causal, jnp.exp(jax.nn.log_sigmoid(z) + after), 0.0)
        return jnp.einsum('bhqk,bkhd->bqhd', a.astype(v.dtype), v)

    out = lax.map(block, (jnp.arange(nb), qb))
    return out.transpose(1, 0, 2, 3, 4).reshape(B, S, H, dh)


def stick_breaking_mixer(u):
    B, S, _ = u.shape
    q, k, v = split_cols(u, [MIX_WIDTH, MIX_WIDTH, MIX_WIDTH])
    shp = (B, S, B_HEADS, HEAD_DIM)
    o = stick_breaking_attention(q.reshape(shp), k.reshape(shp), v.reshape(shp))
    return o.reshape(B, S, MIX_WIDTH).astype(u.dtype)


def causal_depthwise_conv(x, w, b):
    c = x.shape[-1]
    kk = w.shape[0]
    y = lax.conv_general_dilated(x, w[:, None, :].astype(x.dtype), window_strides=(1,),
                                 padding=[(kk - 1, 0)], dimension_numbers=('NWC', 'WIO', 'NWC'),
                                 feature_group_count=c)
    return y + b.astype(x.dtype)


def mlstm_chunkwise(q, k, v, i_g, log_f):
    B, S, H, dh = q.shape
    L = C_CHUNK
    nc = S // L

    def chunks(a):
        a = a.reshape((B, nc, L) + a.shape[2:])
        a = jnp.moveaxis(a, 1, 0)
        return jnp.moveaxis(a, 3, 2)

    tril = jnp.tril(jnp.ones((L, L), dtype=bool))

    def step(carry, xs):
        Cm, nv, m = carry
        qc, kc, vc, ic, fc = xs
        b = jnp.cumsum(fc, axis=-1)
        Dm = jnp.where(tril, b[..., :, None] - b[..., None, :] + ic[..., None, :], NEG_INF)
        m_inter = b + m[..., None]
        m_t = jnp.maximum(m_inter, Dm.max(axis=-1))
        w = jnp.exp(Dm - m_t[..., None])
        a_inter = jnp.exp(m_inter - m_t)
        sqk = jnp.einsum('bhtd,bhsd->bhts', qc, kc) * w
        num = (a_inter[..., None] * jnp.einsum('bhvd,bhtd->bhtv', Cm, qc)
               + jnp.einsum('bhts,bhsv->bhtv', sqk, vc))
        den = a_inter * jnp.einsum('bhd,bhtd->bht', nv, qc) + sqk.sum(axis=-1)
        den = jnp.maximum(jnp.abs(den), jnp.exp(-m_t))
        h = num / den[..., None]
        bL = b[..., -1]
        g = bL[..., None] - b + ic
        m_new = jnp.maximum(bL + m, g.max(axis=-1))
        wk = jnp.exp(g - m_new[..., None])
        decay = jnp.exp(bL + m - m_new)
        Cm = decay[..., None, None] * Cm + jnp.einsum('bhs,bhsv,bhsd->bhvd', wk, vc, kc)
        nv = decay[..., None] * nv + jnp.einsum('bhs,bhsd->bhd', wk, kc)
        return (Cm, nv, m_new), h

    init = (jnp.zeros((B, H, dh, dh), jnp.float32), jnp.zeros((B, H, dh), jnp.float32),
            jnp.zeros((B, H), jnp.float32))
    _, hs = lax.scan(step, init, (chunks(q), chunks(k), chunks(v), chunks(i_g), chunks(log_f)))
    return hs.transpose(1, 0, 3, 2, 4).reshape(B, S, H, dh)


def mlstm_mixer(u, conv_w, conv_b, gate_b, head_norm):
    B, S, _ = u.shape
    H, dh = C_HEADS, C_HEAD_DIM
    q_pre, k_pre, v, o_pre, i_pre, f_pre = split_cols(
        u, [MIX_WIDTH, MIX_WIDTH, MIX_WIDTH, MIX_WIDTH, H, H])
    qk = jax.nn.silu(causal_depthwise_conv(jnp.concatenate([q_pre, k_pre], axis=-1), conv_w, conv_b))
    q, k = jnp.split(qk.astype(jnp.float32), 2, axis=-1)
    q = q.reshape(B, S, H, dh)
    k = k.reshape(B, S, H, dh) * (dh ** -0.5)
    v = v.astype(jnp.float32).reshape(B, S, H, dh)
    gb = gate_b.astype(jnp.float32)
    i_g = i_pre.astype(jnp.float32) + gb[:H]
    log_f = jax.nn.log_sigmoid(f_pre.astype(jnp.float32) + gb[H:])
    h = mlstm_chunkwise(q, k, v, i_g, log_f)
    h = jax.nn.sigmoid(o_pre.astype(jnp.float32)).reshape(B, S, H, dh) * h
    h = h * lax.rsqrt(jnp.mean(h * h, axis=-1, keepdims=True) + EPS)
    h = h * head_norm.astype(jnp.float32).reshape(H, dh)
    return h.reshape(B, S, H * dh).astype(u.dtype)


def sink_window_mixer(u, sinks):
    B, S, _ = u.shape
    q, k, v = split_cols(u, [D_HEADS * D_HEAD_DIM, D_KV * D_HEAD_DIM, D_KV * D_HEAD_DIM])
    o = banded_attention(q.reshape(B, S, D_HEADS, D_HEAD_DIM), k.reshape(B, S, D_KV, D_HEAD_DIM),
                         v.reshape(B, S, D_KV, D_HEAD_DIM), D_WINDOW, sinks)
    return o.reshape(B, S, MIX_WIDTH).astype(u.dtype)


def memory_attention(qm, mem_k, mem_v):
    B, S, _ = qm.shape
    q = qm.reshape(B, S, MEM_HEADS, HEAD_DIM)
    s = jnp.einsum('bshd,bmhd->bhsm', q, mem_k).astype(jnp.float32) * (HEAD_DIM ** -0.5)
    p = jax.nn.softmax(s, axis=-1)
    o = jnp.einsum('bhsm,bmhd->bshd', p.astype(mem_v.dtype), mem_v)
    return o.reshape(B, S, MEM_WIDTH)


def setup_inputs(seed: int = 0) -> dict:
    key = jax.random.key(seed)
    ks = iter(jax.random.split(key, 64))
    f32 = jnp.float32

    def nrm(shape, scale):
        return jax.random.normal(next(ks), shape, f32) * scale

    def gain(n):
        return 1.0 + 0.02 * jax.random.normal(next(ks), (n,), f32)

    d = D_MODEL
    inp = {}
    inp['x'] = nrm((BATCH, SEQ, d), 1.0)
    inp['mem'] = nrm((BATCH, MEM_LEN, d), 1.0)
    inp['mem_norm'] = gain(d)
    inp['mem_w_kv'] = nrm((d, 2 * MEM_WIDTH), d ** -0.5)

    def ffn(prefix):
        inp[prefix + 'norm_ffn'] = gain(d)
        inp[prefix + 'w_gate'] = nrm((d, D_FF), d ** -0.5)
        inp[prefix + 'w_up'] = nrm((d, D_FF), d ** -0.5)
        inp[prefix + 'w_down'] = nrm((D_FF, d), D_FF ** -0.5)

    cmp_in = A_CMP_LEN * HEAD_DIM
    inp['l0_norm_mix'] = gain(d)
    inp['l0_w_in'] = nrm((d, W_IN_A), d ** -0.5)
    inp['l0_cmp_pe_k'] = nrm((A_CMP_LEN, HEAD_DIM), 0.1)
    inp['l0_cmp_w1_k'] = nrm((cmp_in, A_CMP_HID), cmp_in ** -0.5)
    inp['l0_cmp_w2_k'] = nrm((A_CMP_HID, HEAD_DIM), A_CMP_HID ** -0.5)
    inp['l0_cmp_pe_v'] = nrm((A_CMP_LEN, HEAD_DIM), 0.1)
    inp['l0_cmp_w1_v'] = nrm((cmp_in, A_CMP_HID), cmp_in ** -0.5)
    inp['l0_cmp_w2_v'] = nrm((A_CMP_HID, HEAD_DIM), A_CMP_HID ** -0.5)
    inp['l0_w_out'] = nrm((d, d), d ** -0.5)
    ffn('l0_')

    inp['l1_norm_mix'] = gain(d)
    inp['l1_w_in'] = nrm((d, W_IN_B), d ** -0.5)
    inp['l1_w_out'] = nrm((d, d), d ** -0.5)
    ffn('l1_')

    inp['l2_norm_mix'] = gain(d)
    inp['l2_w_in'] = nrm((d, W_IN_C), d ** -0.5)
    inp['l2_conv_w'] = nrm((C_CONV, 2 * MIX_WIDTH), C_CONV ** -0.5)
    inp['l2_conv_b'] = nrm((2 * MIX_WIDTH,), 0.02)
    inp['l2_gate_b'] = jnp.concatenate([nrm((C_HEADS,), 0.1),
                                        jnp.linspace(3.0, 6.0, C_HEADS, dtype=f32) + nrm((C_HEADS,), 0.1)])
    inp['l2_head_norm'] = gain(MIX_WIDTH)
    inp['l2_w_out'] = nrm((d, d), d ** -0.5)
    ffn('l2_')

    inp['l3_norm_mix'] = gain(d)
    inp['l3_w_in'] = nrm((d, W_IN_D), d ** -0.5)
    inp['l3_sinks'] = nrm((D_HEADS,), 0.5)
    inp['l3_w_out'] = nrm((d, d), d ** -0.5)
    ffn('l3_')

    inp['final_norm'] = gain(d)
    return inp


def reference(x, mem, mem_norm, mem_w_kv,
              l0_norm_mix, l0_w_in, l0_cmp_pe_k, l0_cmp_w1_k, l0_cmp_w2_k, l0_cmp_pe_v, l0_cmp_w1_v,
              l0_cmp_w2_v, l0_w_out, l0_norm_ffn, l0_w_gate, l0_w_up, l0_w_down,
              l1_norm_mix, l1_w_in, l1_w_out, l1_norm_ffn, l1_w_gate, l1_w_up, l1_w_down,
              l2_norm_mix, l2_w_in, l2_conv_w, l2_conv_b, l2_gate_b, l2_head_norm, l2_w_out,
              l2_norm_ffn, l2_w_gate, l2_w_up, l2_w_down,
              l3_norm_mix, l3_w_in, l3_sinks, l3_w_out, l3_norm_ffn, l3_w_gate, l3_w_up, l3_w_down,
              final_norm):
    B, M, _ = mem.shape
    mem_kv = rms_norm(mem, mem_norm) @ mem_w_kv
    mem_k, mem_v = jnp.split(mem_kv, 2, axis=-1)
    mem_k = mem_k.reshape(B, M, MEM_HEADS, HEAD_DIM)
    mem_v = mem_v.reshape(B, M, MEM_HEADS, HEAD_DIM)

    mixers = (nsa_mixer, stick_breaking_mixer, mlstm_mixer, sink_window_mixer)
    layers = (
        (l0_norm_mix, l0_w_in, l0_w_out, l0_norm_ffn, l0_w_gate, l0_w_up, l0_w_down,
         (l0_cmp_pe_k, l0_cmp_w1_k, l0_cmp_w2_k, l0_cmp_pe_v, l0_cmp_w1_v, l0_cmp_w2_v)),
        (l1_norm_mix, l1_w_in, l1_w_out, l1_norm_ffn, l1_w_gate, l1_w_up, l1_w_down, ()),
        (l2_norm_mix, l2_w_in, l2_w_out, l2_norm_ffn, l2_w_gate, l2_w_up, l2_w_down,
         (l2_conv_w, l2_conv_b, l2_gate_b, l2_head_norm)),
        (l3_norm_mix, l3_w_in, l3_w_out, l3_norm_ffn, l3_w_gate, l3_w_up, l3_w_down, (l3_sinks,)),
    )
    for i in range(DEPTH):
        norm_mix, w_in, w_out, norm_ffn, w_gate, w_up, w_down, extra = layers[i]
        h = rms_norm(x, norm_mix)
        u = h @ w_in
        mixed = mixers[i % N_MIXERS](u[..., :-MEM_WIDTH], *extra)
        mem_out = memory_attention(u[..., -MEM_WIDTH:], mem_k, mem_v)
        x = x + jnp.concatenate([mixed, mem_out], axis=-1) @ w_out
        h = rms_norm(x, norm_ffn)
        x = x + (jax.nn.silu(h @ w_gate) * (h @ w_up)) @ w_down
    return rms_norm(x, final_norm)
```

```python
from contextlib import ExitStack
import numpy as np
import concourse.bass as bass
import concourse.mybir as mybir
from concourse.bass_utils import run_bass_kernel_spmd

F32 = mybir.dt.float32
BF16 = mybir.dt.bfloat16
AF = mybir.ActivationFunctionType
ALU = mybir.AluOpType
AX = mybir.AxisListType

D = 2048
DFF = 5632
MEM = 256
NCH = D // 128
EPS = 1e-6
SELF_SYNC = True
NR = 2
HQ = 12 // NR
HM = 4 // NR
HS = 24 // NR
NGS = 3 if NR == 1 else 2
NGN = 2 // NR
MW = 1536 // NR
MEMW = 512 // NR
FWD = MW + MEMW
NFC = FWD // 128


MARKS = []


class Dep:
    __slots__ = ("sem", "val", "key")

    def __init__(self, sem, val, key):
        self.sem, self.val, self.key = sem, val, key


class Buf:
    def __init__(self, t, name):
        self.t = t
        self.name = name
        self.w = None
        self.r = {}
        self.ds = None

    def __getitem__(self, idx):
        return self.t[idx]


class FW:
    def __init__(self, nc, stack, n_dsems=44):
        self.nc = nc
        self.stack = stack
        self.engs = {"pe": nc.tensor, "dve": nc.vector, "act": nc.scalar, "pool": nc.gpsimd, "sp": nc.sync}
        self.sem = {}
        self.cnt = {}
        self.waited = {}
        for e in self.engs:
            self.sem[e] = stack.enter_context(nc.semaphore("s_" + e))
            self.cnt[e] = 0
            self.waited[e] = {}
        self.dsems = [stack.enter_context(nc.semaphore("d%d" % i)) for i in range(n_dsems)]
        self.dcnt = [0] * n_dsems
        self.dnext = 0
        self.dfree = 0
        self.pstack = None
        self.uid = 0
        self.ccsem = None
        self.cccnt = 0
        self.ccbuf = None
        self.cq = []
        self.ci = 0

    def phase(self):
        self.pstack = ExitStack()
        return self.pstack

    def sb(self, shape, dtype, name=None):
        self.uid += 1
        name = "%s_%d" % (name or "sb", self.uid)
        t = self.pstack.enter_context(self.nc.sbuf_tensor(name, list(shape), dtype))
        return Buf(t, name)

    def sbg(self, shape, dtype, name=None):
        self.uid += 1
        name = "%s_%d" % (name or "sbg", self.uid)
        t = self.stack.enter_context(self.nc.sbuf_tensor(name, list(shape), dtype))
        return Buf(t, name)

    def ps(self, shape, dtype=F32, name=None):
        self.uid += 1
        name = "%s_%d" % (name or "ps", self.uid)
        t = self.pstack.enter_context(self.nc.psum_tensor(name, list(shape), dtype))
        return Buf(t, name)

    def _wait(self, e, dep, force_self=False, raw=False):
        if dep is None:
            return
        if dep.key == e and not (force_self or (raw and SELF_SYNC and e != "pe")):
            return
        if self.waited[e].get(dep.key, 0) >= dep.val:
            return
        self.engs[e].wait_ge(dep.sem, dep.val)
        self.waited[e][dep.key] = dep.val

    def _pre(self, e, reads, writes, force_self=False):
        for b in reads:
            self._wait(e, b.w, force_self, raw=True)
        for b in writes:
            self._wait(e, b.w, force_self)
            for d in b.r.values():
                self._wait(e, d, force_self)

    def _post(self, dep, reads, writes):
        for b in writes:
            b.w = dep
            b.r = {}
        for b in reads:
            if b not in writes:
                b.r[dep.key] = dep

    def op(self, e, fn, reads=(), writes=()):
        self._pre(e, reads, writes)
        ins = fn(self.engs[e])
        self.cnt[e] += 1
        ins.then_inc(self.sem[e], 1)
        self._post(Dep(self.sem[e], self.cnt[e], e), reads, writes)

    def dma(self, q, out, in_, reads=(), writes=(), cont=False, **kw):
        if not cont:
            self._pre(q, reads, writes, force_self=True)
        owner = None
        for b in list(writes) + list(reads):
            owner = b
            break
        nown = len(self.dsems) - 4
        if owner is None:
            di = nown + self.dfree % 4
            self.dfree += 1
        else:
            if owner.ds is None:
                owner.ds = self.dnext % nown
                self.dnext += 1
            di = owner.ds
        self.dcnt[di] += 16
        self.engs[q].dma_start(out=out, in_=in_, **kw).then_inc(self.dsems[di], 16)
        self._post(Dep(self.dsems[di], self.dcnt[di], "d%d" % di), reads, writes)

    def collective(self, kind, src_ap, dst_ap, groups):
        self.barrier()
        if NR == 1:
            self.dma("sp", dst_ap, src_ap)
            self.barrier()
            return
        if self.ccsem is None:
            self.ccsem = self.stack.enter_context(self.nc.semaphore("ccsem"))
            self.cccnt = 0
        op = ALU.add if kind == "ReduceScatter" else ALU.bypass
        self.nc.gpsimd.collective_compute(kind, op, replica_groups=groups, ins=[src_ap.opt()], outs=[dst_ap.opt()]).then_inc(self.ccsem)
        self.cccnt += 1
        self.nc.gpsimd.wait_ge(self.ccsem, self.cccnt)
        if self.ccbuf is None:
            self.ccbuf = self.sbg([128, 8], F32, "ccbuf")
        self.op("pool", lambda e: e.memset(self.ccbuf[:], 0.0), writes=[self.ccbuf])
        self.barrier()

    def set_casts(self, pairs):
        self.cq = []
        for src_, dst_ in pairs:
            K_, N_ = src_.shape
            for r0 in range(0, K_, 128):
                self.cq.append((dst_[r0:min(K_, r0 + 128), :], src_[r0:min(K_, r0 + 128), :]))
        self.ci = 0

    def cast_tick(self, frac):
        want = min(len(self.cq), int(len(self.cq) * frac + 0.999))
        while self.ci < want:
            o_, i_ = self.cq[self.ci]
            self.dma("pool", o_, i_)
            self.ci += 1

    def barrier(self, wait_casts=False):
        MARKS.append(dict(self.cnt))
        deps = [Dep(self.sem[e], self.cnt[e], e) for e in self.engs if self.cnt[e] > 0]
        nd = len(self.dsems) if wait_casts else len(self.dsems) - 4
        deps += [Dep(self.dsems[i], self.dcnt[i], "d%d" % i) for i in range(nd) if self.dcnt[i] > 0]
        for e in self.engs:
            for d in deps:
                self._wait(e, d, force_self=False)


def mm(fw, out_b, out_ap, lhsT_b, lhsT_ap, rhs_b, rhs_ap, start, stop, skip=False):
    rd = [lhsT_b] if lhsT_b is rhs_b else [lhsT_b, rhs_b]
    fw.op("pe", lambda e: e.matmul(out_ap, lhsT_ap, rhs_ap, start=start, stop=stop, skip_group_check=skip),
          reads=rd, writes=[out_b])


class Consts:
    pass


def make_consts(fw):
    c = Consts()
    c.ones_f = fw.sbg([128, 128], F32, "ones_f")
    fw.op("dve", lambda e: e.memset(c.ones_f[:], 1.0), writes=[c.ones_f])
    c.ones_b = fw.sbg([128, 128], BF16, "ones_b")
    fw.op("dve", lambda e: e.memset(c.ones_b[:], 1.0), writes=[c.ones_b])
    c.eps = fw.sbg([128, 1], F32, "eps")
    fw.op("dve", lambda e: e.memset(c.eps[:], EPS), writes=[c.eps])
    c.ident = fw.sbg([128, 128], BF16, "ident")
    fw.op("pool", lambda e: e.affine_select(out=c.ident[:], in_=c.ones_b[:], pattern=[[-1, 128]],
                                            compare_op=ALU.is_equal, fill=0.0, base=0, channel_multiplier=1),
          reads=[c.ones_b], writes=[c.ident])
    c.ident_f = fw.sbg([128, 128], F32, "ident_f")
    fw.op("pool", lambda e: e.affine_select(out=c.ident_f[:], in_=c.ones_f[:], pattern=[[-1, 128]],
                                            compare_op=ALU.is_equal, fill=0.0, base=0, channel_multiplier=1),
          reads=[c.ones_f], writes=[c.ident_f])
    return c


def tri_mask(fw, c, name, dtype, shape_cols, base, col_mult, chan_mult, op=ALU.is_ge):
    m = fw.sb([128, shape_cols], dtype, name)
    src = fw.sb([128, shape_cols], dtype, name + "_o")
    fw.op("dve", lambda e: e.memset(src[:], 1.0), writes=[src])
    fw.op("pool", lambda e: e.affine_select(out=m[:], in_=src[:], pattern=[[col_mult, shape_cols]],
                                            compare_op=op, fill=0.0, base=base, channel_multiplier=chan_mult),
          reads=[src], writes=[m])
    return m


def rms_norm_tile(fw, c, xs, g, h, psb, T, sqb, rstd):
    for k in range(NCH):
        sq = sqb[k % len(sqb)]
        fw.op("act", lambda e, k=k, sq=sq: e.activation(out=sq[:, :T], in_=xs[:, k, :T], func=AF.Square),
              reads=[xs], writes=[sq])
        mm(fw, psb, psb[:, :T], c.ones_f, c.ones_f[:], sq, sq[:, :T], start=(k == 0), stop=(k == NCH - 1))
    fw.op("act", lambda e: e.activation(out=rstd[:, :T], in_=psb[:, :T], func=AF.Sqrt, bias=c.eps[:, 0:1], scale=1.0 / D),
          reads=[psb, c.eps], writes=[rstd])
    fw.op("dve", lambda e: e.reciprocal(out=rstd[:, :T], in_=rstd[:, :T]), reads=[rstd], writes=[rstd])
    for k in range(NCH):
        fw.op("dve", lambda e, k=k: e.scalar_tensor_tensor(out=h[:, k, :T], in0=xs[:, k, :T], scalar=g[:, k:k + 1],
                                                          in1=rstd[:, :T], op0=ALU.mult, op1=ALU.mult),
              reads=[xs, g, rstd], writes=[h])


def load_wblock(fw, wb, w_dram, kchunks, c0, nb, q="sp"):
    v = w_dram.rearrange("(kc p) n -> p kc n", p=128)
    half = (kchunks + 1) // 2
    fw.dma(q, wb[:, 0:half, 0:nb], v[:, 0:half, c0:c0 + nb], writes=[wb])
    fw.dma(q, wb[:, half:kchunks, 0:nb], v[:, half:kchunks, c0:c0 + nb], writes=[wb], cont=True)


def issue_casts(fw, pairs):
    for src, dst in pairs:
        K, N = src.shape
        step = 256
        for r0 in range(0, K, step):
            r1 = min(K, r0 + step)
            fw.dma("pool", dst[r0:r1, :], src[r0:r1, :])


def phase_cast_weights(fw, pairs):
    issue_casts(fw, pairs)
    fw.barrier(wait_casts=True)


def phase_n1(fw, c, SH, x_src, g_dram, hT_loc, T=512):
    with fw.phase():
        xss = [fw.sb([128, NCH, T], F32, "xs") for _ in range(2)]
        hs = [fw.sb([128, NCH, T], BF16, "h") for _ in range(2)]
        g = fw.sb([128, NCH], F32, "g")
        sqb = [fw.sb([128, T], F32, "sq") for _ in range(3)]
        rstd = fw.sb([128, T], F32, "rstd")
        ps_n = fw.ps([128, 512], F32, "ps_n")
        fw.dma("sp", g[:], g_dram, writes=[g])
        xv = x_src.rearrange("(kc p) s -> p kc s", p=128)
        hv = hT_loc.rearrange("(kc p) s -> p kc s", p=128)
        for tt in range(SH // T):
            t0 = tt * T
            xs, h = xss[tt % 2], hs[tt % 2]
            fw.dma("sp", xs[:, 0:8, :], xv[:, 0:8, t0:t0 + T], writes=[xs])
            fw.dma("sp", xs[:, 8:16, :], xv[:, 8:16, t0:t0 + T], writes=[xs], cont=True)
            rms_norm_tile(fw, c, xs, g, h, ps_n, T, sqb, rstd)
            fw.dma("pool", hv[:, 0:8, t0:t0 + T], h[:, 0:8, :], reads=[h])
            fw.dma("pool", hv[:, 8:16, t0:t0 + T], h[:, 8:16, :], reads=[h], cont=True)
        fw.barrier()


def phase_in(fw, c, S, SH, Gh, w_bf, segs, T=512):
    with fw.phase():
        hs = [fw.sb([128, NCH, T], BF16, "h") for _ in range(2)]
        wbs = [fw.sb([128, NCH, 512], BF16, "wb") for _ in range(3)]
        stg = [fw.sb([128, 512], BF16, "stg") for _ in range(4)]
        stgf = [fw.sb([128, 512], F32, "stgf") for _ in range(2)]
        pss = [fw.ps([128, 512], F32, "ps_o") for _ in range(6)]
        RC = cc_rows(SH)
        KC = RC // 128
        gv = Gh.rearrange("(ci b kc p) s -> ci b p kc s", p=128, kc=KC, b=NR)
        wi = 0
        pi = 0
        si = 0
        for tt in range(S // T):
            t0 = tt * T
            blk = t0 // SH
            l0 = t0 % SH
            h = hs[tt % 2]
            for ci in range(D // RC):
                fw.dma("sp", h[:, ci * KC:(ci + 1) * KC, :], gv[ci, blk, :, :, l0:l0 + T], writes=[h], cont=(ci > 0))
            for (c0, ncols, form, dst, ddt) in segs:
                for b0 in range(0, ncols, 512):
                    nb = min(512, ncols - b0)
                    wb = wbs[wi % 3]
                    wi += 1
                    load_wblock(fw, wb, w_bf, NCH, c0 + b0, nb)
                    if form == "FM":
                        for j0 in range(0, nb, 128):
                            mj = min(128, nb - j0)
                            ps = pss[pi % 6]
                            pi += 1
                            for k in range(NCH):
                                mm(fw, ps, ps[0:mj, :T], wb, wb[:, k, j0:j0 + mj], h, h[:, k, :T],
                                   start=(k == 0), stop=(k == NCH - 1))
                            st = stg[si % 4]
                            if si % 2 == 0:
                                fw.op("act", lambda e, st=st, ps=ps, mj=mj: e.copy(out=st[0:mj, :T], in_=ps[0:mj, :T]),
                                      reads=[ps], writes=[st])
                            else:
                                fw.op("dve", lambda e, st=st, ps=ps, mj=mj: e.tensor_copy(out=st[0:mj, :T], in_=ps[0:mj, :T]),
                                      reads=[ps], writes=[st])
                            si += 1
                            fw.dma("pool", dst[b0 + j0:b0 + j0 + mj, t0:t0 + T], st[0:mj, :T], reads=[st])
                    else:
                        for ti in range(T // 128):
                            ps = pss[pi % 6]
                            pi += 1
                            for k in range(NCH):
                                mm(fw, ps, ps[:, 0:nb], h, h[:, k, ti * 128:(ti + 1) * 128], wb, wb[:, k, 0:nb],
                                   start=(k == 0), stop=(k == NCH - 1))
                            if ddt == F32:
                                st = stgf[si % 2]
                            else:
                                st = stg[si % 4]
                            if si % 2 == 0:
                                fw.op("act", lambda e, st=st, ps=ps, nb=nb: e.copy(out=st[:, 0:nb], in_=ps[:, 0:nb]),
                                      reads=[ps], writes=[st])
                            else:
                                fw.op("dve", lambda e, st=st, ps=ps, nb=nb: e.tensor_copy(out=st[:, 0:nb], in_=ps[:, 0:nb]),
                                      reads=[ps], writes=[st])
                            si += 1
                            fw.dma("pool", dst[t0 + ti * 128:t0 + (ti + 1) * 128, b0:b0 + nb], st[:, 0:nb], reads=[st])
        fw.barrier()


def phase_op(fw, c, S, SH, mixT, wout_bf, ypart, T=512):
    with fw.phase():
        mxs = [fw.sb([128, NFC, T], BF16, "mx") for _ in range(2)]
        wo = fw.sb([128, NFC, D], BF16, "wo")
        stg = [fw.sb([128, 512], BF16, "stg") for _ in range(4)]
        pss = [fw.ps([128, 512], F32, "ps_o") for _ in range(6)]
        wv = wout_bf.rearrange("(kc p) n -> p kc n", p=128)
        for k0 in range(0, NFC, 2):
            fw.dma("sp", wo[:, k0:k0 + 2, :], wv[:, k0:k0 + 2, :], writes=[wo], cont=(k0 > 0))
        mv = mixT.rearrange("(kc p) s -> p kc s", p=128)
        RC = cc_rows(SH)
        KC = RC // 128
        yv = ypart.rearrange("(ci b kc p) s -> ci b p kc s", p=128, kc=KC, b=NR)
        pi = 0
        for tt in range(S // T):
            t0 = tt * T
            blk = t0 // SH
            l0 = t0 % SH
            mx = mxs[tt % 2]
            fw.dma("sp", mx[:, :, :], mv[:, :, t0:t0 + T], writes=[mx])
            for dc in range(NCH):
                ps = pss[pi % 6]
                st = stg[pi % 4]
                for k in range(NFC):
                    mm(fw, ps, ps[:, :T], wo, wo[:, k, dc * 128:(dc + 1) * 128], mx, mx[:, k, :T], start=(k == 0), stop=(k == NFC - 1))
                if pi % 2 == 0:
                    fw.op("act", lambda e, st=st, ps=ps: e.copy(out=st[:, :T], in_=ps[:, :T]), reads=[ps], writes=[st])
                else:
                    fw.op("dve", lambda e, st=st, ps=ps: e.tensor_copy(out=st[:, :T], in_=ps[:, :T]), reads=[ps], writes=[st])
                pi += 1
                fw.dma("pool", yv[dc // KC, blk, :, dc % KC, l0:l0 + T], st[:, :T], reads=[st])
        fw.barrier()


def phase_ffn(fw, c, SH, x_in, x_out, y_my, gffn_dram, wg_bf, wu_bf, wd_bf, final_g=None, out_dram=None, T=512, pre_cast=()):
    NF = DFF // 128
    with fw.phase():
        xs = fw.sb([128, NCH, T], F32, "xs")
        h = fw.sb([128, NCH, T], BF16, "h")
        ys = h
        act = fw.sb([128, NF, T], BF16, "act")
        g = fw.sb([128, NCH], F32, "g")
        gf = fw.sb([128, NCH], F32, "gf")
        sqb = [fw.sb([128, T], F32, "sq") for _ in range(2)]
        rstd = fw.sb([128, T], F32, "rstd")
        wbs = [fw.sb([128, NCH, 512], BF16, "wb") for _ in range(3)]
        wds = [fw.sb([128, NF, 256], BF16, "wd") for _ in range(2)]
        sgs = [fw.sb([128, T], F32, "sg") for _ in range(2)]
        ps_n = fw.ps([128, 512], F32, "ps_n")
        pss = [fw.ps([128, 512], F32, "ps_o") for _ in range(6)]
        fw.dma("sp", g[:], gffn_dram, writes=[g])
        if final_g is not None:
            fw.dma("sp", gf[:], final_g, writes=[gf])
        xv = x_in.rearrange("(kc p) s -> p kc s", p=128)
        xo = x_out.rearrange("(kc p) s -> p kc s", p=128)
        yv = y_my.rearrange("(kc p) s -> p kc s", p=128)
        wi = 0
        pi = 0
        di = 0
        fw.cast_tick(1.0)
        for tt in range(SH // T):
            t0 = tt * T
            fw.dma("sp", ys[:, :, :], yv[:, :, t0:t0 + T], writes=[ys])
            fw.dma("sp", xs[:, 0:8, :], xv[:, 0:8, t0:t0 + T], writes=[xs])
            fw.dma("sp", xs[:, 8:16, :], xv[:, 8:16, t0:t0 + T], writes=[xs], cont=True)
            for k0 in range(0, NCH, 4):
                fw.op("pool", lambda e, k0=k0: e.tensor_tensor(out=xs[:, k0:k0 + 4, :], in0=xs[:, k0:k0 + 4, :], in1=ys[:, k0:k0 + 4, :], op=ALU.add),
                      reads=[xs, ys], writes=[xs])

            rms_norm_tile(fw, c, xs, g, h, ps_n, T, sqb, rstd)
            for b0 in range(0, DFF, 512):
                wg = wbs[wi % 3]
                wi += 1
                load_wblock(fw, wg, wg_bf, NCH, b0, 512)
                wu = wbs[wi % 3]
                wi += 1
                load_wblock(fw, wu, wu_bf, NCH, b0, 512)
                for j in range(4):
                    fc = b0 // 128 + j
                    pg = pss[pi % 6]
                    pi += 1
                    pu = pss[pi % 6]
                    pi += 1
                    for k in range(NCH):
                        mm(fw, pg, pg[:, :T], wg, wg[:, k, j * 128:(j + 1) * 128], h, h[:, k, :T],
                           start=(k == 0), stop=(k == NCH - 1))
                    for k in range(NCH):
                        mm(fw, pu, pu[:, :T], wu, wu[:, k, j * 128:(j + 1) * 128], h, h[:, k, :T],
                           start=(k == 0), stop=(k == NCH - 1))
                    sg = sgs[fc % 2]
                    fw.op("act", lambda e, sg=sg, pg=pg: e.activation(out=sg[:, :T], in_=pg[:, :T], func=AF.Silu),
                          reads=[pg], writes=[sg])
                    fw.op("dve", lambda e, sg=sg, pu=pu, fc=fc: e.tensor_tensor(out=act[:, fc, :T], in0=sg[:, :T], in1=pu[:, :T], op=ALU.mult),
                          reads=[sg, pu], writes=[act])
            for b0 in range(0, D, 256):
                wd = wds[di % 2]
                di += 1
                v = wd_bf.rearrange("(kc p) n -> p kc n", p=128)
                for q0 in range(0, NF, 11):
                    fw.dma("sp", wd[:, q0:q0 + 11, :], v[:, q0:q0 + 11, b0:b0 + 256], writes=[wd], cont=(q0 > 0))
                for j in range(2):
                    dc = b0 // 128 + j
                    ps = pss[pi % 6]
                    pi += 1
                    for k in range(NF):
                        mm(fw, ps, ps[:, :T], wd, wd[:, k, j * 128:(j + 1) * 128], act, act[:, k, :T],
                           start=(k == 0), stop=(k == NF - 1))
                    fw.op("dve", lambda e, dc=dc, ps=ps: e.tensor_tensor(out=xs[:, dc, :T], in0=xs[:, dc, :T], in1=ps[:, :T], op=ALU.add),
                          reads=[xs, ps], writes=[xs])
            if final_g is None:
                fw.dma("pool", xo[:, 0:8, t0:t0 + T], xs[:, 0:8, :], reads=[xs])
                fw.dma("pool", xo[:, 8:16, t0:t0 + T], xs[:, 8:16, :], reads=[xs], cont=True)
            else:
                for k in range(NCH):
                    sq = sqb[k % 2]
                    fw.op("act", lambda e, k=k, sq=sq: e.activation(out=sq[:, :T], in_=xs[:, k, :T], func=AF.Square),
                          reads=[xs], writes=[sq])
                    mm(fw, ps_n, ps_n[:, :T], c.ones_f, c.ones_f[:], sq, sq[:, :T], start=(k == 0), stop=(k == NCH - 1))
                fw.op("act", lambda e: e.activation(out=rstd[:, :T], in_=ps_n[:, :T], func=AF.Sqrt, bias=c.eps[:, 0:1], scale=1.0 / D),
                      reads=[ps_n, c.eps], writes=[rstd])
                fw.op("dve", lambda e: e.reciprocal(out=rstd[:, :T], in_=rstd[:, :T]), reads=[rstd], writes=[rstd])
                oo = out_dram.rearrange("(kc p) s -> p kc s", p=128)
                for k in range(NCH):
                    fw.op("dve", lambda e, k=k: e.scalar_tensor_tensor(out=xs[:, k, :T], in0=xs[:, k, :T], scalar=gf[:, k:k + 1],
                                                                      in1=rstd[:, :T], op0=ALU.mult, op1=ALU.mult),
                          reads=[xs, gf, rstd], writes=[xs])
                fw.dma("pool", oo[:, 0:8, t0:t0 + T], xs[:, 0:8, :], reads=[xs])
                fw.dma("pool", oo[:, 8:16, t0:t0 + T], xs[:, 8:16, :], reads=[xs], cont=True)
        fw.barrier()


class Attn:
    def __init__(self, fw, npt=3):
        self.st = [fw.ps([128, 512], F32, "st") for _ in range(2)]
        self.ob = [[fw.ps([128, 512], F32, "ob") for _ in range(2)] for _ in range(2)]
        self.pt = [fw.sb([128, 512], BF16, "pt") for _ in range(npt + 1)]
        self.rc = [fw.sb([128, 4], F32, "rc") for _ in range(4)]
        self.si = 0
        self.pi = 0
        self.oi = 0
        self.ri = 0
        self.q = []

    def flush(self, keep=0):
        n = len(self.q) - keep
        if n <= 0:
            return
        pending, self.q = self.q[:n], self.q[n:]
        for f in pending:
            f()


def attn_head(fw, A, q_b, q_ap, ktiles, nv, scale, mask_eng="dve"):
    banks = A.ob[A.oi % 2]
    A.oi += 1
    started = [False, False]
    for ti, ktl in enumerate(ktiles):
        (k_b, k_ap, v_b, v_ap, m_b, m_ap, lo, hi) = ktl[:8]
        sz = ktl[8] if len(ktl) > 8 else 128
        st = A.st[A.si % 2]
        A.si += 1
        cs = slice(lo * 128, hi * 128)
        mm(fw, st, st[0:sz, cs], k_b, k_ap, q_b, q_ap(lo, hi), True, True)
        pt = A.pt[A.pi % len(A.pt)]
        A.pi += 1
        fw.op("act", lambda e: e.activation(out=pt[0:sz, cs], in_=st[0:sz, cs], func=AF.Exp, scale=scale), reads=[st], writes=[pt])
        if m_b is not None:
            eng = mask_eng if mask_eng != "alt" else ("dve" if A.pi % 2 == 0 else "pool")
            fw.op(eng, lambda e: e.tensor_tensor(out=pt[0:sz, cs], in0=pt[0:sz, cs], in1=m_ap, op=ALU.mult), reads=[pt, m_b], writes=[pt])
        A.flush(keep=1)

        def pv(pt=pt, sz=sz, lo=lo, hi=hi, v_b=v_b, v_ap=v_ap):
            for qb in range(lo, hi):
                bank = banks[qb // 2]
                col = (qb % 2) * nv
                mm(fw, bank, bank[:, col:col + nv], pt, pt[0:sz, qb * 128:(qb + 1) * 128], v_b, v_ap,
                   start=(not started[qb // 2]), stop=True, skip=True)
                started[qb // 2] = True
        A.q.append(pv)
    return [(banks[qb // 2], (qb % 2) * nv) for qb in range(4)]


def attn_finish_plain(fw, A, outs, dv, dst_b, dst_ap, extra_b=None, extra_ap=None):
    A.q.append(lambda: _attn_finish_plain(fw, A, outs, dv, dst_b, dst_ap, extra_b, extra_ap))


def _attn_finish_plain(fw, A, outs, dv, dst_b, dst_ap, extra_b=None, extra_ap=None):
    for qb, (bank, col) in enumerate(outs):
        rc = A.rc[A.ri % 4]
        A.ri += 1
        if extra_b is not None:
            fw.op("dve", lambda e, rc=rc, bank=bank, col=col: e.tensor_tensor(out=rc[:, 0:1], in0=bank[:, col + dv:col + dv + 1],
                                                                               in1=extra_ap, op=ALU.add),
                  reads=[bank, extra_b], writes=[rc])
            fw.op("dve", lambda e, rc=rc: e.reciprocal(out=rc[:, 0:1], in_=rc[:, 0:1]), reads=[rc], writes=[rc])
        else:
            fw.op("dve", lambda e, rc=rc, bank=bank, col=col: e.reciprocal(out=rc[:, 0:1], in_=bank[:, col + dv:col + dv + 1]),
                  reads=[bank], writes=[rc])
        fw.op("dve", lambda e, rc=rc, bank=bank, col=col, qb=qb: e.tensor_scalar(out=dst_ap(qb), in0=bank[:, col:col + dv],
                                                                                 scalar1=rc[:, 0:1], scalar2=None, op0=ALU.mult),
              reads=[bank, rc], writes=[dst_b])


def mixer_store(fw, c, mixtm, mixT_dram, t0, pst, stgs, cnt, fcs=None, row0=0):
    if getattr(fw, "dbg", None) and "mix" in fw.dbg:
        fw.dma("sp", fw.dbg.pop("mix"), mixtm[:, :, :], reads=[mixtm])
    for fc in (range(NCH) if fcs is None else fcs):
        ps = pst[cnt[0] % len(pst)]
        for qb in range(4):
            fw.op("pe", lambda e, ps=ps, qb=qb, fc=fc: e.transpose(out=ps[:, qb * 128:(qb + 1) * 128],
                                                                   in_=mixtm[:, qb, fc * 128:(fc + 1) * 128], identity=c.ident[:]),
                  reads=[mixtm, c.ident], writes=[ps])
        st = stgs[cnt[0] % len(stgs)]
        eng = "act" if cnt[0] % 2 == 0 else "dve"
        if eng == "act":
            fw.op("act", lambda e, st=st, ps=ps: e.copy(out=st[:, :], in_=ps[:, :]), reads=[ps], writes=[st])
        else:
            fw.op("dve", lambda e, st=st, ps=ps: e.tensor_copy(out=st[:, :], in_=ps[:, :]), reads=[ps], writes=[st])
        cnt[0] += 1
        fw.dma("sp", mixT_dram[row0 + fc * 128:row0 + (fc + 1) * 128, t0:t0 + 512], st[:, :], reads=[st])


def mem_attention_qtile(fw, A, MK, qm, mixtm, col0=MW):
    for hh in range(HM):
        kts = []
        for mt in range(2):
            kts.append((MK.k, MK.k[:, hh, mt * 128:(mt + 1) * 128], MK.v, MK.v[:, mt, hh, 0:129], None, None, 0, 4))
        outs = attn_head(fw, A, qm, lambda lo, hi, hh=hh: qm[:, hh, lo * 128:hi * 128], kts, 129, 128 ** -0.5)
        attn_finish_plain(fw, A, outs, 128, mixtm, lambda qb, hh=hh: mixtm[:, qb, col0 + hh * 128:col0 + (hh + 1) * 128])


class MemKV:
    pass


def phase_mem(fw, c, memT_dram, gmem_dram, wkv_bf):
    MK = MemKV()
    MK.k = fw.sbg([128, HM, MEM], BF16, "memk")
    MK.v = fw.sbg([128, 2, HM, 129], BF16, "memv")
    T = MEM
    with fw.phase():
        xs = fw.sb([128, NCH, T], F32, "xs")
        h = fw.sb([128, NCH, T], BF16, "h")
        g = fw.sb([128, NCH], F32, "g")
        sqb = [fw.sb([128, T], F32, "sq") for _ in range(2)]
        rstd = fw.sb([128, T], F32, "rstd")
        wbs = [fw.sb([128, NCH, 512], BF16, "wb") for _ in range(2)]
        ps_n = fw.ps([128, 512], F32, "ps_n")
        pss = [fw.ps([128, 512], F32, "ps_o") for _ in range(2)]
        fw.dma("sp", g[:], gmem_dram, writes=[g])
        xv = memT_dram.rearrange("(kc p) s -> p kc s", p=128)
        fw.dma("sp", xs[:, :, :], xv[:, :, :], writes=[xs])
        rms_norm_tile(fw, c, xs, g, h, ps_n, T, sqb, rstd)
        fw.op("dve", lambda e: e.memset(MK.v[:, :, :, 128:129], 1.0), writes=[MK.v])
        load_wblock(fw, wbs[0], wkv_bf, NCH, 0, HM * 128)
        load_wblock(fw, wbs[1], wkv_bf, NCH, HM * 128, HM * 128)
        for hh in range(HM):
            ps = pss[hh % 2]
            for k in range(NCH):
                mm(fw, ps, ps[:, :T], wbs[0], wbs[0][:, k, hh * 128:(hh + 1) * 128], h, h[:, k, :T], start=(k == 0), stop=(k == NCH - 1))
            fw.op("dve", lambda e, ps=ps, hh=hh: e.tensor_copy(out=MK.k[:, hh, :], in_=ps[:, :T]), reads=[ps], writes=[MK.k])
        for mt in range(2):
            ps = pss[mt % 2]
            for k in range(NCH):
                mm(fw, ps, ps[:, :HM * 128], h, h[:, k, mt * 128:(mt + 1) * 128], wbs[1], wbs[1][:, k, 0:HM * 128], start=(k == 0), stop=(k == NCH - 1))
            fw.op("dve", lambda e, ps=ps, mt=mt: e.tensor_copy(out=MK.v[:, mt, :, 0:128],
                                                              in_=ps[:, :HM * 128].rearrange("p (h d) -> p h d", h=HM)),
                  reads=[ps], writes=[MK.v])
        fw.barrier()
    return MK


def phase_mix_swa(fw, c, S, MK, qT_d, kT_d, v_d, qmT_d, sinks_d, mixT_d):
    NT = S // 128
    with fw.phase():
        A = Attn(fw)
        k2 = fw.sb([128, NGS, S], BF16, "k2")
        va = fw.sb([128, NT, NGS, 65], BF16, "va")
        es = fw.sb([128, HS], F32, "es")
        m2 = fw.sb([128, 256], BF16, "m2")
        ones2 = fw.sb([128, 256], BF16, "ones2")
        qts = [fw.sb([128, HS // 2, 512], BF16, "qt") for _ in range(2)]
        qms = [fw.sb([128, HM, 512], BF16, "qm") for _ in range(2)]
        mixtms = [fw.sb([128, 4, FWD], BF16, "mixtm") for _ in range(2)]
        pst = [fw.ps([128, 512], BF16, "pst") for _ in range(2)]
        stgs = [fw.sb([128, 512], BF16, "stg") for _ in range(3)]
        cnt = [0]
        fw.op("dve", lambda e: e.memset(ones2[:], 1.0), writes=[ones2])
        fw.op("pool", lambda e: e.affine_select(out=m2[:, 0:128], in_=ones2[:, 0:128], pattern=[[1, 128]], compare_op=ALU.is_ge,
                                                fill=0.0, base=0, channel_multiplier=-1), reads=[ones2], writes=[m2])
        fw.op("pool", lambda e: e.affine_select(out=m2[:, 128:256], in_=ones2[:, 128:256], pattern=[[-1, 128]], compare_op=ALU.is_ge,
                                                fill=0.0, base=-1, channel_multiplier=1), reads=[ones2], writes=[m2])
        for gi in range(NGS):
            fw.dma("sp", k2[0:64, gi, :], kT_d[gi * 64:(gi + 1) * 64, :], writes=[k2], cont=(gi > 0))
            fw.dma("sp", k2[64:128, gi, :], kT_d[gi * 64:(gi + 1) * 64, :], writes=[k2], cont=True)
        vv = v_d.rearrange("(kt p) (g d) -> p kt g d", p=128, g=NGS)
        for k0 in range(0, NT, 8):
            for gi in range(NGS):
                k1 = min(NT, k0 + 8)
                fw.dma("sp", va[:, k0:k1, gi, 0:64], vv[:, k0:k1, gi, :], writes=[va], cont=(k0 > 0 or gi > 0))
        fw.op("dve", lambda e: e.memset(va[:, :, :, 64:65], 1.0), reads=[], writes=[va])
        fw.dma("sp", es[:], sinks_d.partition_broadcast(128), writes=[es])
        fw.op("act", lambda e: e.activation(out=es[:], in_=es[:], func=AF.Exp), reads=[es], writes=[es])
        qv = qT_d.rearrange("(c p) s -> p c s", p=128)
        qmv = qmT_d.rearrange("(c p) s -> p c s", p=128)
        for qt in range(S // 512):
            t0 = qt * 512
            qtb = qts[qt % 2]
            qmb = qms[qt % 2]
            mixtm = mixtms[qt % 2]
            fw.dma("sp", qtb[:, :, :], qv[:, :, t0:t0 + 512], writes=[qtb])
            fw.dma("sp", qmb[:, :, :], qmv[:, :, t0:t0 + 512], writes=[qmb])
            for hh in range(HS):
                fw.cast_tick((qt * HS + hh + 1) / float((S // 512) * HS))
                gi = hh // 8
                pb = (hh % 2) * 64
                ch = hh // 2
                kts = []
                for i in range(-1, 4):
                    kt = qt * 4 + i
                    if kt < 0:
                        continue
                    lo, hi = max(i, 0), min(i + 2, 4)
                    mlo = (lo - i) * 128
                    mhi = (hi - i) * 128
                    kts.append((k2, k2[pb:pb + 64, gi, kt * 128:(kt + 1) * 128], va, va[:, kt, gi, :], m2, m2[:, mlo:mhi], lo, hi))
                outs = attn_head(fw, A, qtb, lambda lo, hi, pb=pb, ch=ch: qtb[pb:pb + 64, ch, lo * 128:hi * 128], kts, 65, 64 ** -0.5)
                attn_finish_plain(fw, A, outs, 64, mixtm, lambda qb, hh=hh: mixtm[:, qb, hh * 64:(hh + 1) * 64],
                                  extra_b=es, extra_ap=es[:, hh:hh + 1])
            mem_attention_qtile(fw, A, MK, qmb, mixtm)
            A.flush()
            mixer_store(fw, c, mixtm, mixT_d, t0, pst, stgs, cnt, fcs=range(NFC))
        fw.barrier()


def mem_part(fw, c, A, MK, S, qmT_d, mixT_d, pst, stgs, cnt):
    qms = [fw.sb([128, HM, 512], BF16, "qm") for _ in range(2)]
    mtm = [fw.sb([128, 4, HM * 128], BF16, "mtm") for _ in range(2)]
    qmv = qmT_d.rearrange("(c p) s -> p c s", p=128)
    for qt in range(S // 512):
        t0 = qt * 512
        qmb = qms[qt % 2]
        fw.dma("sp", qmb[:, :, :], qmv[:, :, t0:t0 + 512], writes=[qmb])
        mem_attention_qtile(fw, A, MK, qmb, mtm[qt % 2], col0=0)
        A.flush()
        mixer_store(fw, c, mtm[qt % 2], mixT_d, t0, pst, stgs, cnt, fcs=range(HM), row0=MW)


def phase_mix_sb(fw, c, S, MK, qT_d, kT_d, v_d, qmT_d, mixT_d):
    NT = S // 128
    scale = 128 ** -0.5
    with fw.phase():
        ones5 = fw.sb([128, 512], BF16, "ones5")
        fw.op("dve", lambda e: e.memset(ones5[:], 1.0), writes=[ones5])
        masks = []
        for i in range(4):
            m = fw.sb([128, 512], BF16, "mk")
            fw.op("pool", lambda e, m=m, i=i: e.affine_select(out=m[:], in_=ones5[:], pattern=[[1, 512]], compare_op=ALU.is_ge,
                                                              fill=0.0, base=-128 * i - 1, channel_multiplier=-1),
                  reads=[ones5], writes=[m])
            masks.append(m)
        tri = fw.sb([128, 128], BF16, "tri")
        fw.op("pool", lambda e: e.affine_select(out=tri[:], in_=ones5[:, 0:128], pattern=[[-1, 128]], compare_op=ALU.is_ge,
                                                fill=0.0, base=-1, channel_multiplier=1), reads=[ones5], writes=[tri])
        one1 = fw.sb([128, 1], F32, "one1")
        fw.op("dve", lambda e: e.memset(one1[:], 1.0), writes=[one1])
        khs = [fw.sb([128, S], BF16, "kh") for _ in range(2)]
        qhs = [fw.sb([128, S], BF16, "qh") for _ in range(2)]
        vhs = [fw.sb([128, NT, 128], BF16, "vh") for _ in range(2)]
        ps_st = [fw.ps([128, 512], F32, "st") for _ in range(2)]
        ps_tri = [fw.ps([128, 512], F32, "ptri") for _ in range(2)]
        ps_one = [fw.ps([128, 512], F32, "pone") for _ in range(2)]
        ps_o = [fw.ps([128, 512], F32, "po") for _ in range(2)]
        ebs = [fw.sb([128, 512], F32, "eb") for _ in range(2)]
        sps = [fw.sb([128, 512], BF16, "spb") for _ in range(3)]
        t1s = [fw.sb([128, 512], F32, "t1") for _ in range(2)]
        R = fw.sb([128, 512], F32, "R")
        nls = [fw.sb([128, 512], BF16, "nl") for _ in range(2)]
        pts = [fw.sb([128, 512], BF16, "pt") for _ in range(2)]
        stgs = [fw.sb([128, 512], BF16, "stg") for _ in range(3)]
        vv = v_d.rearrange("(kt p) f -> p kt f", p=128)
        blocks = []
        for hh in range(HQ):
            for qt in range(S // 512):
                kts = list(range(4 * qt + 3, -1, -1))
                for idx, kt in enumerate(kts):
                    blocks.append((hh, qt, kt, idx, len(kts)))
        hbuf = {}
        state = {"so": 0}

        def stage_a1(n):
            hh, qt, kt, idx, nk = blocks[n]
            if idx == 0 and qt == 0:
                kh, qh, vh = khs[hh % 2], qhs[hh % 2], vhs[hh % 2]
                fw.dma("sp", kh[:, :], kT_d[hh * 128:(hh + 1) * 128, :], writes=[kh])
                fw.dma("sp", qh[:, :], qT_d[hh * 128:(hh + 1) * 128, :], writes=[qh])
                for k0 in range(0, NT, 8):
                    k1 = min(NT, k0 + 8)
                    fw.dma("sp", vh[:, k0:k1, :], vv[:, k0:k1, hh * 128:(hh + 1) * 128], writes=[vh], cont=(k0 > 0))
            kh, qh = khs[hh % 2], qhs[hh % 2]
            t0 = qt * 512
            st = ps_st[n % 2]
            mm(fw, st, st[:, :], kh, kh[:, kt * 128:(kt + 1) * 128], qh, qh[:, t0:t0 + 512], True, True)

        def stage_a2(n):
            hh, qt, kt, idx, nk = blocks[n]
            i = kt - 4 * qt
            st, ptr, pon = ps_st[n % 2], ps_tri[n % 2], ps_one[n % 2]
            eb, sp, nl = ebs[n % 2], sps[n % 3], nls[n % 2]
            fw.op("act", lambda e: e.activation(out=eb[:, :], in_=st[:, :], func=AF.Exp, scale=-scale), reads=[st], writes=[eb])
            fw.op("act", lambda e: e.activation(out=sp[:, :], in_=eb[:, :], func=AF.Ln, bias=one1[:, 0:1], scale=1.0), reads=[eb, one1], writes=[sp])
            fw.op("dve", lambda e: e.scalar_tensor_tensor(out=nl[:, :], in0=st[:, :], scalar=scale, in1=sp[:, :], op0=ALU.mult, op1=ALU.add),
                  reads=[st, sp], writes=[nl])
            if i >= 0:
                fw.op("pool", lambda e: e.tensor_tensor(out=nl[:, :], in0=nl[:, :], in1=masks[i][:, :], op=ALU.mult), reads=[nl, masks[i]], writes=[nl])
            mm(fw, ptr, ptr[:, :], tri, tri[:, :], nl, nl[:, :], True, False)
            mm(fw, ptr, ptr[:, :], c.ident, c.ident[:, :], sp, sp[:, :], False, True)
            mm(fw, pon, pon[:, :], c.ones_b, c.ones_b[:, :], nl, nl[:, :], True, True)

        def stage_b(n):
            hh, qt, kt, idx, nk = blocks[n]
            vh = vhs[hh % 2]
            t0 = qt * 512
            i = kt - 4 * qt
            ptr, pon = ps_tri[n % 2], ps_one[n % 2]
            sp, t1, pt = sps[n % 3], t1s[n % 2], pts[n % 2]
            po = ps_o[(hh * (S // 512) + qt) % 2]
            if idx == 0:
                if nk > 1:
                    fw.op("dve", lambda e: e.tensor_copy(out=R[:, :], in_=pon[:, :]), reads=[pon], writes=[R])
            else:
                fw.op("dve", lambda e: e.tensor_tensor(out=t1[:, :], in0=ptr[:, :], in1=R[:, :], op=ALU.add), reads=[ptr, R], writes=[t1])
                if idx < nk - 1:
                    fw.op("dve", lambda e: e.tensor_tensor(out=R[:, :], in0=R[:, :], in1=pon[:, :], op=ALU.add), reads=[R, pon], writes=[R])

        def stage_b2(n):
            hh, qt, kt, idx, nk = blocks[n]
            vh = vhs[hh % 2]
            t0 = qt * 512
            i = kt - 4 * qt
            ptr = ps_tri[n % 2]
            t1, pt = t1s[n % 2], pts[n % 2]
            po = ps_o[(hh * (S // 512) + qt) % 2]
            if idx == 0:
                fw.op("act", lambda e: e.activation(out=pt[:, :], in_=ptr[:, :], func=AF.Exp, scale=-1.0), reads=[ptr], writes=[pt])
            else:
                fw.op("act", lambda e: e.activation(out=pt[:, :], in_=t1[:, :], func=AF.Exp, scale=-1.0), reads=[t1], writes=[pt])
            if i >= 0:
                fw.op("pool", lambda e: e.tensor_tensor(out=pt[:, :], in0=pt[:, :], in1=masks[i][:, :], op=ALU.mult), reads=[pt, masks[i]], writes=[pt])
            mm(fw, po, po[:, :], vh, vh[:, kt, :], pt, pt[:, :], start=(idx == 0), stop=(idx == nk - 1))
            if idx == nk - 1:
                so = state["so"]
                stg = stgs[so % 3]
                if so % 2 == 0:
                    fw.op("act", lambda e: e.copy(out=stg[:, :], in_=po[:, :]), reads=[po], writes=[stg])
                else:
                    fw.op("dve", lambda e: e.tensor_copy(out=stg[:, :], in_=po[:, :]), reads=[po], writes=[stg])
                state["so"] = so + 1
                fw.dma("sp", mixT_d[hh * 128:(hh + 1) * 128, t0:t0 + 512], stg[:, :], reads=[stg])

        nb_ = len(blocks)
        for n in range(nb_ + 2):
            fw.cast_tick((n + 1) / float(nb_))
            if n < nb_:
                stage_a1(n)
            if n >= 2:
                stage_b(n - 2)
            if 1 <= n <= nb_:
                stage_a2(n - 1)
            if n >= 2:
                stage_b2(n - 2)
        fw.barrier()
    with fw.phase():
        A = Attn(fw)
        pst = [fw.ps([128, 512], BF16, "pst") for _ in range(2)]
        stgs = [fw.sb([128, 512], BF16, "stg") for _ in range(3)]
        mem_part(fw, c, A, MK, S, qmT_d, mixT_d, pst, stgs, [0])
        fw.barrier()


def phase_mix_mlstm(fw, c, S, MK, qT_d, kT_d, ktm_d, v_d, o_d, g_d, qmT_d, mixT_d, cw_d, cb_d, gb_d, hn_d):
    NT = S // 128
    DH = 384
    with fw.phase():
        NQC = MW // 128
        cw = fw.sb([128, 2 * NQC, 4], F32, "cw")
        cb = fw.sb([128, 2 * NQC], F32, "cb")
        fw.dma("sp", cw[:, :, :], cw_d, writes=[cw])
        fw.dma("sp", cb[:, :], cb_d, writes=[cb])
        xins = [fw.sb([128, S + 4], BF16, "xin") for _ in range(2)]
        ys = [fw.sb([128, S], BF16, "y") for _ in range(2)]
        dgs = [fw.sb([128, 4, 128], BF16, "dg") for _ in range(2)]
        cbs = fw.sb([128, 2 * NQC], F32, "cbs")
        psc = [fw.ps([128, 512], F32, "psc") for _ in range(3)]
        pst = [fw.ps([128, 512], BF16, "pst") for _ in range(2)]
        stgs = [fw.sb([128, 4, 128], BF16, "stg") for _ in range(3)]
        kv = ktm_d.rearrange("(tb p) f -> p tb f", p=128)
        for x_ in xins:
            fw.op("dve", lambda e, x_=x_: e.memset(x_[:, 0:4], 0.0), writes=[x_])
        kscale = float(DH ** -0.5)
        n = 0
        pc = 0
        for cc in range(2 * NQC):
            src_d = qT_d if cc < NQC else kT_d
            r0 = (cc % NQC) * 128
            xin, y, dg = xins[cc % 2], ys[cc % 2], dgs[cc % 2]
            fw.dma("sp", xin[:, 4:S + 4], src_d[r0:r0 + 128, :], writes=[xin])
            for j in range(4):
                fw.op("dve", lambda e, dg=dg, j=j, cc=cc: e.tensor_scalar(out=dg[:, j, :], in0=c.ident_f[:, :], scalar1=cw[:, cc, j:j + 1], scalar2=None, op0=ALU.mult),
                      reads=[c.ident_f, cw], writes=[dg])
            for t0 in range(0, S, 512):
                ps = psc[pc % 3]
                pc += 1
                for j in range(4):
                    mm(fw, ps, ps[:, :], dg, dg[:, j, :], xin, xin[:, t0 + 1 + j:t0 + 1 + j + 512], start=(j == 0), stop=(j == 3))
                if cc < NQC:
                    fw.op("act", lambda e, ps=ps, y=y, t0=t0, cc=cc: e.activation(out=y[:, t0:t0 + 512], in_=ps[:, :], func=AF.Silu, bias=cb[:, cc:cc + 1], scale=1.0),
                          reads=[ps, cb], writes=[y])
                else:
                    fw.op("act", lambda e, ps=ps, y=y, t0=t0, cc=cc: e.activation(out=y[:, t0:t0 + 512], in_=ps[:, :], func=AF.Silu, bias=cb[:, cc:cc + 1], scale=1.0),
                          reads=[ps, cb], writes=[y])
                    fw.op("pool", lambda e, y=y, t0=t0: e.tensor_scalar(out=y[:, t0:t0 + 512], in0=y[:, t0:t0 + 512], scalar1=kscale, scalar2=None, op0=ALU.mult),
                          reads=[y], writes=[y])
            fw.dma("sp", src_d[r0:r0 + 128, :], y[:, :], reads=[y])
            if cc >= NQC:
                for tb0 in range(0, NT, 4):
                    ps = pst[n % 2]
                    st = stgs[n % 3]
                    n += 1
                    for i in range(4):
                        fw.op("pe", lambda e, ps=ps, y=y, i=i, tb0=tb0: e.transpose(out=ps[:, i * 128:(i + 1) * 128], in_=y[:, (tb0 + i) * 128:(tb0 + i + 1) * 128],
                                                                                   identity=c.ident[:]), reads=[y, c.ident], writes=[ps])
                    fw.op("dve", lambda e, ps=ps, st=st: e.tensor_copy(out=st[:, :, :], in_=ps[:, :].rearrange("p (a b) -> p a b", a=4)), reads=[ps], writes=[st])
                    fw.dma("sp", kv[:, tb0:tb0 + 4, r0:r0 + 128], st[:, :, :], reads=[st])
        fw.barrier()
    with fw.phase():
        NQC = MW // 128
        NG = NT * HM
        G = fw.sb([128, NT, 2 * HM], F32, "G")
        gb = fw.sb([128, 2 * HM], F32, "gb")
        hn = fw.sb([128, MW], F32, "hn")
        IG = fw.sb([128, NG], F32, "IG")
        LF = fw.sb([128, NG], F32, "LF")
        Bc = fw.sb([128, NG], F32, "Bc")
        BL = fw.sb([128, NG], F32, "BL")
        Ac = fw.sb([128, NG], F32, "Ac")
        WK = fw.sb([128, NG], F32, "WK")
        DEC = fw.sb([128, NG], F32, "DEC")
        tmp = fw.sb([128, NG], F32, "tmp")
        one1 = fw.sb([128, 1], F32, "one1")
        fw.op("dve", lambda e: e.memset(one1[:], 1.0), writes=[one1])
        onesf = fw.sb([128, 128], F32, "onesf")
        fw.op("dve", lambda e: e.memset(onesf[:], 1.0), writes=[onesf])
        trif = fw.sb([128, 128], F32, "trif")
        fw.op("pool", lambda e: e.affine_select(out=trif[:], in_=onesf[:], pattern=[[1, 128]], compare_op=ALU.is_ge,
                                                fill=0.0, base=0, channel_multiplier=-1), reads=[onesf], writes=[trif])
        fw.dma("sp", G[:, :, :], g_d.rearrange("(c p) e -> p c e", p=128), writes=[G])
        fw.dma("sp", gb[:, :], gb_d.partition_broadcast(128), writes=[gb])
        fw.dma("sp", hn[:, :], hn_d.partition_broadcast(128), writes=[hn])
        IGv = IG[:, :].rearrange("p (c h) -> p c h", h=HM)
        LFv = LF[:, :].rearrange("p (c h) -> p c h", h=HM)
        for cc in range(NT):
            fw.op("dve", lambda e, cc=cc: e.tensor_tensor(out=IGv[:, cc, :], in0=G[:, cc, 0:HM], in1=gb[:, 0:HM], op=ALU.add), reads=[G, gb], writes=[IG])
            fw.op("dve", lambda e, cc=cc: e.tensor_tensor(out=LFv[:, cc, :], in0=G[:, cc, HM:2 * HM], in1=gb[:, HM:2 * HM], op=ALU.add), reads=[G, gb], writes=[LF])
        fw.op("act", lambda e: e.activation(out=LF[:, :], in_=LF[:, :], func=AF.Exp, scale=-1.0), reads=[LF], writes=[LF])
        fw.op("act", lambda e: e.activation(out=LF[:, :], in_=LF[:, :], func=AF.Ln, bias=one1[:, 0:1], scale=1.0), reads=[LF, one1], writes=[LF])
        fw.op("dve", lambda e: e.tensor_scalar(out=LF[:, :], in0=LF[:, :], scalar1=-1.0, scalar2=None, op0=ALU.mult), reads=[LF], writes=[LF])
        ps_s = fw.ps([128, 512], F32, "ps_s")
        ps_b = fw.ps([128, 512], F32, "ps_b")
        psg = ps_b
        for c0 in range(0, NG, 512):
            c1 = min(NG, c0 + 512)
            mm(fw, psg, psg[:, 0:c1 - c0], trif, trif[:, :], LF, LF[:, c0:c1], True, True)
            fw.op("dve", lambda e, c0=c0, c1=c1: e.tensor_copy(out=Bc[:, c0:c1], in_=psg[:, 0:c1 - c0]), reads=[psg], writes=[Bc])
            mm(fw, psg, psg[:, 0:c1 - c0], onesf, onesf[:, :], LF, LF[:, c0:c1], True, True)
            fw.op("dve", lambda e, c0=c0, c1=c1: e.tensor_copy(out=BL[:, c0:c1], in_=psg[:, 0:c1 - c0]), reads=[psg], writes=[BL])
        fw.op("dve", lambda e: e.tensor_tensor(out=Ac[:, :], in0=IG[:, :], in1=Bc[:, :], op=ALU.subtract), reads=[IG, Bc], writes=[Ac])
        fw.op("dve", lambda e: e.tensor_tensor(out=tmp[:, :], in0=Ac[:, :], in1=BL[:, :], op=ALU.add), reads=[Ac, BL], writes=[tmp])
        fw.op("act", lambda e: e.activation(out=WK[:, :], in_=tmp[:, :], func=AF.Exp), reads=[tmp], writes=[WK])
        fw.op("act", lambda e: e.activation(out=DEC[:, :], in_=BL[:, :], func=AF.Exp), reads=[BL], writes=[DEC])
        CT = [fw.sb([128, 3, 385], F32, "CT") for _ in range(HM)]
        CTb = [fw.sb([128, 3, 385], BF16, "CTb") for _ in range(HM)]
        qcs = [fw.sb([128, NQC, 128], BF16, "qc") for _ in range(2)]
        kcs = [fw.sb([128, NQC, 128], BF16, "kc") for _ in range(2)]
        kts = [fw.sb([128, MW], BF16, "ktm") for _ in range(2)]
        vas = [fw.sb([128, HM, 385], BF16, "va") for _ in range(2)]
        ocs = [fw.sb([128, MW], BF16, "oc") for _ in range(2)]
        ogs = [fw.sb([128, MW], F32, "og") for _ in range(2)]
        mixtms = [fw.sb([128, 4, MW], BF16, "mixtm") for _ in range(2)]
        lfbs = [fw.sb([128, 128], F32, "lfb") for _ in range(2)]
        wts = [fw.sb([128, 128], F32, "wt") for _ in range(2)]
        ebrs = [fw.sb([128, 128], F32, "ebr") for _ in range(2)]
        ptb = [fw.sb([128, 128], BF16, "ptb") for _ in range(2)]
        qss = [fw.sb([128, 3, 128], BF16, "qs") for _ in range(2)]
        vws = [fw.sb([128, 385], BF16, "vw") for _ in range(2)]
        hgs = [fw.sb([128, 384], F32, "hg") for _ in range(2)]
        junk = fw.sb([128, 384], F32, "junk")
        sm = [fw.sb([128, 4], F32, "sm") for _ in range(4)]
        ps_n = [fw.ps([128, 512], F32, "ps_n") for _ in range(2)]
        ps_u = [fw.ps([128, 512], F32, "ps_u") for _ in range(3)]
        pst = [fw.ps([128, 512], BF16, "pst") for _ in range(1)]
        stgs = [fw.sb([128, 512], BF16, "stg") for _ in range(3)]
        cnt = [0]
        for v_ in vas:
            fw.op("dve", lambda e, v_=v_: e.memset(v_[:, :, 384:385], 1.0), writes=[v_])
        qv = qT_d.rearrange("(a p) s -> p a s", p=128)
        kvv = kT_d.rearrange("(a p) s -> p a s", p=128)
        n = 0
        for cc in range(NT):
            t0 = cc * 128
            qc, kc, ktm, va, oc, og = qcs[cc % 2], kcs[cc % 2], kts[cc % 2], vas[cc % 2], ocs[cc % 2], ogs[cc % 2]
            mixtm = mixtms[(cc // 4) % 2]
            fw.dma("sp", qc[:, :, :], qv[:, :, t0:t0 + 128], writes=[qc])
            fw.dma("sp", kc[:, :, :], kvv[:, :, t0:t0 + 128], writes=[kc])
            fw.dma("sp", ktm[:, :], ktm_d[t0:t0 + 128, :], writes=[ktm])
            fw.dma("sp", va[:, :, 0:384], v_d[t0:t0 + 128, :].rearrange("p (h d) -> p h d", h=HM), writes=[va])
            fw.dma("sp", oc[:, :], o_d[t0:t0 + 128, :], writes=[oc])
            fw.op("act", lambda e, og=og, oc=oc: e.activation(out=og[:, :], in_=oc[:, :], func=AF.Sigmoid), reads=[oc], writes=[og])
            fw.cast_tick((cc + 1) / float(NT))
            for hh in range(HM):
                ch = cc * HM + hh
                lfb, wt, ebr, pt, qs, vw, hg = lfbs[n % 2], wts[n % 2], ebrs[n % 2], ptb[n % 2], qss[n % 2], vws[n % 2], hgs[n % 2]
                pn = ps_n[n % 2]
                s1, s2 = sm[(2 * n) % 4], sm[(2 * n + 1) % 4]
                n += 1
                for dc in range(3):
                    mm(fw, ps_s, ps_s[:, 0:128], kc, kc[:, 3 * hh + dc, :], qc, qc[:, 3 * hh + dc, :], start=(dc == 0), stop=(dc == 2))
                fw.op("dve", lambda e, lfb=lfb, ch=ch: e.tensor_scalar(out=lfb[:, :], in0=onesf[:, :], scalar1=LF[:, ch:ch + 1], scalar2=None, op0=ALU.mult),
                      reads=[onesf, LF], writes=[lfb])
                mm(fw, ps_b, ps_b[:, 0:128], lfb, lfb[:, :], trif, trif[:, :], True, True)
                fw.op("act", lambda e, wt=wt, ch=ch: e.activation(out=wt[:, :], in_=ps_b[:, 0:128], func=AF.Exp, bias=Ac[:, ch:ch + 1], scale=1.0),
                      reads=[ps_b, Ac], writes=[wt])
                fw.op("act", lambda e, ebr=ebr: e.activation(out=ebr[:, :], in_=ps_b[:, 0:128], func=AF.Exp), reads=[ps_b], writes=[ebr])
                fw.op("pool", lambda e, wt=wt: e.tensor_tensor(out=wt[:, :], in0=wt[:, :], in1=trif[:, :], op=ALU.mult), reads=[wt, trif], writes=[wt])
                fw.op("dve", lambda e, pt=pt, wt=wt: e.tensor_tensor(out=pt[:, :], in0=ps_s[:, 0:128], in1=wt[:, :], op=ALU.mult), reads=[ps_s, wt], writes=[pt])
                mm(fw, pn, pn[:, 0:385], pt, pt[:, :], va, va[:, hh, :], start=True, stop=(cc == 0))
                if cc > 0:
                    for dc in range(3):
                        fw.op("dve", lambda e, qs=qs, dc=dc, hh=hh, ebr=ebr: e.tensor_tensor(out=qs[:, dc, :], in0=qc[:, 3 * hh + dc, :], in1=ebr[:, :], op=ALU.mult),
                              reads=[qc, ebr], writes=[qs])
                    for dc in range(3):
                        mm(fw, pn, pn[:, 0:385], qs, qs[:, dc, :], CTb[hh], CTb[hh][:, dc, :], start=False, stop=(dc == 2))
                fw.op("dve", lambda e, s1=s1, pn=pn: e.tensor_scalar(out=s1[:, 2:3], in0=pn[:, 384:385], scalar1=-1.0, scalar2=None, op0=ALU.mult), reads=[pn], writes=[s1])
                fw.op("dve", lambda e, s1=s1, pn=pn: e.tensor_tensor(out=s1[:, 0:1], in0=pn[:, 384:385], in1=s1[:, 2:3], op=ALU.max), reads=[pn, s1], writes=[s1])
                fw.op("dve", lambda e, s1=s1: e.tensor_scalar(out=s1[:, 0:1], in0=s1[:, 0:1], scalar1=1.0, scalar2=None, op0=ALU.max), reads=[s1], writes=[s1])
                fw.op("dve", lambda e, s1=s1: e.reciprocal(out=s1[:, 1:2], in_=s1[:, 0:1]), reads=[s1], writes=[s1])
                fw.op("dve", lambda e, hg=hg, pn=pn, s1=s1, hh=hh, og=og: e.scalar_tensor_tensor(out=hg[:, :], in0=pn[:, 0:384], scalar=s1[:, 1:2],
                                                                                                 in1=og[:, hh * 384:(hh + 1) * 384], op0=ALU.mult, op1=ALU.mult),
                      reads=[pn, s1, og], writes=[hg])
                fw.op("act", lambda e, hg=hg, s2=s2: e.activation(out=junk[:, :], in_=hg[:, :], func=AF.Square, accum_out=s2[:, 0:1]), reads=[hg], writes=[junk, s2])
                fw.op("act", lambda e, s2=s2: e.activation(out=s2[:, 1:2], in_=s2[:, 0:1], func=AF.Sqrt, bias=c.eps[:, 0:1], scale=1.0 / DH), reads=[s2, c.eps], writes=[s2])
                fw.op("dve", lambda e, s2=s2: e.reciprocal(out=s2[:, 2:3], in_=s2[:, 1:2]), reads=[s2], writes=[s2])
                fw.op("dve", lambda e, hg=hg, s2=s2, hh=hh, cc=cc, mixtm=mixtm: e.scalar_tensor_tensor(
                    out=mixtm[:, cc % 4, hh * 384:(hh + 1) * 384], in0=hg[:, :], scalar=s2[:, 2:3], in1=hn[:, hh * 384:(hh + 1) * 384], op0=ALU.mult, op1=ALU.mult),
                    reads=[hg, s2, hn], writes=[mixtm])
                if cc < NT - 1:
                    fw.op("pool", lambda e, vw=vw, va=va, hh=hh, ch=ch: e.tensor_scalar(out=vw[:, :], in0=va[:, hh, :], scalar1=WK[:, ch:ch + 1], scalar2=None, op0=ALU.mult),
                          reads=[va, WK], writes=[vw])
                    for dc in range(3):
                        mm(fw, ps_u[dc], ps_u[dc][:, 0:385], ktm, ktm[:, (3 * hh + dc) * 128:(3 * hh + dc + 1) * 128], vw, vw[:, :], True, True)
                    for dc in range(3):
                        if cc == 0:
                            fw.op("dve", lambda e, hh=hh, dc=dc: e.tensor_copy(out=CT[hh][:, dc, :], in_=ps_u[dc][:, 0:385]), reads=[ps_u[dc]], writes=[CT[hh]])
                        else:
                            fw.op("dve", lambda e, hh=hh, dc=dc, ch=ch: e.scalar_tensor_tensor(out=CT[hh][:, dc, :], in0=CT[hh][:, dc, :], scalar=DEC[:, ch:ch + 1],
                                                                                              in1=ps_u[dc][:, 0:385], op0=ALU.mult, op1=ALU.add),
                                  reads=[CT[hh], DEC, ps_u[dc]], writes=[CT[hh]])
                    fw.op("act", lambda e, hh=hh: e.copy(out=CTb[hh][:, :, :], in_=CT[hh][:, :, :]), reads=[CT[hh]], writes=[CTb[hh]])
            if cc % 4 == 3:
                mixer_store(fw, c, mixtm, mixT_d, (cc - 3) * 128, pst, stgs, cnt, fcs=range(NQC))
        fw.barrier()
    with fw.phase():
        A = Attn(fw)
        pst = [fw.ps([128, 512], BF16, "pst") for _ in range(2)]
        stgs = [fw.sb([128, 512], BF16, "stg") for _ in range(3)]
        mem_part(fw, c, A, MK, S, qmT_d, mixT_d, pst, stgs, [0])
        fw.barrier()


def attn_finish_gated(fw, A, outs, sum_col, gate_b, gate_ap, acc_b, acc_ap, mode, dst_b=None, dst_ap=None, dv=128):
    A.q.append(lambda: _attn_finish_gated(fw, A, outs, sum_col, gate_b, gate_ap, acc_b, acc_ap, mode, dst_b, dst_ap, dv))


def _attn_finish_gated(fw, A, outs, sum_col, gate_b, gate_ap, acc_b, acc_ap, mode, dst_b=None, dst_ap=None, dv=128):
    for qb, (bank, col) in enumerate(outs):
        rc = A.rc[A.ri % 4]
        A.ri += 1
        fw.op("dve", lambda e, rc=rc, bank=bank, col=col: e.tensor_scalar(out=rc[:, 0:1], in0=bank[:, col + sum_col:col + sum_col + 1], scalar1=1e-30,
                                                                           scalar2=None, op0=ALU.max), reads=[bank], writes=[rc])
        fw.op("dve", lambda e, rc=rc: e.reciprocal(out=rc[:, 1:2], in_=rc[:, 0:1]), reads=[rc], writes=[rc])
        fw.op("dve", lambda e, rc=rc, qb=qb: e.tensor_tensor(out=rc[:, 2:3], in0=rc[:, 1:2], in1=gate_ap(qb), op=ALU.mult), reads=[rc, gate_b], writes=[rc])
        if mode == "set":
            fw.op("dve", lambda e, rc=rc, bank=bank, col=col, qb=qb: e.tensor_scalar(out=acc_ap(qb), in0=bank[:, col:col + dv], scalar1=rc[:, 2:3], scalar2=None,
                                                                                      op0=ALU.mult), reads=[bank, rc], writes=[acc_b])
        elif mode == "add":
            fw.op("dve", lambda e, rc=rc, bank=bank, col=col, qb=qb: e.scalar_tensor_tensor(out=acc_ap(qb), in0=bank[:, col:col + dv], scalar=rc[:, 2:3], in1=acc_ap(qb),
                                                                                             op0=ALU.mult, op1=ALU.add), reads=[bank, rc, acc_b], writes=[acc_b])
        else:
            fw.op("dve", lambda e, rc=rc, bank=bank, col=col, qb=qb: e.scalar_tensor_tensor(out=dst_ap(qb), in0=bank[:, col:col + dv], scalar=rc[:, 2:3], in1=acc_ap(qb),
                                                                                             op0=ALU.mult, op1=ALU.add), reads=[bank, rc, acc_b], writes=[dst_b])


def phase_mix_nsa(fw, c, S, MK, qT_d, kvT_d, v_d, g_d, qmT_d, mixT_d, peT_d, w1_bf, w2_bf):
    NT = S // 128
    NQ = S // 512
    NCMP = (S - 32) // 16 + 1
    NCT = (NCMP + 127) // 128
    scale = 128 ** -0.5
    with fw.phase():
        KC = [fw.sb([128, 256], BF16, "KC") for _ in range(NGN)]
        VC = [fw.sb([128, 2, 193], BF16, "VC") for _ in range(NGN)]
        KS = [fw.sb([128, S], BF16, "KS") for _ in range(NGN)]
        KW = [fw.sb([128, S], BF16, "KW") for _ in range(NGN)]
        VS = [fw.sb([128, NT, 129], BF16, "VS") for _ in range(NGN)]
        VW = [fw.sb([128, NT, 129], BF16, "VW") for _ in range(NGN)]
        ones5 = fw.sb([128, 512], BF16, "ones5")
        fw.op("dve", lambda e: e.memset(ones5[:], 1.0), writes=[ones5])
        A = Attn(fw)
        pst = [fw.ps([128, 512], BF16, "pst") for _ in range(1)]
        psm = fw.ps([128, 512], F32, "psm")
        vv = v_d.rearrange("(kt p) f -> p kt f", p=128)
        GB = NGN * 128
        for g in range(NGN):
            fw.dma("sp", KS[g][:, :], kvT_d[2 * GB + g * 128:2 * GB + (g + 1) * 128, :], writes=[KS[g]])
            fw.dma("sp", KW[g][:, :], kvT_d[3 * GB + g * 128:3 * GB + (g + 1) * 128, :], writes=[KW[g]])
            for k0 in range(0, NT, 8):
                k1 = min(NT, k0 + 8)
                fw.dma("sp", VS[g][:, k0:k1, 0:128], vv[:, k0:k1, g * 128:(g + 1) * 128], writes=[VS[g]], cont=(k0 > 0))
                fw.dma("sp", VW[g][:, k0:k1, 0:128], vv[:, k0:k1, GB + g * 128:GB + (g + 1) * 128], writes=[VW[g]], cont=(k0 > 0))
            fw.op("dve", lambda e, g=g: e.memset(VS[g][:, :, 128:129], 1.0), writes=[VS[g]])
            fw.op("dve", lambda e, g=g: e.memset(VW[g][:, :, 128:129], 1.0), writes=[VW[g]])
            fw.op("dve", lambda e, g=g: e.memset(VC[g][:, :, 128:193], 1.0), writes=[VC[g]])
            for nt in range(2):
                fw.op("pool", lambda e, g=g, nt=nt: e.affine_select(out=VC[g][:, nt, 129:193], in_=VC[g][:, nt, 129:193], pattern=[[-4, 64]], compare_op=ALU.is_gt,
                                                                    fill=0.0, base=nt * 128 + 2, channel_multiplier=1), reads=[VC[g]], writes=[VC[g]])
                fw.op("pool", lambda e, g=g, nt=nt: e.affine_select(out=VC[g][:, nt, 129:193], in_=VC[g][:, nt, 129:193], pattern=[[4, 64]], compare_op=ALU.is_gt,
                                                                    fill=0.0, base=4 - nt * 128, channel_multiplier=-1), reads=[VC[g]], writes=[VC[g]])
        with ExitStack() as cst:
            save = fw.pstack
            fw.pstack = cst
            xc = [fw.sb([128, S], BF16, "xc") for _ in range(2)]
            w1s = [fw.sb([128, 32, 256], BF16, "w1s") for _ in range(2)]
            w2s = [fw.sb([128, 2, 128], BF16, "w2s") for _ in range(2)]
            pef = fw.sb([128, 32], F32, "pef")
            peb = [fw.sb([128, 32], BF16, "peb") for _ in range(2)]
            cbias = [fw.sb([128, 2], F32, "cbias") for _ in range(2)]
            xh = [fw.sb([128, 256], F32, "xh") for _ in range(2)]
            x2 = [fw.sb([128, 256], F32, "x2") for _ in range(2)]
            hact = [[fw.sb([128, 256], BF16, "hact") for _ in range(2)] for _ in range(2)]
            it = 0
            for which in range(2):
                w1v = w1_bf[which].rearrange("(l p) j -> p l j", p=128)
                w2v = w2_bf[which].rearrange("(jc p) d -> p jc d", p=128)
                w1, w2, pb, cbs = w1s[which], w2s[which], peb[which], cbias[which]
                fw.dma("sp", w1[:, 0:16, :], w1v[:, 0:16, :], writes=[w1])
                fw.dma("sp", w1[:, 16:32, :], w1v[:, 16:32, :], writes=[w1], cont=True)
                fw.dma("sp", w2[:, :, :], w2v, writes=[w2])
                fw.dma("sp", pef[:, :], peT_d[which], writes=[pef])
                fw.op("dve", lambda e, pb=pb: e.tensor_copy(out=pb[:, :], in_=pef[:, :]), reads=[pef], writes=[pb])
                for jc in range(2):
                    for l in range(32):
                        mm(fw, psm, psm[:, 0:1], w1, w1[:, l, jc * 128:(jc + 1) * 128], pb, pb[:, l:l + 1], start=(l == 0), stop=(l == 31))
                    fw.op("dve", lambda e, cbs=cbs, jc=jc: e.tensor_copy(out=cbs[:, jc:jc + 1], in_=psm[:, 0:1]), reads=[psm], writes=[cbs])
                for g in range(NGN):
                    x = xc[it % 2]
                    it += 1
                    r0 = which * GB + g * 128
                    fw.dma("sp", x[:, :], kvT_d[r0:r0 + 128, :], writes=[x])
                    for jc in range(2):
                        ha = hact[jc][g]
                        xhh, xx2 = xh[jc], x2[jc]
                        for l in range(32):
                            mm(fw, psm, psm[:, 0:NCMP], w1, w1[:, l, jc * 128:(jc + 1) * 128], x, x[:, l:l + 16 * (NCMP - 1) + 1:16], start=(l == 0), stop=(l == 31))
                        fw.op("act", lambda e, xhh=xhh, cbs=cbs, jc=jc: e.activation(out=xhh[:, 0:NCMP], in_=psm[:, 0:NCMP], func=AF.Identity, bias=cbs[:, jc:jc + 1], scale=1.0),
                              reads=[psm, cbs], writes=[xhh])
                        fw.op("dve", lambda e, xhh=xhh, xx2=xx2: e.tensor_tensor(out=xx2[:, 0:NCMP], in0=xhh[:, 0:NCMP], in1=xhh[:, 0:NCMP], op=ALU.mult), reads=[xhh], writes=[xx2])
                        fw.op("dve", lambda e, xx2=xx2: e.tensor_scalar(out=xx2[:, 0:NCMP], in0=xx2[:, 0:NCMP], scalar1=0.044715, scalar2=1.0, op0=ALU.mult, op1=ALU.add),
                              reads=[xx2], writes=[xx2])
                        fw.op("dve", lambda e, xhh=xhh, xx2=xx2: e.tensor_tensor(out=xx2[:, 0:NCMP], in0=xx2[:, 0:NCMP], in1=xhh[:, 0:NCMP], op=ALU.mult), reads=[xhh, xx2], writes=[xx2])
                        fw.op("act", lambda e, xx2=xx2: e.activation(out=xx2[:, 0:NCMP], in_=xx2[:, 0:NCMP], func=AF.Sigmoid, scale=1.5957691216057308), reads=[xx2], writes=[xx2])
                        fw.op("dve", lambda e, xhh=xhh, xx2=xx2, ha=ha: e.tensor_tensor(out=ha[:, 0:NCMP], in0=xx2[:, 0:NCMP], in1=xhh[:, 0:NCMP], op=ALU.mult), reads=[xhh, xx2], writes=[ha])
                    if which == 0:
                        for jc in range(2):
                            mm(fw, psm, psm[:, 0:NCMP], w2, w2[:, jc, :], hact[jc][g], hact[jc][g][:, 0:NCMP], start=(jc == 0), stop=(jc == 1))
                        fw.op("dve", lambda e, g=g: e.tensor_copy(out=KC[g][:, 0:NCMP], in_=psm[:, 0:NCMP]), reads=[psm], writes=[KC[g]])
                    else:
                        for nt in range(NCT):
                            sz = min(128, NCMP - nt * 128)
                            for jc in range(2):
                                mm(fw, psm, psm[0:sz, 0:128], hact[jc][g], hact[jc][g][:, nt * 128:nt * 128 + sz], w2, w2[:, jc, :], start=(jc == 0), stop=(jc == 1))
                            fw.op("dve", lambda e, g=g, nt=nt, sz=sz: e.tensor_copy(out=VC[g][0:sz, nt, 0:128], in_=psm[0:sz, 0:128]), reads=[psm], writes=[VC[g]])
            fw.barrier()
            fw.pstack = save
        CM = []
        for i in range(4):
            m = fw.sb([128, 512], BF16, "CM")
            fw.op("pool", lambda e, m=m, i=i: e.affine_select(out=m[:], in_=ones5[:], pattern=[[1, 512]], compare_op=ALU.is_ge, fill=0.0, base=-128 * i, channel_multiplier=-1),
                  reads=[ones5], writes=[m])
            CM.append(m)
        WM = {}
        for i in range(-4, 0):
            m = fw.sb([128, 512], BF16, "WM")
            fw.op("pool", lambda e, m=m, i=i: e.affine_select(out=m[:], in_=ones5[:], pattern=[[-1, 512]], compare_op=ALU.is_ge, fill=0.0, base=511 + 128 * i, channel_multiplier=1),
                  reads=[ones5], writes=[m])
            WM[i] = m
        EXP = fw.sb([64, S], BF16, "EXP")
        fw.op("dve", lambda e: e.memset(EXP[:], 1.0), writes=[EXP])
        fw.op("pool", lambda e: e.affine_select(out=EXP[:], in_=EXP[:], pattern=[[1, S]], compare_op=ALU.is_ge, fill=0.0, base=0, channel_multiplier=-64), reads=[EXP], writes=[EXP])
        fw.op("pool", lambda e: e.affine_select(out=EXP[:], in_=EXP[:], pattern=[[-1, S]], compare_op=ALU.is_ge, fill=0.0, base=63, channel_multiplier=64), reads=[EXP], writes=[EXP])
        cms = [fw.sb([128, 512], BF16, "cmk") for _ in range(2)]
        mks = [fw.sb([128, 512], BF16, "mk") for _ in range(NT)]
        qts = [fw.sb([128, HQ, 512], BF16, "qt") for _ in range(1)]
        qms = [fw.sb([128, HM, 512], BF16, "qm") for _ in range(1)]
        GS = fw.sb([128, 4, HQ * 3], F32, "GS")
        IMP = fw.sb([128, 4, NGN, 64], F32, "IMP")
        SC = fw.sb([128, 64], F32, "SC")
        SC2 = fw.sb([128, 64], F32, "SC2")
        M8 = fw.sb([128, 16], F32, "M8")
        SEL = fw.sb([128, 64], BF16, "SEL")
        SELT = [fw.sb([64, 512], BF16, "SELT") for _ in range(NGN)]
        acc = [fw.sb([128, 4, 128], F32, "acc") for _ in range(2)]
        mixtm = fw.sb([128, 4, FWD], BF16, "mixtm")
        stgs = [fw.sb([128, 512], BF16, "stg") for _ in range(3)]
        cnt = [0]
        qv = qT_d.rearrange("(a p) s -> p a s", p=128)
        qmv = qmT_d.rearrange("(a p) s -> p a s", p=128)
        for qt in range(NQ):
            t0 = qt * 512
            qtb, qmb = qts[0], qms[0]
            fw.dma("sp", qtb[:, 0:HQ // 2, :], qv[:, 0:HQ // 2, t0:t0 + 512], writes=[qtb])
            fw.dma("sp", qtb[:, HQ // 2:HQ, :], qv[:, HQ // 2:HQ, t0:t0 + 512], writes=[qtb], cont=True)
            fw.dma("sp", qmb[:, :, :], qmv[:, :, t0:t0 + 512], writes=[qmb])
            fw.dma("sp", GS[:, :, :], g_d[t0:t0 + 512, :].rearrange("(qb p) e -> p qb e", p=128), writes=[GS])
            fw.op("act", lambda e: e.activation(out=GS[:, :, :], in_=GS[:, :, :], func=AF.Sigmoid), reads=[GS], writes=[GS])
            ckt = []
            for nt in range(NCT):
                sz = min(128, NCMP - nt * 128)
                first_t = 16 * (nt * 128) + 31
                last_t = 16 * (nt * 128 + sz - 1) + 31
                if t0 + 511 < first_t:
                    continue
                if t0 >= last_t:
                    ckt.append((nt, sz, None))
                else:
                    cm = cms[nt % 2]
                    fw.op("pool", lambda e, cm=cm, nt=nt: e.affine_select(out=cm[:], in_=ones5[:], pattern=[[1, 512]], compare_op=ALU.is_ge, fill=0.0,
                                                                          base=t0 - 31 - 2048 * nt, channel_multiplier=-16), reads=[ones5], writes=[cm])
                    ckt.append((nt, sz, cm))

            def cmp_tiles(g, nv):
                return [(KC[g], KC[g][:, nt * 128:nt * 128 + sz], VC[g], VC[g][0:sz, nt, 0:nv], cm, (cm[0:sz, :] if cm is not None else None), 0, 4, sz)
                        for (nt, sz, cm) in ckt]


            def imp_acc(outs, hh, g):
                for qb, (bank, col) in enumerate(outs):
                    rc = A.rc[A.ri % 4]
                    A.ri += 1
                    fw.op("dve", lambda e, rc=rc, bank=bank, col=col: e.tensor_scalar(out=rc[:, 0:1], in0=bank[:, col + 128:col + 129], scalar1=1e-30, scalar2=None, op0=ALU.max),
                          reads=[bank], writes=[rc])
                    fw.op("dve", lambda e, rc=rc: e.reciprocal(out=rc[:, 1:2], in_=rc[:, 0:1]), reads=[rc], writes=[rc])
                    if hh % 6 == 0:
                        fw.op("dve", lambda e, rc=rc, bank=bank, col=col, qb=qb, g=g: e.tensor_scalar(out=IMP[:, qb, g, :], in0=bank[:, col + 129:col + 193], scalar1=rc[:, 1:2],
                                                                                                      scalar2=None, op0=ALU.mult), reads=[bank, rc], writes=[IMP])
                    else:
                        fw.op("dve", lambda e, rc=rc, bank=bank, col=col, qb=qb, g=g: e.scalar_tensor_tensor(out=IMP[:, qb, g, :], in0=bank[:, col + 129:col + 193], scalar=rc[:, 1:2],
                                                                                                             in1=IMP[:, qb, g, :], op0=ALU.mult, op1=ALU.add),
                              reads=[bank, rc, IMP], writes=[IMP])
            for hh in range(HQ):
                g = hh // 6
                outs = attn_head(fw, A, qtb, lambda lo, hi, hh=hh: qtb[:, hh, lo * 128:hi * 128], cmp_tiles(g, 193), 193, scale)
                A.q.append(lambda outs=outs, hh=hh, g=g: imp_acc(outs, hh, g))
            A.flush()
            for g in range(NGN):
                for qb in range(4):
                    qa = 4 * qt + qb
                    fw.op("dve", lambda e, qb=qb, g=g: e.tensor_copy(out=SC[:, :], in_=IMP[:, qb, g, :]), reads=[IMP], writes=[SC])
                    for hf in range(2):
                        cur = 2 * qa + hf
                        rs = slice(hf * 64, hf * 64 + 64)
                        fw.op("dve", lambda e, rs=rs: e.memset(SC[rs, 0:1], 1e6), writes=[SC])
                        fw.op("dve", lambda e, rs=rs, cur=cur: e.memset(SC[rs, max(cur - 1, 0):cur + 1], 1e6), writes=[SC])
                        if cur < 63:
                            fw.op("dve", lambda e, rs=rs, cur=cur: e.memset(SC[rs, cur + 1:64], -1e9), writes=[SC])
                    fw.op("dve", lambda e: e.max(out=M8[:, 0:8], in_=SC[:, :]), reads=[SC], writes=[M8])
                    fw.op("dve", lambda e: e.match_replace(out=SC2[:, :], in_to_replace=M8[:, 0:8], in_values=SC[:, :], imm_value=-3e9), reads=[SC, M8], writes=[SC2])
                    fw.op("dve", lambda e: e.max(out=M8[:, 8:16], in_=SC2[:, :]), reads=[SC2], writes=[M8])
                    fw.op("dve", lambda e: e.tensor_scalar(out=SEL[:, :], in0=SC[:, :], scalar1=M8[:, 15:16], scalar2=None, op0=ALU.is_ge), reads=[SC, M8], writes=[SEL])
                    for hf in range(2):
                        cur = 2 * qa + hf
                        if cur < 63:
                            fw.op("dve", lambda e, hf=hf, cur=cur: e.memset(SEL[hf * 64:hf * 64 + 64, cur + 1:64], 0.0), writes=[SEL])
                    ps = pst[0]
                    fw.op("pe", lambda e, ps=ps: e.transpose(out=ps[0:64, 0:128], in_=SEL[:, :], identity=c.ident[:]), reads=[SEL, c.ident], writes=[ps])
                    fw.op("act", lambda e, ps=ps, g=g, qb=qb: e.copy(out=SELT[g][:, qb * 128:(qb + 1) * 128], in_=ps[0:64, 0:128]), reads=[ps], writes=[SELT[g]])
                nkt = 4 * qt + 4
                for kt in range(nkt):
                    i = kt - 4 * qt
                    mm(fw, psm, psm[:, :], EXP, EXP[:, kt * 128:(kt + 1) * 128], SELT[g], SELT[g][:, :], True, True)
                    mk = mks[kt]
                    if i >= 0:
                        fw.op("dve", lambda e, mk=mk, i=i: e.tensor_tensor(out=mk[:, :], in0=psm[:, :], in1=CM[i][:, :], op=ALU.mult), reads=[psm, CM[i]], writes=[mk])
                    elif kt % 2 == 0:
                        fw.op("act", lambda e, mk=mk: e.copy(out=mk[:, :], in_=psm[:, :]), reads=[psm], writes=[mk])
                    else:
                        fw.op("dve", lambda e, mk=mk: e.tensor_copy(out=mk[:, :], in_=psm[:, :]), reads=[psm], writes=[mk])
                for hp in range(6):
                    hh = g * 6 + hp
                    fw.cast_tick((qt * HQ + hh + 1) / float(NQ * HQ))
                    ac = acc[hh % 2]
                    qf = lambda lo, hi, hh=hh: qtb[:, hh, lo * 128:hi * 128]
                    outs = attn_head(fw, A, qtb, qf, cmp_tiles(g, 129), 129, scale)
                    attn_finish_gated(fw, A, outs, 128, GS, lambda qb, hh=hh: GS[:, qb, hh * 3:hh * 3 + 1], ac, lambda qb, ac=ac: ac[:, qb, :], "set")
                    kts = []
                    for kt in range(nkt):
                        i = kt - 4 * qt
                        kts.append((KS[g], KS[g][:, kt * 128:(kt + 1) * 128], VS[g], VS[g][:, kt, :], mks[kt], mks[kt][:, max(i, 0) * 128:512], max(i, 0), 4))
                    outs = attn_head(fw, A, qtb, qf, kts, 129, scale, mask_eng="alt")
                    attn_finish_gated(fw, A, outs, 128, GS, lambda qb, hh=hh: GS[:, qb, hh * 3 + 1:hh * 3 + 2], ac, lambda qb, ac=ac: ac[:, qb, :], "add")
                    kts = []
                    for i in range(-4, 4):
                        kt = 4 * qt + i
                        if kt < 0:
                            continue
                        if i >= 0:
                            lo, hi, mb = i, 4, CM[i]
                        else:
                            lo, hi, mb = 0, 5 + i, WM[i]
                        kts.append((KW[g], KW[g][:, kt * 128:(kt + 1) * 128], VW[g], VW[g][:, kt, :], mb, mb[:, lo * 128:hi * 128], lo, hi))
                    outs = attn_head(fw, A, qtb, qf, kts, 129, scale)
                    attn_finish_gated(fw, A, outs, 128, GS, lambda qb, hh=hh: GS[:, qb, hh * 3 + 2:hh * 3 + 3], ac, lambda qb, ac=ac: ac[:, qb, :], "final",
                                      dst_b=mixtm, dst_ap=lambda qb, hh=hh: mixtm[:, qb, hh * 128:(hh + 1) * 128])
            mem_attention_qtile(fw, A, MK, qmb, mixtm)
            A.flush()
            mixer_store(fw, c, mixtm, mixT_d, t0, pst, stgs, cnt, fcs=range(NFC))
        fw.barrier()


W_IN = {0: 3620, 1: 5120, 2: 6664, 3: 2432}
W_MY = {0: MW + 6 * NGN * 128 + HQ * 3 + MEMW, 1: 3 * MW + MEMW, 2: 4 * MW + 2 * HM + MEMW, 3: MW + 2 * NGS * 64 + MEMW}
DEBUG = set()
LAST = {}
GROUPS = [[0, 1], [2, 3], [4, 5], [6, 7]]


def cc_rows(SH):
    return max(128, min(D, (1 << 20) // SH))


def gl(v):
    return np.ascontiguousarray(np.asarray(v, np.float32).reshape(NCH, 128).T)


def build(S, layers):
    SH = S // NR
    nc = bass.Bass("TRN2", target_bir_lowering=False)
    dr = {}

    def din(name, shape, dt=F32):
        dr[name] = nc.dram_tensor(name, list(shape), dt, kind="ExternalInput").ap()
        return dr[name]

    def dsc(name, shape, dt):
        dr[name] = nc.dram_tensor(name, list(shape), dt, kind=("ExternalOutput" if name in DEBUG else "Internal")).ap()
        return dr[name]

    din("xT", [D, SH])
    din("memT", [D, MEM])
    din("mem_norm", [128, NCH])
    din("mem_w_kv", [D, 2 * HM * 128])
    din("final_norm", [128, NCH])
    casts = [("mem_w_kv", [D, 2 * HM * 128])]
    lcasts = {}
    for l in layers:
        ncast0 = len(casts)
        p = "l%d_" % l
        din(p + "norm_mix", [128, NCH])
        din(p + "norm_ffn", [128, NCH])
        din(p + "w_in", [D, W_MY[l]])
        din(p + "w_out", [FWD, D])
        din(p + "w_gate", [D, DFF])
        din(p + "w_up", [D, DFF])
        din(p + "w_down", [DFF, D])
        casts += [(p + "w_in", [D, W_MY[l]]), (p + "w_out", [FWD, D]), (p + "w_gate", [D, DFF]), (p + "w_up", [D, DFF]),
                  (p + "w_down", [DFF, D])]
        if l == 3:
            din(p + "sinks", [HS])
        if l == 0:
            for kv_ in ("k", "v"):
                din(p + "cmp_pe_" + kv_, [128, 32])
                din(p + "cmp_w1_" + kv_, [4096, 256])
                din(p + "cmp_w2_" + kv_, [256, 128])
                casts += [(p + "cmp_w1_" + kv_, [4096, 256]), (p + "cmp_w2_" + kv_, [256, 128])]
        if l == 2:
            din(p + "conv_w", [128, 2 * MW // 128, 4])
            din(p + "conv_b", [128, 2 * MW // 128])
            din(p + "gate_b", [2 * HM])
            din(p + "head_norm", [MW])
        lcasts[l] = [n for n, _ in casts[ncast0:]]
    for name, shp in casts:
        dsc(name + "_bf", shp, BF16)
    dsc("xres", [D, SH], F32)
    dsc("hT", [D, SH], BF16)
    dsc("Gh", [NR * D, SH], BF16)
    dsc("ypart", [NR * D, SH], BF16)
    dsc("ymy", [D, SH], BF16)
    dsc("mixT", [FWD, S], BF16)
    dsc("qT", [MW, S], BF16)
    dsc("qmT", [MEMW, S], BF16)
    dsc("kT", [MW, S], BF16)
    dsc("vtm", [S, MW], BF16)
    dsc("otm", [S, MW], BF16)
    dsc("ktm", [S, MW], BF16)
    dsc("gtm", [S, 2 * HM], F32)
    dsc("g36", [S, HQ * 3], F32)
    out = nc.dram_tensor("outT", [D, SH], F32, kind="ExternalOutput").ap()

    with ExitStack() as stack:
        fw = FW(nc, stack)
        c = make_consts(fw)
        if "dbg_pt" in DEBUG:
            fw.dbg = {"pt": dsc("dbg_pt", [128, 512], BF16), "o": dsc("dbg_o", [128, 2, 512], F32), "mix": dsc("dbg_mix", [128, 4, FWD], BF16)}
            fw.dbg_sb = fw.sbg([128, 2, 512], F32, "dbg_sb")
        l0 = layers[0]
        ffn_names = ["l%d_w_gate" % l0, "l%d_w_up" % l0, "l%d_w_down" % l0]
        phase_cast_weights(fw, [(dr[n], dr[n + "_bf"]) for n in (["mem_w_kv"] + [n for n in lcasts[l0] if n not in ffn_names])])
        issue_casts(fw, [(dr[n], dr[n + "_bf"]) for n in ffn_names])
        MK = phase_mem(fw, c, dr["memT"], dr["mem_norm"], dr["mem_w_kv_bf"])
        x_cur = dr["xT"]
        for li, l in enumerate(layers):
            p = "l%d_" % l
            last = li == len(layers) - 1
            phase_n1(fw, c, SH, x_cur, dr[p + "norm_mix"], dr["hT"])
            if not last:
                fw.set_casts([(dr[n], dr[n + "_bf"]) for n in lcasts[layers[li + 1]]])
            RC = cc_rows(SH)
            for ci in range(D // RC):
                fw.collective("AllGather", dr["hT"][ci * RC:(ci + 1) * RC, :], dr["Gh"][ci * NR * RC:(ci + 1) * NR * RC, :], GROUPS)
            wbf = dr[p + "w_in_bf"]
            if li > 0:
                fw.barrier(wait_casts=True)
            if l == 3:
                kw = NGS * 64
                segs = [(0, MW, "FM", dr["qT"], BF16), (MW, kw, "FM", dr["kT"][0:kw, :], BF16),
                        (MW + kw, kw, "TM", dr["vtm"][:, 0:kw], BF16), (MW + 2 * kw, MEMW, "FM", dr["qmT"], BF16)]
                phase_in(fw, c, S, SH, dr["Gh"], wbf, segs)
                phase_mix_swa(fw, c, S, MK, dr["qT"], dr["kT"][0:kw, :], dr["vtm"][:, 0:kw], dr["qmT"], dr[p + "sinks"], dr["mixT"])
            elif l == 1:
                segs = [(0, MW, "FM", dr["qT"], BF16), (MW, MW, "FM", dr["kT"], BF16),
                        (2 * MW, MW, "TM", dr["vtm"], BF16), (3 * MW, MEMW, "FM", dr["qmT"], BF16)]
                phase_in(fw, c, S, SH, dr["Gh"], wbf, segs)
                phase_mix_sb(fw, c, S, MK, dr["qT"], dr["kT"], dr["vtm"], dr["qmT"], dr["mixT"])
            elif l == 0:
                gb_ = NGN * 128
                o = MW
                segs = [(0, MW, "FM", dr["qT"], BF16), (o, 3 * gb_, "FM", dr["kT"][0:3 * gb_, :], BF16),
                        (o + 3 * gb_, gb_, "TM", dr["vtm"][:, 0:gb_], BF16), (o + 4 * gb_, gb_, "FM", dr["kT"][3 * gb_:4 * gb_, :], BF16),
                        (o + 5 * gb_, gb_, "TM", dr["vtm"][:, gb_:2 * gb_], BF16), (o + 6 * gb_, HQ * 3, "TM", dr["g36"], F32),
                        (o + 6 * gb_ + HQ * 3, MEMW, "FM", dr["qmT"], BF16)]
                phase_in(fw, c, S, SH, dr["Gh"], wbf, segs)
                phase_mix_nsa(fw, c, S, MK, dr["qT"], dr["kT"], dr["vtm"], dr["g36"], dr["qmT"], dr["mixT"],
                              [dr[p + "cmp_pe_k"], dr[p + "cmp_pe_v"]], [dr[p + "cmp_w1_k_bf"], dr[p + "cmp_w1_v_bf"]],
                              [dr[p + "cmp_w2_k_bf"], dr[p + "cmp_w2_v_bf"]])
            elif l == 2:
                segs = [(0, MW, "FM", dr["qT"], BF16), (MW, MW, "FM", dr["kT"], BF16),
                        (2 * MW, MW, "TM", dr["vtm"], BF16), (3 * MW, MW, "TM", dr["otm"], BF16),
                        (4 * MW, 2 * HM, "TM", dr["gtm"], F32), (4 * MW + 2 * HM, MEMW, "FM", dr["qmT"], BF16)]
                phase_in(fw, c, S, SH, dr["Gh"], wbf, segs)
                phase_mix_mlstm(fw, c, S, MK, dr["qT"], dr["kT"], dr["ktm"], dr["vtm"], dr["otm"], dr["gtm"], dr["qmT"], dr["mixT"],
                                dr[p + "conv_w"], dr[p + "conv_b"], dr[p + "gate_b"], dr[p + "head_norm"])
            phase_op(fw, c, S, SH, dr["mixT"], dr[p + "w_out_bf"], dr["ypart"])
            for ci in range(D // RC):
                fw.collective("ReduceScatter", dr["ypart"][ci * NR * RC:(ci + 1) * NR * RC, :], dr["ymy"][ci * RC:(ci + 1) * RC, :], GROUPS)
            if li == 0:
                fw.barrier(wait_casts=True)
            phase_ffn(fw, c, SH, x_cur, dr["xres"], dr["ymy"], dr[p + "norm_ffn"],
                      dr[p + "w_gate_bf"], dr[p + "w_up_bf"], dr[p + "w_down_bf"],
                      final_g=(dr["final_norm"] if last else None), out_dram=(out if last else None))
            x_cur = dr["xres"]
    return nc


def f32c(a):
    return np.ascontiguousarray(np.asarray(a, np.float32))


def host_inputs(inp, b, r, layers, S):
    SH = S // NR
    m = {}
    m["xT"] = f32c(np.asarray(inp["x"][b], np.float32)[r * SH:(r + 1) * SH].T)
    m["memT"] = f32c(np.asarray(inp["mem"][b], np.float32).T)
    m["mem_norm"] = gl(inp["mem_norm"])
    wkv = np.asarray(inp["mem_w_kv"], np.float32)
    mh = np.arange(r * HM * 128, (r + 1) * HM * 128)
    m["mem_w_kv"] = f32c(wkv[:, np.concatenate([mh, 512 + mh])])
    m["final_norm"] = gl(inp["final_norm"])
    memq = np.arange(r * MEMW, (r + 1) * MEMW)
    mixf = np.arange(r * MW, (r + 1) * MW)
    for l in layers:
        p = "l%d_" % l
        m[p + "norm_mix"] = gl(inp[p + "norm_mix"])
        m[p + "norm_ffn"] = gl(inp[p + "norm_ffn"])
        for w in ("w_gate", "w_up", "w_down"):
            m[p + w] = f32c(inp[p + w])
        w_in = np.asarray(inp[p + "w_in"], np.float32)
        w_out = np.asarray(inp[p + "w_out"], np.float32)
        feats = mixf
        if l == 0:
            gs = np.arange(r * NGN, (r + 1) * NGN)
            kvc = [1536 + j * 256 + np.concatenate([g * 128 + np.arange(128) for g in gs]) for j in range(6)]
            cols = np.concatenate([mixf] + kvc + [3072 + np.arange(r * HQ * 3, (r + 1) * HQ * 3), 3108 + memq])
            for kv_ in ("k", "v"):
                m[p + "cmp_pe_" + kv_] = f32c(np.asarray(inp[p + "cmp_pe_" + kv_], np.float32).T)
                m[p + "cmp_w1_" + kv_] = f32c(inp[p + "cmp_w1_" + kv_])
                m[p + "cmp_w2_" + kv_] = f32c(inp[p + "cmp_w2_" + kv_])
        elif l == 1:
            cols = np.concatenate([mixf, 1536 + mixf, 3072 + mixf, 4608 + memq])
        elif l == 2:
            hsel = np.arange(r * HM, (r + 1) * HM)
            cols = np.concatenate([mixf, 1536 + mixf, 3072 + mixf, 4608 + mixf, 6144 + hsel, 6148 + hsel, 6152 + memq])
            cwf = np.asarray(inp[p + "conv_w"], np.float32)[:, np.concatenate([mixf, 1536 + mixf])]
            m[p + "conv_w"] = f32c(cwf.T.reshape(2 * MW // 128, 128, 4).transpose(1, 0, 2))
            cbf = np.asarray(inp[p + "conv_b"], np.float32)[np.concatenate([mixf, 1536 + mixf])]
            m[p + "conv_b"] = f32c(cbf.reshape(2 * MW // 128, 128).T)
            m[p + "gate_b"] = f32c(np.asarray(inp[p + "gate_b"], np.float32)[np.concatenate([hsel, 4 + hsel])])
            m[p + "head_norm"] = f32c(np.asarray(inp[p + "head_norm"], np.float32)[mixf])
        else:
            if NR == 1:
                heads = np.arange(24)
                grp = [0, 1, 2]
            elif r == 0:
                heads = np.arange(12)
                grp = [0, 1]
            else:
                heads = np.concatenate([np.arange(16, 24), np.arange(12, 16)])
                grp = [2, 1]
            feats = np.concatenate([h * 64 + np.arange(64) for h in heads])
            kc = np.concatenate([g * 64 + np.arange(64) for g in grp])
            cols = np.concatenate([feats, 1536 + kc, 1728 + kc, 1920 + memq])
            m[p + "sinks"] = f32c(np.asarray(inp[p + "sinks"], np.float32)[heads])
        m[p + "w_in"] = f32c(w_in[:, cols])
        m[p + "w_out"] = f32c(w_out[np.concatenate([feats, 1536 + memq]), :])
    return m


def run(inp, S, layers, n_cores=8):
    nc = build(S, layers)
    B = inp["x"].shape[0]
    SH = S // NR
    in_maps = []
    for i in range(n_cores):
        b, r = (i // NR) % B, i % NR
        in_maps.append(host_inputs(inp, b, r, layers, S))
    res = run_bass_kernel_spmd(nc, in_maps, core_ids=list(range(n_cores)))
    LAST["res"] = res.results
    outs = []
    for b in range(B):
        outs.append(np.concatenate([np.asarray(res.results[b * NR + r]["outT"]).T for r in range(NR)], axis=0))
    return np.stack(outs, axis=0).astype(np.float32)


def kernel(x, mem, mem_norm, mem_w_kv,
           l0_norm_mix, l0_w_in, l0_cmp_pe_k, l0_cmp_w1_k, l0_cmp_w2_k, l0_cmp_pe_v, l0_cmp_w1_v,
           l0_cmp_w2_v, l0_w_out, l0_norm_ffn, l0_w_gate, l0_w_up, l0_w_down,
           l1_norm_mix, l1_w_in, l1_w_out, l1_norm_ffn, l1_w_gate, l1_w_up, l1_w_down,
           l2_norm_mix, l2_w_in, l2_conv_w, l2_conv_b, l2_gate_b, l2_head_norm, l2_w_out,
           l2_norm_ffn, l2_w_gate, l2_w_up, l2_w_down,
           l3_norm_mix, l3_w_in, l3_sinks, l3_w_out, l3_norm_ffn, l3_w_gate, l3_w_up, l3_w_down,
           final_norm):
    inputs = dict(
        x=x, mem=mem, mem_norm=mem_norm, mem_w_kv=mem_w_kv,
        l0_norm_mix=l0_norm_mix, l0_w_in=l0_w_in, l0_cmp_pe_k=l0_cmp_pe_k, l0_cmp_w1_k=l0_cmp_w1_k, l0_cmp_w2_k=l0_cmp_w2_k,
        l0_cmp_pe_v=l0_cmp_pe_v, l0_cmp_w1_v=l0_cmp_w1_v, l0_cmp_w2_v=l0_cmp_w2_v, l0_w_out=l0_w_out, l0_norm_ffn=l0_norm_ffn,
        l0_w_gate=l0_w_gate, l0_w_up=l0_w_up, l0_w_down=l0_w_down,
        l1_norm_mix=l1_norm_mix, l1_w_in=l1_w_in, l1_w_out=l1_w_out, l1_norm_ffn=l1_norm_ffn, l1_w_gate=l1_w_gate, l1_w_up=l1_w_up,
        l1_w_down=l1_w_down,
        l2_norm_mix=l2_norm_mix, l2_w_in=l2_w_in, l2_conv_w=l2_conv_w, l2_conv_b=l2_conv_b, l2_gate_b=l2_gate_b,
        l2_head_norm=l2_head_norm, l2_w_out=l2_w_out, l2_norm_ffn=l2_norm_ffn, l2_w_gate=l2_w_gate, l2_w_up=l2_w_up, l2_w_down=l2_w_down,
        l3_norm_mix=l3_norm_mix, l3_w_in=l3_w_in, l3_sinks=l3_sinks, l3_w_out=l3_w_out, l3_norm_ffn=l3_norm_ffn, l3_w_gate=l3_w_gate,
        l3_w_up=l3_w_up, l3_w_down=l3_w_down, final_norm=final_norm)
    return run(inputs, 4096, [0, 1, 2, 3])
```

```python
from contextlib import ExitStack
import numpy as np
import concourse.bass as bass
import concourse.mybir as mybir
from concourse.bass_utils import run_bass_kernel_spmd

F32 = mybir.dt.float32
BF16 = mybir.dt.bfloat16
AF = mybir.ActivationFunctionType
ALU = mybir.AluOpType
AX = mybir.AxisListType

D = 2048
DFF = 5632
MEM = 256
NCH = D // 128
EPS = 1e-6
SELF_SYNC = True
NR = 2
HQ = 12 // NR
HM = 4 // NR
HS = 24 // NR
NGS = 3 if NR == 1 else 2
NGN = 2 // NR
MW = 1536 // NR
MEMW = 512 // NR
FWD = MW + MEMW
NFC = FWD // 128
WARM_A = 2
WARM_B = 1


MARKS = []


class Dep:
    __slots__ = ("sem", "val", "key")

    def __init__(self, sem, val, key):
        self.sem, self.val, self.key = sem, val, key


class Buf:
    def __init__(self, t, name):
        self.t = t
        self.name = name
        self.w = None
        self.r = {}
        self.ds = None

    def __getitem__(self, idx):
        return self.t[idx]


class FW:
    def __init__(self, nc, stack, n_dsems=44):
        self.nc = nc
        self.stack = stack
        self.engs = {"pe": nc.tensor, "dve": nc.vector, "act": nc.scalar, "pool": nc.gpsimd, "sp": nc.sync}
        self.sem = {}
        self.cnt = {}
        self.waited = {}
        for e in self.engs:
            self.sem[e] = stack.enter_context(nc.semaphore("s_" + e))
            self.cnt[e] = 0
            self.waited[e] = {}
        self.dsems = [stack.enter_context(nc.semaphore("d%d" % i)) for i in range(n_dsems)]
        self.dcnt = [0] * n_dsems
        self.dnext = 0
        self.dfree = 0
        self.pstack = None
        self.uid = 0
        self.ccsem = None
        self.cccnt = 0
        self.ccbuf = None
        self.cq = []
        self.ci = 0

    def phase(self):
        self.pstack = ExitStack()
        return self.pstack

    def sb(self, shape, dtype, name=None):
        self.uid += 1
        name = "%s_%d" % (name or "sb", self.uid)
        t = self.pstack.enter_context(self.nc.sbuf_tensor(name, list(shape), dtype))
        return Buf(t, name)

    def sbg(self, shape, dtype, name=None):
        self.uid += 1
        name = "%s_%d" % (name or "sbg", self.uid)
        t = self.stack.enter_context(self.nc.sbuf_tensor(name, list(shape), dtype))
        return Buf(t, name)

    def ps(self, shape, dtype=F32, name=None):
        self.uid += 1
        name = "%s_%d" % (name or "ps", self.uid)
        t = self.pstack.enter_context(self.nc.psum_tensor(name, list(shape), dtype))
        return Buf(t, name)

    def _wait(self, e, dep, force_self=False, raw=False):
        if dep is None:
            return
        if dep.key == e and not (force_self or (raw and SELF_SYNC and e != "pe")):
            return
        if self.waited[e].get(dep.key, 0) >= dep.val:
            return
        self.engs[e].wait_ge(dep.sem, dep.val)
        self.waited[e][dep.key] = dep.val

    def _pre(self, e, reads, writes, force_self=False):
        for b in reads:
            self._wait(e, b.w, force_self, raw=True)
        for b in writes:
            self._wait(e, b.w, force_self)
            for d in b.r.values():
                self._wait(e, d, force_self)

    def _post(self, dep, reads, writes):
        for b in writes:
            b.w = dep
            b.r = {}
        for b in reads:
            if b not in writes:
                b.r[dep.key] = dep

    def op(self, e, fn, reads=(), writes=()):
        self._pre(e, reads, writes)
        ins = fn(self.engs[e])
        self.cnt[e] += 1
        ins.then_inc(self.sem[e], 1)
        self._post(Dep(self.sem[e], self.cnt[e], e), reads, writes)

    def dma(self, q, out, in_, reads=(), writes=(), cont=False, **kw):
        if not cont:
            self._pre(q, reads, writes, force_self=True)
        owner = None
        for b in list(writes) + list(reads):
            owner = b
            break
        nown = len(self.dsems) - 4
        if owner is None:
            di = nown + self.dfree % 4
            self.dfree += 1
        else:
            if owner.ds is None:
                owner.ds = self.dnext % nown
                self.dnext += 1
            di = owner.ds
        self.dcnt[di] += 16
        self.engs[q].dma_start(out=out, in_=in_, **kw).then_inc(self.dsems[di], 16)
        self._post(Dep(self.dsems[di], self.dcnt[di], "d%d" % di), reads, writes)

    def collective(self, kind, src_ap, dst_ap, groups):
        self.barrier()
        if NR == 1:
            self.dma("sp", dst_ap, src_ap)
            self.barrier()
            return
        if self.ccsem is None:
            self.ccsem = self.stack.enter_context(self.nc.semaphore("ccsem"))
            self.cccnt = 0
        op = ALU.add if kind == "ReduceScatter" else ALU.bypass
        self.nc.gpsimd.collective_compute(kind, op, replica_groups=groups, ins=[src_ap.opt()], outs=[dst_ap.opt()]).then_inc(self.ccsem)
        self.cccnt += 1
        self.nc.gpsimd.wait_ge(self.ccsem, self.cccnt)
        if self.ccbuf is None:
            self.ccbuf = self.sbg([128, 8], F32, "ccbuf")
        self.op("pool", lambda e: e.memset(self.ccbuf[:], 0.0), writes=[self.ccbuf])
        self.barrier()

    def set_casts(self, pairs):
        self.cq = []
        for src_, dst_ in pairs:
            K_, N_ = src_.shape
            for r0 in range(0, K_, 128):
                self.cq.append((dst_[r0:min(K_, r0 + 128), :], src_[r0:min(K_, r0 + 128), :]))
        self.ci = 0

    def cast_tick(self, frac):
        want = min(len(self.cq), int(len(self.cq) * frac + 0.999))
        while self.ci < want:
            o_, i_ = self.cq[self.ci]
            self.dma("pool", o_, i_)
            self.ci += 1

    def barrier(self, wait_casts=False):
        MARKS.append(dict(self.cnt))
        deps = [Dep(self.sem[e], self.cnt[e], e) for e in self.engs if self.cnt[e] > 0]
        nd = len(self.dsems) if wait_casts else len(self.dsems) - 4
        deps += [Dep(self.dsems[i], self.dcnt[i], "d%d" % i) for i in range(nd) if self.dcnt[i] > 0]
        for e in self.engs:
            for d in deps:
                self._wait(e, d, force_self=False)


def mm(fw, out_b, out_ap, lhsT_b, lhsT_ap, rhs_b, rhs_ap, start, stop, skip=False):
    rd = [lhsT_b] if lhsT_b is rhs_b else [lhsT_b, rhs_b]
    fw.op("pe", lambda e: e.matmul(out_ap, lhsT_ap, rhs_ap, start=start, stop=stop, skip_group_check=skip),
          reads=rd, writes=[out_b])


class Consts:
    pass


def make_consts(fw):
    c = Consts()
    c.ones_f = fw.sbg([128, 128], F32, "ones_f")
    fw.op("dve", lambda e: e.memset(c.ones_f[:], 1.0), writes=[c.ones_f])
    c.ones_b = fw.sbg([128, 128], BF16, "ones_b")
    fw.op("dve", lambda e: e.memset(c.ones_b[:], 1.0), writes=[c.ones_b])
    c.eps = fw.sbg([128, 1], F32, "eps")
    fw.op("dve", lambda e: e.memset(c.eps[:], EPS), writes=[c.eps])
    c.ident = fw.sbg([128, 128], BF16, "ident")
    fw.op("pool", lambda e: e.affine_select(out=c.ident[:], in_=c.ones_b[:], pattern=[[-1, 128]],
                                            compare_op=ALU.is_equal, fill=0.0, base=0, channel_multiplier=1),
          reads=[c.ones_b], writes=[c.ident])
    c.ident_f = fw.sbg([128, 128], F32, "ident_f")
    fw.op("pool", lambda e: e.affine_select(out=c.ident_f[:], in_=c.ones_f[:], pattern=[[-1, 128]],
                                            compare_op=ALU.is_equal, fill=0.0, base=0, channel_multiplier=1),
          reads=[c.ones_f], writes=[c.ident_f])
    return c


def tri_mask(fw, c, name, dtype, shape_cols, base, col_mult, chan_mult, op=ALU.is_ge):
    m = fw.sb([128, shape_cols], dtype, name)
    src = fw.sb([128, shape_cols], dtype, name + "_o")
    fw.op("dve", lambda e: e.memset(src[:], 1.0), writes=[src])
    fw.op("pool", lambda e: e.affine_select(out=m[:], in_=src[:], pattern=[[col_mult, shape_cols]],
                                            compare_op=op, fill=0.0, base=base, channel_multiplier=chan_mult),
          reads=[src], writes=[m])
    return m


def rms_norm_tile(fw, c, xs, g, h, psb, T, sqb, rstd):
    for k in range(NCH):
        sq = sqb[k % len(sqb)]
        fw.op("act", lambda e, k=k, sq=sq: e.activation(out=sq[:, :T], in_=xs[:, k, :T], func=AF.Square),
              reads=[xs], writes=[sq])
        mm(fw, psb, psb[:, :T], c.ones_f, c.ones_f[:], sq, sq[:, :T], start=(k == 0), stop=(k == NCH - 1))
    fw.op("act", lambda e: e.activation(out=rstd[:, :T], in_=psb[:, :T], func=AF.Sqrt, bias=c.eps[:, 0:1], scale=1.0 / D),
          reads=[psb, c.eps], writes=[rstd])
    fw.op("dve", lambda e: e.reciprocal(out=rstd[:, :T], in_=rstd[:, :T]), reads=[rstd], writes=[rstd])
    for k in range(NCH):
        fw.op("dve", lambda e, k=k: e.scalar_tensor_tensor(out=h[:, k, :T], in0=xs[:, k, :T], scalar=g[:, k:k + 1],
                                                          in1=rstd[:, :T], op0=ALU.mult, op1=ALU.mult),
              reads=[xs, g, rstd], writes=[h])


def load_wblock(fw, wb, w_dram, kchunks, c0, nb, q="sp"):
    v = w_dram.rearrange("(kc p) n -> p kc n", p=128)
    half = (kchunks + 1) // 2
    fw.dma(q, wb[:, 0:half, 0:nb], v[:, 0:half, c0:c0 + nb], writes=[wb])
    fw.dma(q, wb[:, half:kchunks, 0:nb], v[:, half:kchunks, c0:c0 + nb], writes=[wb], cont=True)


def issue_casts(fw, pairs):
    for src, dst in pairs:
        K, N = src.shape
        step = 256
        for r0 in range(0, K, step):
            r1 = min(K, r0 + step)
            fw.dma("pool", dst[r0:r1, :], src[r0:r1, :])


def phase_cast_weights(fw, pairs):
    issue_casts(fw, pairs)
    fw.barrier(wait_casts=True)


def phase_n1(fw, c, SH, x_src, g_dram, hT_loc, T=512):
    with fw.phase():
        xss = [fw.sb([128, NCH, T], F32, "xs") for _ in range(2)]
        hs = [fw.sb([128, NCH, T], BF16, "h") for _ in range(2)]
        g = fw.sb([128, NCH], F32, "g")
        sqb = [fw.sb([128, T], F32, "sq") for _ in range(3)]
        rstd = fw.sb([128, T], F32, "rstd")
        ps_n = fw.ps([128, 512], F32, "ps_n")
        fw.dma("sp", g[:], g_dram, writes=[g])
        xv = x_src.rearrange("(kc p) s -> p kc s", p=128)
        hv = hT_loc.rearrange("(kc p) s -> p kc s", p=128)
        for tt in range(SH // T):
            t0 = tt * T
            xs, h = xss[tt % 2], hs[tt % 2]
            fw.dma("sp", xs[:, 0:8, :], xv[:, 0:8, t0:t0 + T], writes=[xs])
            fw.dma("sp", xs[:, 8:16, :], xv[:, 8:16, t0:t0 + T], writes=[xs], cont=True)
            rms_norm_tile(fw, c, xs, g, h, ps_n, T, sqb, rstd)
            fw.dma("pool", hv[:, 0:8, t0:t0 + T], h[:, 0:8, :], reads=[h])
            fw.dma("pool", hv[:, 8:16, t0:t0 + T], h[:, 8:16, :], reads=[h], cont=True)
        fw.barrier()


def phase_in(fw, c, S, SH, Gh, w_bf, segs, T=512):
    with fw.phase():
        hs = [fw.sb([128, NCH, T], BF16, "h") for _ in range(2)]
        wbs = [fw.sb([128, NCH, 512], BF16, "wb") for _ in range(3)]
        stg = [fw.sb([128, 512], BF16, "stg") for _ in range(4)]
        stgf = [fw.sb([128, 512], F32, "stgf") for _ in range(2)]
        pss = [fw.ps([128, 512], F32, "ps_o") for _ in range(6)]
        RC = cc_rows(SH)
        KC = RC // 128
        gv = Gh.rearrange("(ci b kc p) s -> ci b p kc s", p=128, kc=KC, b=NR)
        wi = 0
        pi = 0
        si = 0
        for tt in range(S // T):
            t0 = tt * T
            blk = t0 // SH
            l0 = t0 % SH
            h = hs[tt % 2]
            for ci in range(D // RC):
                fw.dma("sp", h[:, ci * KC:(ci + 1) * KC, :], gv[ci, blk, :, :, l0:l0 + T], writes=[h], cont=(ci > 0))
            for (c0, ncols, form, dst, ddt) in segs:
                for b0 in range(0, ncols, 512):
                    nb = min(512, ncols - b0)
                    wb = wbs[wi % 3]
                    wi += 1
                    load_wblock(fw, wb, w_bf, NCH, c0 + b0, nb)
                    if form == "FM":
                        for j0 in range(0, nb, 128):
                            mj = min(128, nb - j0)
                            ps = pss[pi % 6]
                            pi += 1
                            for k in range(NCH):
                                mm(fw, ps, ps[0:mj, :T], wb, wb[:, k, j0:j0 + mj], h, h[:, k, :T],
                                   start=(k == 0), stop=(k == NCH - 1))
                            st = stg[si % 4]
                            if si % 2 == 0:
                                fw.op("act", lambda e, st=st, ps=ps, mj=mj: e.copy(out=st[0:mj, :T], in_=ps[0:mj, :T]),
                                      reads=[ps], writes=[st])
                            else:
                                fw.op("dve", lambda e, st=st, ps=ps, mj=mj: e.tensor_copy(out=st[0:mj, :T], in_=ps[0:mj, :T]),
                                      reads=[ps], writes=[st])
                            si += 1
                            fw.dma("pool", dst[b0 + j0:b0 + j0 + mj, t0:t0 + T], st[0:mj, :T], reads=[st])
                    else:
                        for ti in range(T // 128):
                            ps = pss[pi % 6]
                            pi += 1
                            for k in range(NCH):
                                mm(fw, ps, ps[:, 0:nb], h, h[:, k, ti * 128:(ti + 1) * 128], wb, wb[:, k, 0:nb],
                                   start=(k == 0), stop=(k == NCH - 1))
                            if ddt == F32:
                                st = stgf[si % 2]
                            else:
                                st = stg[si % 4]
                            if si % 2 == 0:
                                fw.op("act", lambda e, st=st, ps=ps, nb=nb: e.copy(out=st[:, 0:nb], in_=ps[:, 0:nb]),
                                      reads=[ps], writes=[st])
                            else:
                                fw.op("dve", lambda e, st=st, ps=ps, nb=nb: e.tensor_copy(out=st[:, 0:nb], in_=ps[:, 0:nb]),
                                      reads=[ps], writes=[st])
                            si += 1
                            fw.dma("pool", dst[t0 + ti * 128:t0 + (ti + 1) * 128, b0:b0 + nb], st[:, 0:nb], reads=[st])
        fw.barrier()


def phase_op(fw, c, S, SH, mixT, wout_bf, ypart, T=512):
    with fw.phase():
        mxs = [fw.sb([128, NFC, T], BF16, "mx") for _ in range(2)]
        wo = fw.sb([128, NFC, D], BF16, "wo")
        stg = [fw.sb([128, 512], BF16, "stg") for _ in range(4)]
        pss = [fw.ps([128, 512], F32, "ps_o") for _ in range(6)]
        wv = wout_bf.rearrange("(kc p) n -> p kc n", p=128)
        for k0 in range(0, NFC, 2):
            fw.dma("sp", wo[:, k0:k0 + 2, :], wv[:, k0:k0 + 2, :], writes=[wo], cont=(k0 > 0))
        mv = mixT.rearrange("(kc p) s -> p kc s", p=128)
        RC = cc_rows(SH)
        KC = RC // 128
        yv = ypart.rearrange("(ci b kc p) s -> ci b p kc s", p=128, kc=KC, b=NR)
        pi = 0
        for tt in range(S // T):
            t0 = tt * T
            blk = t0 // SH
            l0 = t0 % SH
            mx = mxs[tt % 2]
            fw.dma("sp", mx[:, :, :], mv[:, :, t0:t0 + T], writes=[mx])
            for dc in range(NCH):
                ps = pss[pi % 6]
                st = stg[pi % 4]
                for k in range(NFC):
                    mm(fw, ps, ps[:, :T], wo, wo[:, k, dc * 128:(dc + 1) * 128], mx, mx[:, k, :T], start=(k == 0), stop=(k == NFC - 1))
                if pi % 2 == 0:
                    fw.op("act", lambda e, st=st, ps=ps: e.copy(out=st[:, :T], in_=ps[:, :T]), reads=[ps], writes=[st])
                else:
                    fw.op("dve", lambda e, st=st, ps=ps: e.tensor_copy(out=st[:, :T], in_=ps[:, :T]), reads=[ps], writes=[st])
                pi += 1
                fw.dma("pool", yv[dc // KC, blk, :, dc % KC, l0:l0 + T], st[:, :T], reads=[st])
        fw.barrier()


def phase_ffn(fw, c, SH, x_in, x_out, y_my, gffn_dram, wg_bf, wu_bf, wd_bf, final_g=None, out_dram=None, T=512, pre_cast=()):
    NF = DFF // 128
    with fw.phase():
        xs = fw.sb([128, NCH, T], F32, "xs")
        h = fw.sb([128, NCH, T], BF16, "h")
        ys = h
        act = fw.sb([128, NF, T], BF16, "act")
        g = fw.sb([128, NCH], F32, "g")
        gf = fw.sb([128, NCH], F32, "gf")
        sqb = [fw.sb([128, T], F32, "sq") for _ in range(2)]
        rstd = fw.sb([128, T], F32, "rstd")
        wbs = [fw.sb([128, NCH, 512], BF16, "wb") for _ in range(3)]
        wds = [fw.sb([128, NF, 256], BF16, "wd") for _ in range(2)]
        sgs = [fw.sb([128, T], F32, "sg") for _ in range(2)]
        ps_n = fw.ps([128, 512], F32, "ps_n")
        pss = [fw.ps([128, 512], F32, "ps_o") for _ in range(6)]
        fw.dma("sp", g[:], gffn_dram, writes=[g])
        if final_g is not None:
            fw.dma("sp", gf[:], final_g, writes=[gf])
        xv = x_in.rearrange("(kc p) s -> p kc s", p=128)
        xo = x_out.rearrange("(kc p) s -> p kc s", p=128)
        yv = y_my.rearrange("(kc p) s -> p kc s", p=128)
        wi = 0
        pi = 0
        di = 0
        fw.cast_tick(1.0)
        for tt in range(SH // T):
            t0 = tt * T
            fw.dma("sp", ys[:, :, :], yv[:, :, t0:t0 + T], writes=[ys])
            fw.dma("sp", xs[:, 0:8, :], xv[:, 0:8, t0:t0 + T], writes=[xs])
            fw.dma("sp", xs[:, 8:16, :], xv[:, 8:16, t0:t0 + T], writes=[xs], cont=True)
            for k0 in range(0, NCH, 4):
                fw.op("pool", lambda e, k0=k0: e.tensor_tensor(out=xs[:, k0:k0 + 4, :], in0=xs[:, k0:k0 + 4, :], in1=ys[:, k0:k0 + 4, :], op=ALU.add),
                      reads=[xs, ys], writes=[xs])

            rms_norm_tile(fw, c, xs, g, h, ps_n, T, sqb, rstd)
            for b0 in range(0, DFF, 512):
                wg = wbs[wi % 3]
                wi += 1
                load_wblock(fw, wg, wg_bf, NCH, b0, 512)
                wu = wbs[wi % 3]
                wi += 1
                load_wblock(fw, wu, wu_bf, NCH, b0, 512)
                for j in range(4):
                    fc = b0 // 128 + j
                    pg = pss[pi % 6]
                    pi += 1
                    pu = pss[pi % 6]
                    pi += 1
                    for k in range(NCH):
                        mm(fw, pg, pg[:, :T], wg, wg[:, k, j * 128:(j + 1) * 128], h, h[:, k, :T],
                           start=(k == 0), stop=(k == NCH - 1))
                    for k in range(NCH):
                        mm(fw, pu, pu[:, :T], wu, wu[:, k, j * 128:(j + 1) * 128], h, h[:, k, :T],
                           start=(k == 0), stop=(k == NCH - 1))
                    sg = sgs[fc % 2]
                    fw.op("act", lambda e, sg=sg, pg=pg: e.activation(out=sg[:, :T], in_=pg[:, :T], func=AF.Silu),
                          reads=[pg], writes=[sg])
                    fw.op("dve", lambda e, sg=sg, pu=pu, fc=fc: e.tensor_tensor(out=act[:, fc, :T], in0=sg[:, :T], in1=pu[:, :T], op=ALU.mult),
                          reads=[sg, pu], writes=[act])
            for b0 in range(0, D, 256):
                wd = wds[di % 2]
                di += 1
                v = wd_bf.rearrange("(kc p) n -> p kc n", p=128)
                for q0 in range(0, NF, 11):
                    fw.dma("sp", wd[:, q0:q0 + 11, :], v[:, q0:q0 + 11, b0:b0 + 256], writes=[wd], cont=(q0 > 0))
                for j in range(2):
                    dc = b0 // 128 + j
                    ps = pss[pi % 6]
                    pi += 1
                    for k in range(NF):
                        mm(fw, ps, ps[:, :T], wd, wd[:, k, j * 128:(j + 1) * 128], act, act[:, k, :T],
                           start=(k == 0), stop=(k == NF - 1))
                    fw.op("dve", lambda e, dc=dc, ps=ps: e.tensor_tensor(out=xs[:, dc, :T], in0=xs[:, dc, :T], in1=ps[:, :T], op=ALU.add),
                          reads=[xs, ps], writes=[xs])
            if final_g is None:
                fw.dma("pool", xo[:, 0:8, t0:t0 + T], xs[:, 0:8, :], reads=[xs])
                fw.dma("pool", xo[:, 8:16, t0:t0 + T], xs[:, 8:16, :], reads=[xs], cont=True)
            else:
                for k in range(NCH):
                    sq = sqb[k % 2]
                    fw.op("act", lambda e, k=k, sq=sq: e.activation(out=sq[:, :T], in_=xs[:, k, :T], func=AF.Square),
                          reads=[xs], writes=[sq])
                    mm(fw, ps_n, ps_n[:, :T], c.ones_f, c.ones_f[:], sq, sq[:, :T], start=(k == 0), stop=(k == NCH - 1))
                fw.op("act", lambda e: e.activation(out=rstd[:, :T], in_=ps_n[:, :T], func=AF.Sqrt, bias=c.eps[:, 0:1], scale=1.0 / D),
                      reads=[ps_n, c.eps], writes=[rstd])
                fw.op("dve", lambda e: e.reciprocal(out=rstd[:, :T], in_=rstd[:, :T]), reads=[rstd], writes=[rstd])
                oo = out_dram.rearrange("(kc p) s -> p kc s", p=128)
                for k in range(NCH):
                    fw.op("dve", lambda e, k=k: e.scalar_tensor_tensor(out=xs[:, k, :T], in0=xs[:, k, :T], scalar=gf[:, k:k + 1],
                                                                      in1=rstd[:, :T], op0=ALU.mult, op1=ALU.mult),
                          reads=[xs, gf, rstd], writes=[xs])
                fw.dma("pool", oo[:, 0:8, t0:t0 + T], xs[:, 0:8, :], reads=[xs])
                fw.dma("pool", oo[:, 8:16, t0:t0 + T], xs[:, 8:16, :], reads=[xs], cont=True)
        fw.barrier()


class Attn:
    def __init__(self, fw, npt=3):
        self.st = [fw.ps([128, 512], F32, "st") for _ in range(2)]
        self.ob = [[fw.ps([128, 512], F32, "ob") for _ in range(2)] for _ in range(2)]
        self.pt = [fw.sb([128, 512], BF16, "pt") for _ in range(npt + 1)]
        self.rc = [fw.sb([128, 4], F32, "rc") for _ in range(4)]
        self.si = 0
        self.pi = 0
        self.oi = 0
        self.ri = 0
        self.q = []

    def flush(self, keep=0):
        n = len(self.q) - keep
        if n <= 0:
            return
        pending, self.q = self.q[:n], self.q[n:]
        for f in pending:
            f()


def attn_head(fw, A, q_b, q_ap, ktiles, nv, scale, mask_eng="dve"):
    banks = A.ob[A.oi % 2]
    A.oi += 1
    started = [False, False]
    for ti, ktl in enumerate(ktiles):
        (k_b, k_ap, v_b, v_ap, m_b, m_ap, lo, hi) = ktl[:8]
        sz = ktl[8] if len(ktl) > 8 else 128
        st = A.st[A.si % 2]
        A.si += 1
        cs = slice(lo * 128, hi * 128)
        mm(fw, st, st[0:sz, cs], k_b, k_ap, q_b, q_ap(lo, hi), True, True)
        pt = A.pt[A.pi % len(A.pt)]
        A.pi += 1
        fw.op("act", lambda e: e.activation(out=pt[0:sz, cs], in_=st[0:sz, cs], func=AF.Exp, scale=scale), reads=[st], writes=[pt])
        if m_b is not None:
            eng = mask_eng if mask_eng != "alt" else ("dve" if A.pi % 2 == 0 else "pool")
            fw.op(eng, lambda e: e.tensor_tensor(out=pt[0:sz, cs], in0=pt[0:sz, cs], in1=m_ap, op=ALU.mult), reads=[pt, m_b], writes=[pt])
        A.flush(keep=1)

        def pv(pt=pt, sz=sz, lo=lo, hi=hi, v_b=v_b, v_ap=v_ap):
            for qb in range(lo, hi):
                bank = banks[qb // 2]
                col = (qb % 2) * nv
                mm(fw, bank, bank[:, col:col + nv], pt, pt[0:sz, qb * 128:(qb + 1) * 128], v_b, v_ap,
                   start=(not started[qb // 2]), stop=True, skip=True)
                started[qb // 2] = True
        A.q.append(pv)
    return [(banks[qb // 2], (qb % 2) * nv) for qb in range(4)]


def attn_finish_plain(fw, A, outs, dv, dst_b, dst_ap, extra_b=None, extra_ap=None):
    A.q.append(lambda: _attn_finish_plain(fw, A, outs, dv, dst_b, dst_ap, extra_b, extra_ap))


def _attn_finish_plain(fw, A, outs, dv, dst_b, dst_ap, extra_b=None, extra_ap=None):
    for qb, (bank, col) in enumerate(outs):
        rc = A.rc[A.ri % 4]
        A.ri += 1
        if extra_b is not None:
            fw.op("dve", lambda e, rc=rc, bank=bank, col=col: e.tensor_tensor(out=rc[:, 0:1], in0=bank[:, col + dv:col + dv + 1],
                                                                               in1=extra_ap, op=ALU.add),
                  reads=[bank, extra_b], writes=[rc])
            fw.op("dve", lambda e, rc=rc: e.reciprocal(out=rc[:, 0:1], in_=rc[:, 0:1]), reads=[rc], writes=[rc])
        else:
            fw.op("dve", lambda e, rc=rc, bank=bank, col=col: e.reciprocal(out=rc[:, 0:1], in_=bank[:, col + dv:col + dv + 1]),
                  reads=[bank], writes=[rc])
        fw.op("dve", lambda e, rc=rc, bank=bank, col=col, qb=qb: e.tensor_scalar(out=dst_ap(qb), in0=bank[:, col:col + dv],
                                                                                 scalar1=rc[:, 0:1], scalar2=None, op0=ALU.mult),
              reads=[bank, rc], writes=[dst_b])


def mixer_store(fw, c, mixtm, mixT_dram, t0, pst, stgs, cnt, fcs=None, row0=0):
    if getattr(fw, "dbg", None) and "mix" in fw.dbg:
        fw.dma("sp", fw.dbg.pop("mix"), mixtm[:, :, :], reads=[mixtm])
    for fc in (range(NCH) if fcs is None else fcs):
        ps = pst[cnt[0] % len(pst)]
        for qb in range(4):
            fw.op("pe", lambda e, ps=ps, qb=qb, fc=fc: e.transpose(out=ps[:, qb * 128:(qb + 1) * 128],
                                                                   in_=mixtm[:, qb, fc * 128:(fc + 1) * 128], identity=c.ident[:]),
                  reads=[mixtm, c.ident], writes=[ps])
        st = stgs[cnt[0] % len(stgs)]
        eng = "act" if cnt[0] % 2 == 0 else "dve"
        if eng == "act":
            fw.op("act", lambda e, st=st, ps=ps: e.copy(out=st[:, :], in_=ps[:, :]), reads=[ps], writes=[st])
        else:
            fw.op("dve", lambda e, st=st, ps=ps: e.tensor_copy(out=st[:, :], in_=ps[:, :]), reads=[ps], writes=[st])
        cnt[0] += 1
        fw.dma("sp", mixT_dram[row0 + fc * 128:row0 + (fc + 1) * 128, t0:t0 + 512], st[:, :], reads=[st])


def mem_attention_qtile(fw, A, MK, qm, mixtm, col0=MW):
    for hh in range(HM):
        kts = []
        for mt in range(2):
            kts.append((MK.k, MK.k[:, hh, mt * 128:(mt + 1) * 128], MK.v, MK.v[:, mt, hh, 0:129], None, None, 0, 4))
        outs = attn_head(fw, A, qm, lambda lo, hi, hh=hh: qm[:, hh, lo * 128:hi * 128], kts, 129, 128 ** -0.5)
        attn_finish_plain(fw, A, outs, 128, mixtm, lambda qb, hh=hh: mixtm[:, qb, col0 + hh * 128:col0 + (hh + 1) * 128])


class MemKV:
    pass


def phase_mem(fw, c, memT_dram, gmem_dram, wkv_bf):
    MK = MemKV()
    MK.k = fw.sbg([128, HM, MEM], BF16, "memk")
    MK.v = fw.sbg([128, 2, HM, 129], BF16, "memv")
    T = MEM
    with fw.phase():
        xs = fw.sb([128, NCH, T], F32, "xs")
        h = fw.sb([128, NCH, T], BF16, "h")
        g = fw.sb([128, NCH], F32, "g")
        sqb = [fw.sb([128, T], F32, "sq") for _ in range(2)]
        rstd = fw.sb([128, T], F32, "rstd")
        wbs = [fw.sb([128, NCH, 512], BF16, "wb") for _ in range(2)]
        ps_n = fw.ps([128, 512], F32, "ps_n")
        pss = [fw.ps([128, 512], F32, "ps_o") for _ in range(2)]
        fw.dma("sp", g[:], gmem_dram, writes=[g])
        xv = memT_dram.rearrange("(kc p) s -> p kc s", p=128)
        fw.dma("sp", xs[:, :, :], xv[:, :, :], writes=[xs])
        rms_norm_tile(fw, c, xs, g, h, ps_n, T, sqb, rstd)
        fw.op("dve", lambda e: e.memset(MK.v[:, :, :, 128:129], 1.0), writes=[MK.v])
        load_wblock(fw, wbs[0], wkv_bf, NCH, 0, HM * 128)
        load_wblock(fw, wbs[1], wkv_bf, NCH, HM * 128, HM * 128)
        for hh in range(HM):
            ps = pss[hh % 2]
            for k in range(NCH):
                mm(fw, ps, ps[:, :T], wbs[0], wbs[0][:, k, hh * 128:(hh + 1) * 128], h, h[:, k, :T], start=(k == 0), stop=(k == NCH - 1))
            fw.op("dve", lambda e, ps=ps, hh=hh: e.tensor_copy(out=MK.k[:, hh, :], in_=ps[:, :T]), reads=[ps], writes=[MK.k])
        for mt in range(2):
            ps = pss[mt % 2]
            for k in range(NCH):
                mm(fw, ps, ps[:, :HM * 128], h, h[:, k, mt * 128:(mt + 1) * 128], wbs[1], wbs[1][:, k, 0:HM * 128], start=(k == 0), stop=(k == NCH - 1))
            fw.op("dve", lambda e, ps=ps, mt=mt: e.tensor_copy(out=MK.v[:, mt, :, 0:128],
                                                              in_=ps[:, :HM * 128].rearrange("p (h d) -> p h d", h=HM)),
                  reads=[ps], writes=[MK.v])
        fw.barrier()
    return MK


def phase_mix_swa(fw, c, S, MK, qT_d, kT_d, v_d, qmT_d, sinks_d, mixT_d):
    NT = S // 128
    with fw.phase():
        A = Attn(fw)
        k2 = fw.sb([128, NGS, S], BF16, "k2")
        va = fw.sb([128, NT, NGS, 65], BF16, "va")
        es = fw.sb([128, HS], F32, "es")
        m2 = fw.sb([128, 256], BF16, "m2")
        ones2 = fw.sb([128, 256], BF16, "ones2")
        qts = [fw.sb([128, HS // 2, 512], BF16, "qt") for _ in range(2)]
        qms = [fw.sb([128, HM, 512], BF16, "qm") for _ in range(2)]
        mixtms = [fw.sb([128, 4, FWD], BF16, "mixtm") for _ in range(2)]
        pst = [fw.ps([128, 512], BF16, "pst") for _ in range(2)]
        stgs = [fw.sb([128, 512], BF16, "stg") for _ in range(3)]
        cnt = [0]
        fw.op("dve", lambda e: e.memset(ones2[:], 1.0), writes=[ones2])
        fw.op("pool", lambda e: e.affine_select(out=m2[:, 0:128], in_=ones2[:, 0:128], pattern=[[1, 128]], compare_op=ALU.is_ge,
                                                fill=0.0, base=0, channel_multiplier=-1), reads=[ones2], writes=[m2])
        fw.op("pool", lambda e: e.affine_select(out=m2[:, 128:256], in_=ones2[:, 128:256], pattern=[[-1, 128]], compare_op=ALU.is_ge,
                                                fill=0.0, base=-1, channel_multiplier=1), reads=[ones2], writes=[m2])
        for gi in range(NGS):
            fw.dma("sp", k2[0:64, gi, :], kT_d[gi * 64:(gi + 1) * 64, :], writes=[k2], cont=(gi > 0))
            fw.dma("sp", k2[64:128, gi, :], kT_d[gi * 64:(gi + 1) * 64, :], writes=[k2], cont=True)
        vv = v_d.rearrange("(kt p) (g d) -> p kt g d", p=128, g=NGS)
        for k0 in range(0, NT, 8):
            for gi in range(NGS):
                k1 = min(NT, k0 + 8)
                fw.dma("sp", va[:, k0:k1, gi, 0:64], vv[:, k0:k1, gi, :], writes=[va], cont=(k0 > 0 or gi > 0))
        fw.op("dve", lambda e: e.memset(va[:, :, :, 64:65], 1.0), reads=[], writes=[va])
        fw.dma("sp", es[:], sinks_d.partition_broadcast(128), writes=[es])
        fw.op("act", lambda e: e.activation(out=es[:], in_=es[:], func=AF.Exp), reads=[es], writes=[es])
        qv = qT_d.rearrange("(c p) s -> p c s", p=128)
        qmv = qmT_d.rearrange("(c p) s -> p c s", p=128)
        for qt in range(S // 512):
            t0 = qt * 512
            qtb = qts[qt % 2]
            qmb = qms[qt % 2]
            mixtm = mixtms[qt % 2]
            fw.dma("sp", qtb[:, :, :], qv[:, :, t0:t0 + 512], writes=[qtb])
            fw.dma("sp", qmb[:, :, :], qmv[:, :, t0:t0 + 512], writes=[qmb])
            for hh in range(HS):
                fw.cast_tick((qt * HS + hh + 1) / float((S // 512) * HS))
                gi = hh // 8
                pb = (hh % 2) * 64
                ch = hh // 2
                kts = []
                for i in range(-1, 4):
                    kt = qt * 4 + i
                    if kt < 0:
                        continue
                    lo, hi = max(i, 0), min(i + 2, 4)
                    mlo = (lo - i) * 128
                    mhi = (hi - i) * 128
                    kts.append((k2, k2[pb:pb + 64, gi, kt * 128:(kt + 1) * 128], va, va[:, kt, gi, :], m2, m2[:, mlo:mhi], lo, hi))
                outs = attn_head(fw, A, qtb, lambda lo, hi, pb=pb, ch=ch: qtb[pb:pb + 64, ch, lo * 128:hi * 128], kts, 65, 64 ** -0.5)
                attn_finish_plain(fw, A, outs, 64, mixtm, lambda qb, hh=hh: mixtm[:, qb, hh * 64:(hh + 1) * 64],
                                  extra_b=es, extra_ap=es[:, hh:hh + 1])
            mem_attention_qtile(fw, A, MK, qmb, mixtm)
            A.flush()
            mixer_store(fw, c, mixtm, mixT_d, t0, pst, stgs, cnt, fcs=range(NFC))
        fw.barrier()


def mem_part(fw, c, A, MK, S, qmT_d, mixT_d, pst, stgs, cnt):
    qms = [fw.sb([128, HM, 512], BF16, "qm") for _ in range(2)]
    mtm = [fw.sb([128, 4, HM * 128], BF16, "mtm") for _ in range(2)]
    qmv = qmT_d.rearrange("(c p) s -> p c s", p=128)
    for qt in range(S // 512):
        t0 = qt * 512
        qmb = qms[qt % 2]
        fw.dma("sp", qmb[:, :, :], qmv[:, :, t0:t0 + 512], writes=[qmb])
        mem_attention_qtile(fw, A, MK, qmb, mtm[qt % 2], col0=0)
        A.flush()
        mixer_store(fw, c, mtm[qt % 2], mixT_d, t0, pst, stgs, cnt, fcs=range(HM), row0=MW)


def phase_mix_sb(fw, c, S, MK, qT_d, kT_d, v_d, qmT_d, mixT_d):
    NT = S // 128
    scale = 128 ** -0.5
    with fw.phase():
        ones5 = fw.sb([128, 512], BF16, "ones5")
        fw.op("dve", lambda e: e.memset(ones5[:], 1.0), writes=[ones5])
        masks = []
        for i in range(4):
            m = fw.sb([128, 512], BF16, "mk")
            fw.op("pool", lambda e, m=m, i=i: e.affine_select(out=m[:], in_=ones5[:], pattern=[[1, 512]], compare_op=ALU.is_ge,
                                                              fill=0.0, base=-128 * i - 1, channel_multiplier=-1),
                  reads=[ones5], writes=[m])
            masks.append(m)
        tri = fw.sb([128, 128], BF16, "tri")
        fw.op("pool", lambda e: e.affine_select(out=tri[:], in_=ones5[:, 0:128], pattern=[[-1, 128]], compare_op=ALU.is_ge,
                                                fill=0.0, base=-1, channel_multiplier=1), reads=[ones5], writes=[tri])
        one1 = fw.sb([128, 1], F32, "one1")
        fw.op("dve", lambda e: e.memset(one1[:], 1.0), writes=[one1])
        khs = [fw.sb([128, S], BF16, "kh") for _ in range(2)]
        qhs = [fw.sb([128, S], BF16, "qh") for _ in range(2)]
        vhs = [fw.sb([128, NT, 128], BF16, "vh") for _ in range(2)]
        ps_st = [fw.ps([128, 512], F32, "st") for _ in range(2)]
        ps_tri = [fw.ps([128, 512], F32, "ptri") for _ in range(2)]
        ps_one = [fw.ps([128, 512], F32, "pone") for _ in range(1)] * 2
        ps_dum = fw.ps([128, 512], F32, "pdum")
        ps_o = [fw.ps([128, 512], F32, "po") for _ in range(2)]

        def warm(k):
            for _ in range(k):
                mm(fw, ps_dum, ps_dum[:, :], c.ones_b, c.ones_b[:, :], ones5, ones5[:, :], True, True)

        ebs = [fw.sb([128, 512], F32, "eb") for _ in range(2)]
        sps = [fw.sb([128, 512], BF16, "spb") for _ in range(3)]
        t1s = [fw.sb([128, 512], F32, "t1") for _ in range(2)]
        R = fw.sb([128, 512], F32, "R")
        nls = [fw.sb([128, 512], BF16, "nl") for _ in range(2)]
        pts = [fw.sb([128, 512], BF16, "pt") for _ in range(2)]
        stgs = [fw.sb([128, 512], BF16, "stg") for _ in range(3)]
        vv = v_d.rearrange("(kt p) f -> p kt f", p=128)
        blocks = []
        for hh in range(HQ):
            for qt in range(S // 512):
                kts = list(range(4 * qt + 3, -1, -1))
                for idx, kt in enumerate(kts):
                    blocks.append((hh, qt, kt, idx, len(kts)))
        hbuf = {}
        state = {"so": 0}

        def stage_a1(n):
            hh, qt, kt, idx, nk = blocks[n]
            if idx == 0 and qt == 0:
                kh, qh, vh = khs[hh % 2], qhs[hh % 2], vhs[hh % 2]
                fw.dma("sp", kh[:, :], kT_d[hh * 128:(hh + 1) * 128, :], writes=[kh])
                fw.dma("sp", qh[:, :], qT_d[hh * 128:(hh + 1) * 128, :], writes=[qh])
                for k0 in range(0, NT, 8):
                    k1 = min(NT, k0 + 8)
                    fw.dma("sp", vh[:, k0:k1, :], vv[:, k0:k1, hh * 128:(hh + 1) * 128], writes=[vh], cont=(k0 > 0))
            kh, qh = khs[hh % 2], qhs[hh % 2]
            t0 = qt * 512
            st = ps_st[n % 2]
            mm(fw, st, st[:, :], kh, kh[:, kt * 128:(kt + 1) * 128], qh, qh[:, t0:t0 + 512], True, True)

        def stage_a2(n):
            hh, qt, kt, idx, nk = blocks[n]
            i = kt - 4 * qt
            st, ptr, pon = ps_st[n % 2], ps_tri[n % 2], ps_one[n % 2]
            eb, sp, nl = ebs[n % 2], sps[n % 3], nls[n % 2]
            fw.op("act", lambda e: e.activation(out=eb[:, :], in_=st[:, :], func=AF.Exp, scale=-scale), reads=[st], writes=[eb])
            fw.op("act", lambda e: e.activation(out=sp[:, :], in_=eb[:, :], func=AF.Ln, bias=one1[:, 0:1], scale=1.0), reads=[eb, one1], writes=[sp])
            fw.op("dve", lambda e: e.scalar_tensor_tensor(out=nl[:, :], in0=st[:, :], scalar=scale, in1=sp[:, :], op0=ALU.mult, op1=ALU.add),
                  reads=[st, sp], writes=[nl])
            if i >= 0:
                fw.op("pool", lambda e: e.tensor_tensor(out=nl[:, :], in0=nl[:, :], in1=masks[i][:, :], op=ALU.mult), reads=[nl, masks[i]], writes=[nl])
            warm(WARM_A)
            mm(fw, ptr, ptr[:, :], tri, tri[:, :], nl, nl[:, :], True, False)
            mm(fw, ptr, ptr[:, :], c.ident, c.ident[:, :], sp, sp[:, :], False, True)
            mm(fw, pon, pon[:, :], c.ones_b, c.ones_b[:, :], nl, nl[:, :], True, True)

        def stage_b(n):
            hh, qt, kt, idx, nk = blocks[n]
            vh = vhs[hh % 2]
            t0 = qt * 512
            i = kt - 4 * qt
            ptr, pon = ps_tri[n % 2], ps_one[n % 2]
            sp, t1, pt = sps[n % 3], t1s[n % 2], pts[n % 2]
            po = ps_o[(hh * (S // 512) + qt) % 2]
            if idx == 0:
                if nk > 1:
                    fw.op("dve", lambda e: e.tensor_copy(out=R[:, :], in_=pon[:, :]), reads=[pon], writes=[R])
            else:
                fw.op("dve", lambda e: e.tensor_tensor(out=t1[:, :], in0=ptr[:, :], in1=R[:, :], op=ALU.add), reads=[ptr, R], writes=[t1])
                if idx < nk - 1:
                    fw.op("dve", lambda e: e.tensor_tensor(out=R[:, :], in0=R[:, :], in1=pon[:, :], op=ALU.add), reads=[R, pon], writes=[R])

        def stage_b2(n):
            hh, qt, kt, idx, nk = blocks[n]
            vh = vhs[hh % 2]
            t0 = qt * 512
            i = kt - 4 * qt
            ptr = ps_tri[n % 2]
            t1, pt = t1s[n % 2], pts[n % 2]
            po = ps_o[(hh * (S // 512) + qt) % 2]
            if idx == 0:
                fw.op("act", lambda e: e.activation(out=pt[:, :], in_=ptr[:, :], func=AF.Exp, scale=-1.0), reads=[ptr], writes=[pt])
            else:
                fw.op("act", lambda e: e.activation(out=pt[:, :], in_=t1[:, :], func=AF.Exp, scale=-1.0), reads=[t1], writes=[pt])
            if i >= 0:
                fw.op("pool", lambda e: e.tensor_tensor(out=pt[:, :], in0=pt[:, :], in1=masks[i][:, :], op=ALU.mult), reads=[pt, masks[i]], writes=[pt])
            warm(WARM_B)
            mm(fw, po, po[:, :], vh, vh[:, kt, :], pt, pt[:, :], start=(idx == 0), stop=(idx == nk - 1))
            if idx == nk - 1:
                so = state["so"]
                stg = stgs[so % 3]
                if so % 2 == 0:
                    fw.op("act", lambda e: e.copy(out=stg[:, :], in_=po[:, :]), reads=[po], writes=[stg])
                else:
                    fw.op("dve", lambda e: e.tensor_copy(out=stg[:, :], in_=po[:, :]), reads=[po], writes=[stg])
                state["so"] = so + 1
                fw.dma("sp", mixT_d[hh * 128:(hh + 1) * 128, t0:t0 + 512], stg[:, :], reads=[stg])

        nb_ = len(blocks)
        for n in range(nb_ + 2):
            fw.cast_tick((n + 1) / float(nb_))
            if n < nb_:
                stage_a1(n)
            if n >= 2:
                stage_b(n - 2)
            if 1 <= n <= nb_:
                stage_a2(n - 1)
            if n >= 2:
                stage_b2(n - 2)
        fw.barrier()
    with fw.phase():
        A = Attn(fw)
        pst = [fw.ps([128, 512], BF16, "pst") for _ in range(2)]
        stgs = [fw.sb([128, 512], BF16, "stg") for _ in range(3)]
        mem_part(fw, c, A, MK, S, qmT_d, mixT_d, pst, stgs, [0])
        fw.barrier()


def phase_mix_mlstm(fw, c, S, MK, qT_d, kT_d, ktm_d, v_d, o_d, g_d, qmT_d, mixT_d, cw_d, cb_d, gb_d, hn_d):
    NT = S // 128
    DH = 384
    with fw.phase():
        NQC = MW // 128
        cw = fw.sb([128, 2 * NQC, 4], F32, "cw")
        cb = fw.sb([128, 2 * NQC], F32, "cb")
        fw.dma("sp", cw[:, :, :], cw_d, writes=[cw])
        fw.dma("sp", cb[:, :], cb_d, writes=[cb])
        xins = [fw.sb([128, S + 4], BF16, "xin") for _ in range(2)]
        ys = [fw.sb([128, S], BF16, "y") for _ in range(2)]
        dgs = [fw.sb([128, 4, 128], BF16, "dg") for _ in range(2)]
        cbs = fw.sb([128, 2 * NQC], F32, "cbs")
        psc = [fw.ps([128, 512], F32, "psc") for _ in range(3)]
        pst = [fw.ps([128, 512], BF16, "pst") for _ in range(2)]
        stgs = [fw.sb([128, 4, 128], BF16, "stg") for _ in range(3)]
        kv = ktm_d.rearrange("(tb p) f -> p tb f", p=128)
        for x_ in xins:
            fw.op("dve", lambda e, x_=x_: e.memset(x_[:, 0:4], 0.0), writes=[x_])
        kscale = float(DH ** -0.5)
        n = 0
        pc = 0
        for cc in range(2 * NQC):
            src_d = qT_d if cc < NQC else kT_d
            r0 = (cc % NQC) * 128
            xin, y, dg = xins[cc % 2], ys[cc % 2], dgs[cc % 2]
            fw.dma("sp", xin[:, 4:S + 4], src_d[r0:r0 + 128, :], writes=[xin])
            for j in range(4):
                fw.op("dve", lambda e, dg=dg, j=j, cc=cc: e.tensor_scalar(out=dg[:, j, :], in0=c.ident_f[:, :], scalar1=cw[:, cc, j:j + 1], scalar2=None, op0=ALU.mult),
                      reads=[c.ident_f, cw], writes=[dg])
            for t0 in range(0, S, 512):
                ps = psc[pc % 3]
                pc += 1
                for j in range(4):
                    mm(fw, ps, ps[:, :], dg, dg[:, j, :], xin, xin[:, t0 + 1 + j:t0 + 1 + j + 512], start=(j == 0), stop=(j == 3))
                if cc < NQC:
                    fw.op("act", lambda e, ps=ps, y=y, t0=t0, cc=cc: e.activation(out=y[:, t0:t0 + 512], in_=ps[:, :], func=AF.Silu, bias=cb[:, cc:cc + 1], scale=1.0),
                          reads=[ps, cb], writes=[y])
                else:
                    fw.op("act", lambda e, ps=ps, y=y, t0=t0, cc=cc: e.activation(out=y[:, t0:t0 + 512], in_=ps[:, :], func=AF.Silu, bias=cb[:, cc:cc + 1], scale=1.0),
                          reads=[ps, cb], writes=[y])
                    fw.op("pool", lambda e, y=y, t0=t0: e.tensor_scalar(out=y[:, t0:t0 + 512], in0=y[:, t0:t0 + 512], scalar1=kscale, scalar2=None, op0=ALU.mult),
                          reads=[y], writes=[y])
            fw.dma("sp", src_d[r0:r0 + 128, :], y[:, :], reads=[y])
            if cc >= NQC:
                for tb0 in range(0, NT, 4):
                    ps = pst[n % 2]
                    st = stgs[n % 3]
                    n += 1
                    for i in range(4):
                        fw.op("pe", lambda e, ps=ps, y=y, i=i, tb0=tb0: e.transpose(out=ps[:, i * 128:(i + 1) * 128], in_=y[:, (tb0 + i) * 128:(tb0 + i + 1) * 128],
                                                                                   identity=c.ident[:]), reads=[y, c.ident], writes=[ps])
                    fw.op("dve", lambda e, ps=ps, st=st: e.tensor_copy(out=st[:, :, :], in_=ps[:, :].rearrange("p (a b) -> p a b", a=4)), reads=[ps], writes=[st])
                    fw.dma("sp", kv[:, tb0:tb0 + 4, r0:r0 + 128], st[:, :, :], reads=[st])
        fw.barrier()
    with fw.phase():
        NQC = MW // 128
        NG = NT * HM
        G = fw.sb([128, NT, 2 * HM], F32, "G")
        gb = fw.sb([128, 2 * HM], F32, "gb")
        hn = fw.sb([128, MW], F32, "hn")
        IG = fw.sb([128, NG], F32, "IG")
        LF = fw.sb([128, NG], F32, "LF")
        Bc = fw.sb([128, NG], F32, "Bc")
        BL = fw.sb([128, NG], F32, "BL")
        Ac = fw.sb([128, NG], F32, "Ac")
        WK = fw.sb([128, NG], F32, "WK")
        DEC = fw.sb([128, NG], F32, "DEC")
        tmp = fw.sb([128, NG], F32, "tmp")
        one1 = fw.sb([128, 1], F32, "one1")
        fw.op("dve", lambda e: e.memset(one1[:], 1.0), writes=[one1])
        onesf = fw.sb([128, 128], F32, "onesf")
        fw.op("dve", lambda e: e.memset(onesf[:], 1.0), writes=[onesf])
        trif = fw.sb([128, 128], F32, "trif")
        fw.op("pool", lambda e: e.affine_select(out=trif[:], in_=onesf[:], pattern=[[1, 128]], compare_op=ALU.is_ge,
                                                fill=0.0, base=0, channel_multiplier=-1), reads=[onesf], writes=[trif])
        fw.dma("sp", G[:, :, :], g_d.rearrange("(c p) e -> p c e", p=128), writes=[G])
        fw.dma("sp", gb[:, :], gb_d.partition_broadcast(128), writes=[gb])
        fw.dma("sp", hn[:, :], hn_d.partition_broadcast(128), writes=[hn])
        IGv = IG[:, :].rearrange("p (c h) -> p c h", h=HM)
        LFv = LF[:, :].rearrange("p (c h) -> p c h", h=HM)
        for cc in range(NT):
            fw.op("dve", lambda e, cc=cc: e.tensor_tensor(out=IGv[:, cc, :], in0=G[:, cc, 0:HM], in1=gb[:, 0:HM], op=ALU.add), reads=[G, gb], writes=[IG])
            fw.op("dve", lambda e, cc=cc: e.tensor_tensor(out=LFv[:, cc, :], in0=G[:, cc, HM:2 * HM], in1=gb[:, HM:2 * HM], op=ALU.add), reads=[G, gb], writes=[LF])
        fw.op("act", lambda e: e.activation(out=LF[:, :], in_=LF[:, :], func=AF.Exp, scale=-1.0), reads=[LF], writes=[LF])
        fw.op("act", lambda e: e.activation(out=LF[:, :], in_=LF[:, :], func=AF.Ln, bias=one1[:, 0:1], scale=1.0), reads=[LF, one1], writes=[LF])
        fw.op("dve", lambda e: e.tensor_scalar(out=LF[:, :], in0=LF[:, :], scalar1=-1.0, scalar2=None, op0=ALU.mult), reads=[LF], writes=[LF])
        ps_s = fw.ps([128, 512], F32, "ps_s")
        ps_b = fw.ps([128, 512], F32, "ps_b")
        psg = ps_b
        for c0 in range(0, NG, 512):
            c1 = min(NG, c0 + 512)
            mm(fw, psg, psg[:, 0:c1 - c0], trif, trif[:, :], LF, LF[:, c0:c1], True, True)
            fw.op("dve", lambda e, c0=c0, c1=c1: e.tensor_copy(out=Bc[:, c0:c1], in_=psg[:, 0:c1 - c0]), reads=[psg], writes=[Bc])
            mm(fw, psg, psg[:, 0:c1 - c0], onesf, onesf[:, :], LF, LF[:, c0:c1], True, True)
            fw.op("dve", lambda e, c0=c0, c1=c1: e.tensor_copy(out=BL[:, c0:c1], in_=psg[:, 0:c1 - c0]), reads=[psg], writes=[BL])
        fw.op("dve", lambda e: e.tensor_tensor(out=Ac[:, :], in0=IG[:, :], in1=Bc[:, :], op=ALU.subtract), reads=[IG, Bc], writes=[Ac])
        fw.op("dve", lambda e: e.tensor_tensor(out=tmp[:, :], in0=Ac[:, :], in1=BL[:, :], op=ALU.add), reads=[Ac, BL], writes=[tmp])
        fw.op("act", lambda e: e.activation(out=WK[:, :], in_=tmp[:, :], func=AF.Exp), reads=[tmp], writes=[WK])
        fw.op("act", lambda e: e.activation(out=DEC[:, :], in_=BL[:, :], func=AF.Exp), reads=[BL], writes=[DEC])
        CT = [fw.sb([128, 3, 385], F32, "CT") for _ in range(HM)]
        CTb = [fw.sb([128, 3, 385], BF16, "CTb") for _ in range(HM)]
        qcs = [fw.sb([128, NQC, 128], BF16, "qc") for _ in range(2)]
        kcs = [fw.sb([128, NQC, 128], BF16, "kc") for _ in range(2)]
        kts = [fw.sb([128, MW], BF16, "ktm") for _ in range(2)]
        vas = [fw.sb([128, HM, 385], BF16, "va") for _ in range(2)]
        ocs = [fw.sb([128, MW], BF16, "oc") for _ in range(2)]
        ogs = [fw.sb([128, MW], F32, "og") for _ in range(2)]
        mixtms = [fw.sb([128, 4, MW], BF16, "mixtm") for _ in range(2)]
        lfbs = [fw.sb([128, 128], F32, "lfb") for _ in range(2)]
        wts = [fw.sb([128, 128], F32, "wt") for _ in range(2)]
        ebrs = [fw.sb([128, 128], F32, "ebr") for _ in range(2)]
        ptb = [fw.sb([128, 128], BF16, "ptb") for _ in range(2)]
        qss = [fw.sb([128, 3, 128], BF16, "qs") for _ in range(2)]
        vws = [fw.sb([128, 385], BF16, "vw") for _ in range(2)]
        hgs = [fw.sb([128, 384], F32, "hg") for _ in range(2)]
        junk = fw.sb([128, 384], F32, "junk")
        sm = [fw.sb([128, 4], F32, "sm") for _ in range(4)]
        ps_n = [fw.ps([128, 512], F32, "ps_n") for _ in range(2)]
        ps_u = [fw.ps([128, 512], F32, "ps_u") for _ in range(3)]
        pst = [fw.ps([128, 512], BF16, "pst") for _ in range(1)]
        stgs = [fw.sb([128, 512], BF16, "stg") for _ in range(3)]
        cnt = [0]
        for v_ in vas:
            fw.op("dve", lambda e, v_=v_: e.memset(v_[:, :, 384:385], 1.0), writes=[v_])
        qv = qT_d.rearrange("(a p) s -> p a s", p=128)
        kvv = kT_d.rearrange("(a p) s -> p a s", p=128)
        n = 0
        for cc in range(NT):
            t0 = cc * 128
            qc, kc, ktm, va, oc, og = qcs[cc % 2], kcs[cc % 2], kts[cc % 2], vas[cc % 2], ocs[cc % 2], ogs[cc % 2]
            mixtm = mixtms[(cc // 4) % 2]
            fw.dma("sp", qc[:, :, :], qv[:, :, t0:t0 + 128], writes=[qc])
            fw.dma("sp", kc[:, :, :], kvv[:, :, t0:t0 + 128], writes=[kc])
            fw.dma("sp", ktm[:, :], ktm_d[t0:t0 + 128, :], writes=[ktm])
            fw.dma("sp", va[:, :, 0:384], v_d[t0:t0 + 128, :].rearrange("p (h d) -> p h d", h=HM), writes=[va])
            fw.dma("sp", oc[:, :], o_d[t0:t0 + 128, :], writes=[oc])
            fw.op("act", lambda e, og=og, oc=oc: e.activation(out=og[:, :], in_=oc[:, :], func=AF.Sigmoid), reads=[oc], writes=[og])
            fw.cast_tick((cc + 1) / float(NT))
            for hh in range(HM):
                ch = cc * HM + hh
                lfb, wt, ebr, pt, qs, vw, hg = lfbs[n % 2], wts[n % 2], ebrs[n % 2], ptb[n % 2], qss[n % 2], vws[n % 2], hgs[n % 2]
                pn = ps_n[n % 2]
                s1, s2 = sm[(2 * n) % 4], sm[(2 * n + 1) % 4]
                n += 1
                for dc in range(3):
                    mm(fw, ps_s, ps_s[:, 0:128], kc, kc[:, 3 * hh + dc, :], qc, qc[:, 3 * hh + dc, :], start=(dc == 0), stop=(dc == 2))
                fw.op("dve", lambda e, lfb=lfb, ch=ch: e.tensor_scalar(out=lfb[:, :], in0=onesf[:, :], scalar1=LF[:, ch:ch + 1], scalar2=None, op0=ALU.mult),
                      reads=[onesf, LF], writes=[lfb])
                mm(fw, ps_b, ps_b[:, 0:128], lfb, lfb[:, :], trif, trif[:, :], True, True)
                fw.op("act", lambda e, wt=wt, ch=ch: e.activation(out=wt[:, :], in_=ps_b[:, 0:128], func=AF.Exp, bias=Ac[:, ch:ch + 1], scale=1.0),
                      reads=[ps_b, Ac], writes=[wt])
                fw.op("act", lambda e, ebr=ebr: e.activation(out=ebr[:, :], in_=ps_b[:, 0:128], func=AF.Exp), reads=[ps_b], writes=[ebr])
                fw.op("pool", lambda e, wt=wt: e.tensor_tensor(out=wt[:, :], in0=wt[:, :], in1=trif[:, :], op=ALU.mult), reads=[wt, trif], writes=[wt])
                fw.op("dve", lambda e, pt=pt, wt=wt: e.tensor_tensor(out=pt[:, :], in0=ps_s[:, 0:128], in1=wt[:, :], op=ALU.mult), reads=[ps_s, wt], writes=[pt])
                mm(fw, pn, pn[:, 0:385], pt, pt[:, :], va, va[:, hh, :], start=True, stop=(cc == 0))
                if cc > 0:
                    for dc in range(3):
                        fw.op("dve", lambda e, qs=qs, dc=dc, hh=hh, ebr=ebr: e.tensor_tensor(out=qs[:, dc, :], in0=qc[:, 3 * hh + dc, :], in1=ebr[:, :], op=ALU.mult),
                              reads=[qc, ebr], writes=[qs])
                    for dc in range(3):
                        mm(fw, pn, pn[:, 0:385], qs, qs[:, dc, :], CTb[hh], CTb[hh][:, dc, :], start=False, stop=(dc == 2))
                fw.op("dve", lambda e, s1=s1, pn=pn: e.tensor_scalar(out=s1[:, 2:3], in0=pn[:, 384:385], scalar1=-1.0, scalar2=None, op0=ALU.mult), reads=[pn], writes=[s1])
                fw.op("dve", lambda e, s1=s1, pn=pn: e.tensor_tensor(out=s1[:, 0:1], in0=pn[:, 384:385], in1=s1[:, 2:3], op=ALU.max), reads=[pn, s1], writes=[s1])
                fw.op("dve", lambda e, s1=s1: e.tensor_scalar(out=s1[:, 0:1], in0=s1[:, 0:1], scalar1=1.0, scalar2=None, op0=ALU.max), reads=[s1], writes=[s1])
                fw.op("dve", lambda e, s1=s1: e.reciprocal(out=s1[:, 1:2], in_=s1[:, 0:1]), reads=[s1], writes=[s1])
                fw.op("dve", lambda e, hg=hg, pn=pn, s1=s1, hh=hh, og=og: e.scalar_tensor_tensor(out=hg[:, :], in0=pn[:, 0:384], scalar=s1[:, 1:2],
                                                                                                 in1=og[:, hh * 384:(hh + 1) * 384], op0=ALU.mult, op1=ALU.mult),
                      reads=[pn, s1, og], writes=[hg])
                fw.op("act", lambda e, hg=hg, s2=s2: e.activation(out=junk[:, :], in_=hg[:, :], func=AF.Square, accum_out=s2[:, 0:1]), reads=[hg], writes=[junk, s2])
                fw.op("act", lambda e, s2=s2: e.activation(out=s2[:, 1:2], in_=s2[:, 0:1], func=AF.Sqrt, bias=c.eps[:, 0:1], scale=1.0 / DH), reads=[s2, c.eps], writes=[s2])
                fw.op("dve", lambda e, s2=s2: e.reciprocal(out=s2[:, 2:3], in_=s2[:, 1:2]), reads=[s2], writes=[s2])
                fw.op("dve", lambda e, hg=hg, s2=s2, hh=hh, cc=cc, mixtm=mixtm: e.scalar_tensor_tensor(
                    out=mixtm[:, cc % 4, hh * 384:(hh + 1) * 384], in0=hg[:, :], scalar=s2[:, 2:3], in1=hn[:, hh * 384:(hh + 1) * 384], op0=ALU.mult, op1=ALU.mult),
                    reads=[hg, s2, hn], writes=[mixtm])
                if cc < NT - 1:
                    fw.op("pool", lambda e, vw=vw, va=va, hh=hh, ch=ch: e.tensor_scalar(out=vw[:, :], in0=va[:, hh, :], scalar1=WK[:, ch:ch + 1], scalar2=None, op0=ALU.mult),
                          reads=[va, WK], writes=[vw])
                    for dc in range(3):
                        mm(fw, ps_u[dc], ps_u[dc][:, 0:385], ktm, ktm[:, (3 * hh + dc) * 128:(3 * hh + dc + 1) * 128], vw, vw[:, :], True, True)
                    for dc in range(3):
                        if cc == 0:
                            fw.op("dve", lambda e, hh=hh, dc=dc: e.tensor_copy(out=CT[hh][:, dc, :], in_=ps_u[dc][:, 0:385]), reads=[ps_u[dc]], writes=[CT[hh]])
                        else:
                            fw.op("dve", lambda e, hh=hh, dc=dc, ch=ch: e.scalar_tensor_tensor(out=CT[hh][:, dc, :], in0=CT[hh][:, dc, :], scalar=DEC[:, ch:ch + 1],
                                                                                              in1=ps_u[dc][:, 0:385], op0=ALU.mult, op1=ALU.add),
                                  reads=[CT[hh], DEC, ps_u[dc]], writes=[CT[hh]])
                    fw.op("act", lambda e, hh=hh: e.copy(out=CTb[hh][:, :, :], in_=CT[hh][:, :, :]), reads=[CT[hh]], writes=[CTb[hh]])
            if cc % 4 == 3:
                mixer_store(fw, c, mixtm, mixT_d, (cc - 3) * 128, pst, stgs, cnt, fcs=range(NQC))
        fw.barrier()
    with fw.phase():
        A = Attn(fw)
        pst = [fw.ps([128, 512], BF16, "pst") for _ in range(2)]
        stgs = [fw.sb([128, 512], BF16, "stg") for _ in range(3)]
        mem_part(fw, c, A, MK, S, qmT_d, mixT_d, pst, stgs, [0])
        fw.barrier()


def attn_finish_gated(fw, A, outs, sum_col, gate_b, gate_ap, acc_b, acc_ap, mode, dst_b=None, dst_ap=None, dv=128):
    A.q.append(lambda: _attn_finish_gated(fw, A, outs, sum_col, gate_b, gate_ap, acc_b, acc_ap, mode, dst_b, dst_ap, dv))


def _attn_finish_gated(fw, A, outs, sum_col, gate_b, gate_ap, acc_b, acc_ap, mode, dst_b=None, dst_ap=None, dv=128):
    for qb, (bank, col) in enumerate(outs):
        rc = A.rc[A.ri % 4]
        A.ri += 1
        fw.op("dve", lambda e, rc=rc, bank=bank, col=col: e.tensor_scalar(out=rc[:, 0:1], in0=bank[:, col + sum_col:col + sum_col + 1], scalar1=1e-30,
                                                                           scalar2=None, op0=ALU.max), reads=[bank], writes=[rc])
        fw.op("dve", lambda e, rc=rc: e.reciprocal(out=rc[:, 1:2], in_=rc[:, 0:1]), reads=[rc], writes=[rc])
        fw.op("dve", lambda e, rc=rc, qb=qb: e.tensor_tensor(out=rc[:, 2:3], in0=rc[:, 1:2], in1=gate_ap(qb), op=ALU.mult), reads=[rc, gate_b], writes=[rc])
        if mode == "set":
            fw.op("dve", lambda e, rc=rc, bank=bank, col=col, qb=qb: e.tensor_scalar(out=acc_ap(qb), in0=bank[:, col:col + dv], scalar1=rc[:, 2:3], scalar2=None,
                                                                                      op0=ALU.mult), reads=[bank, rc], writes=[acc_b])
        elif mode == "add":
            fw.op("dve", lambda e, rc=rc, bank=bank, col=col, qb=qb: e.scalar_tensor_tensor(out=acc_ap(qb), in0=bank[:, col:col + dv], scalar=rc[:, 2:3], in1=acc_ap(qb),
                                                                                             op0=ALU.mult, op1=ALU.add), reads=[bank, rc, acc_b], writes=[acc_b])
        else:
            fw.op("dve", lambda e, rc=rc, bank=bank, col=col, qb=qb: e.scalar_tensor_tensor(out=dst_ap(qb), in0=bank[:, col:col + dv], scalar=rc[:, 2:3], in1=acc_ap(qb),
                                                                                             op0=ALU.mult, op1=ALU.add), reads=[bank, rc, acc_b], writes=[dst_b])


def phase_mix_nsa(fw, c, S, MK, qT_d, kvT_d, v_d, g_d, qmT_d, mixT_d, peT_d, w1_bf, w2_bf):
    NT = S // 128
    NQ = S // 512
    NCMP = (S - 32) // 16 + 1
    NCT = (NCMP + 127) // 128
    scale = 128 ** -0.5
    with fw.phase():
        KC = [fw.sb([128, 256], BF16, "KC") for _ in range(NGN)]
        VC = [fw.sb([128, 2, 193], BF16, "VC") for _ in range(NGN)]
        KS = [fw.sb([128, S], BF16, "KS") for _ in range(NGN)]
        KW = [fw.sb([128, S], BF16, "KW") for _ in range(NGN)]
        VS = [fw.sb([128, NT, 129], BF16, "VS") for _ in range(NGN)]
        VW = [fw.sb([128, NT, 129], BF16, "VW") for _ in range(NGN)]
        ones5 = fw.sb([128, 512], BF16, "ones5")
        fw.op("dve", lambda e: e.memset(ones5[:], 1.0), writes=[ones5])
        A = Attn(fw)
        pst = [fw.ps([128, 512], BF16, "pst") for _ in range(1)]
        psm = fw.ps([128, 512], F32, "psm")
        vv = v_d.rearrange("(kt p) f -> p kt f", p=128)
        GB = NGN * 128
        for g in range(NGN):
            fw.dma("sp", KS[g][:, :], kvT_d[2 * GB + g * 128:2 * GB + (g + 1) * 128, :], writes=[KS[g]])
            fw.dma("sp", KW[g][:, :], kvT_d[3 * GB + g * 128:3 * GB + (g + 1) * 128, :], writes=[KW[g]])
            for k0 in range(0, NT, 8):
                k1 = min(NT, k0 + 8)
                fw.dma("sp", VS[g][:, k0:k1, 0:128], vv[:, k0:k1, g * 128:(g + 1) * 128], writes=[VS[g]], cont=(k0 > 0))
                fw.dma("sp", VW[g][:, k0:k1, 0:128], vv[:, k0:k1, GB + g * 128:GB + (g + 1) * 128], writes=[VW[g]], cont=(k0 > 0))
            fw.op("dve", lambda e, g=g: e.memset(VS[g][:, :, 128:129], 1.0), writes=[VS[g]])
            fw.op("dve", lambda e, g=g: e.memset(VW[g][:, :, 128:129], 1.0), writes=[VW[g]])
            fw.op("dve", lambda e, g=g: e.memset(VC[g][:, :, 128:193], 1.0), writes=[VC[g]])
            for nt in range(2):
                fw.op("pool", lambda e, g=g, nt=nt: e.affine_select(out=VC[g][:, nt, 129:193], in_=VC[g][:, nt, 129:193], pattern=[[-4, 64]], compare_op=ALU.is_gt,
                                                                    fill=0.0, base=nt * 128 + 2, channel_multiplier=1), reads=[VC[g]], writes=[VC[g]])
                fw.op("pool", lambda e, g=g, nt=nt: e.affine_select(out=VC[g][:, nt, 129:193], in_=VC[g][:, nt, 129:193], pattern=[[4, 64]], compare_op=ALU.is_gt,
                                                                    fill=0.0, base=4 - nt * 128, channel_multiplier=-1), reads=[VC[g]], writes=[VC[g]])
        with ExitStack() as cst:
            save = fw.pstack
            fw.pstack = cst
            xc = [fw.sb([128, S], BF16, "xc") for _ in range(2)]
            w1s = [fw.sb([128, 32, 256], BF16, "w1s") for _ in range(2)]
            w2s = [fw.sb([128, 2, 128], BF16, "w2s") for _ in range(2)]
            pef = fw.sb([128, 32], F32, "pef")
            peb = [fw.sb([128, 32], BF16, "peb") for _ in range(2)]
            cbias = [fw.sb([128, 2], F32, "cbias") for _ in range(2)]
            xh = [fw.sb([128, 256], F32, "xh") for _ in range(2)]
            x2 = [fw.sb([128, 256], F32, "x2") for _ in range(2)]
            hact = [[fw.sb([128, 256], BF16, "hact") for _ in range(2)] for _ in range(2)]
            it = 0
            for which in range(2):
                w1v = w1_bf[which].rearrange("(l p) j -> p l j", p=128)
                w2v = w2_bf[which].rearrange("(jc p) d -> p jc d", p=128)
                w1, w2, pb, cbs = w1s[which], w2s[which], peb[which], cbias[which]
                fw.dma("sp", w1[:, 0:16, :], w1v[:, 0:16, :], writes=[w1])
                fw.dma("sp", w1[:, 16:32, :], w1v[:, 16:32, :], writes=[w1], cont=True)
                fw.dma("sp", w2[:, :, :], w2v, writes=[w2])
                fw.dma("sp", pef[:, :], peT_d[which], writes=[pef])
                fw.op("dve", lambda e, pb=pb: e.tensor_copy(out=pb[:, :], in_=pef[:, :]), reads=[pef], writes=[pb])
                for jc in range(2):
                    for l in range(32):
                        mm(fw, psm, psm[:, 0:1], w1, w1[:, l, jc * 128:(jc + 1) * 128], pb, pb[:, l:l + 1], start=(l == 0), stop=(l == 31))
                    fw.op("dve", lambda e, cbs=cbs, jc=jc: e.tensor_copy(out=cbs[:, jc:jc + 1], in_=psm[:, 0:1]), reads=[psm], writes=[cbs])
                for g in range(NGN):
                    x = xc[it % 2]
                    it += 1
                    r0 = which * GB + g * 128
                    fw.dma("sp", x[:, :], kvT_d[r0:r0 + 128, :], writes=[x])
                    for jc in range(2):
                        ha = hact[jc][g]
                        xhh, xx2 = xh[jc], x2[jc]
                        for l in range(32):
                            mm(fw, psm, psm[:, 0:NCMP], w1, w1[:, l, jc * 128:(jc + 1) * 128], x, x[:, l:l + 16 * (NCMP - 1) + 1:16], start=(l == 0), stop=(l == 31))
                        fw.op("act", lambda e, xhh=xhh, cbs=cbs, jc=jc: e.activation(out=xhh[:, 0:NCMP], in_=psm[:, 0:NCMP], func=AF.Identity, bias=cbs[:, jc:jc + 1], scale=1.0),
                              reads=[psm, cbs], writes=[xhh])
                        fw.op("dve", lambda e, xhh=xhh, xx2=xx2: e.tensor_tensor(out=xx2[:, 0:NCMP], in0=xhh[:, 0:NCMP], in1=xhh[:, 0:NCMP], op=ALU.mult), reads=[xhh], writes=[xx2])
                        fw.op("dve", lambda e, xx2=xx2: e.tensor_scalar(out=xx2[:, 0:NCMP], in0=xx2[:, 0:NCMP], scalar1=0.044715, scalar2=1.0, op0=ALU.mult, op1=ALU.add),
                              reads=[xx2], writes=[xx2])
                        fw.op("dve", lambda e, xhh=xhh, xx2=xx2: e.tensor_tensor(out=xx2[:, 0:NCMP], in0=xx2[:, 0:NCMP], in1=xhh[:, 0:NCMP], op=ALU.mult), reads=[xhh, xx2], writes=[xx2])
                        fw.op("act", lambda e, xx2=xx2: e.activation(out=xx2[:, 0:NCMP], in_=xx2[:, 0:NCMP], func=AF.Sigmoid, scale=1.5957691216057308), reads=[xx2], writes=[xx2])
                        fw.op("dve", lambda e, xhh=xhh, xx2=xx2, ha=ha: e.tensor_tensor(out=ha[:, 0:NCMP], in0=xx2[:, 0:NCMP], in1=xhh[:, 0:NCMP], op=ALU.mult), reads=[xhh, xx2], writes=[ha])
                    if which == 0:
                        for jc in range(2):
                            mm(fw, psm, psm[:, 0:NCMP], w2, w2[:, jc, :], hact[jc][g], hact[jc][g][:, 0:NCMP], start=(jc == 0), stop=(jc == 1))
                        fw.op("dve", lambda e, g=g: e.tensor_copy(out=KC[g][:, 0:NCMP], in_=psm[:, 0:NCMP]), reads=[psm], writes=[KC[g]])
                    else:
                        for nt in range(NCT):
                            sz = min(128, NCMP - nt * 128)
                            for jc in range(2):
                                mm(fw, psm, psm[0:sz, 0:128], hact[jc][g], hact[jc][g][:, nt * 128:nt * 128 + sz], w2, w2[:, jc, :], start=(jc == 0), stop=(jc == 1))
                            fw.op("dve", lambda e, g=g, nt=nt, sz=sz: e.tensor_copy(out=VC[g][0:sz, nt, 0:128], in_=psm[0:sz, 0:128]), reads=[psm], writes=[VC[g]])
            fw.barrier()
            fw.pstack = save
        CM = []
        for i in range(4):
            m = fw.sb([128, 512], BF16, "CM")
            fw.op("pool", lambda e, m=m, i=i: e.affine_select(out=m[:], in_=ones5[:], pattern=[[1, 512]], compare_op=ALU.is_ge, fill=0.0, base=-128 * i, channel_multiplier=-1),
                  reads=[ones5], writes=[m])
            CM.append(m)
        WM = {}
        for i in range(-4, 0):
            m = fw.sb([128, 512], BF16, "WM")
            fw.op("pool", lambda e, m=m, i=i: e.affine_select(out=m[:], in_=ones5[:], pattern=[[-1, 512]], compare_op=ALU.is_ge, fill=0.0, base=511 + 128 * i, channel_multiplier=1),
                  reads=[ones5], writes=[m])
            WM[i] = m
        EXP = fw.sb([64, S], BF16, "EXP")
        fw.op("dve", lambda e: e.memset(EXP[:], 1.0), writes=[EXP])
        fw.op("pool", lambda e: e.affine_select(out=EXP[:], in_=EXP[:], pattern=[[1, S]], compare_op=ALU.is_ge, fill=0.0, base=0, channel_multiplier=-64), reads=[EXP], writes=[EXP])
        fw.op("pool", lambda e: e.affine_select(out=EXP[:], in_=EXP[:], pattern=[[-1, S]], compare_op=ALU.is_ge, fill=0.0, base=63, channel_multiplier=64), reads=[EXP], writes=[EXP])
        cms = [fw.sb([128, 512], BF16, "cmk") for _ in range(2)]
        mks = [fw.sb([128, 512], BF16, "mk") for _ in range(NT)]
        qts = [fw.sb([128, HQ, 512], BF16, "qt") for _ in range(1)]
        qms = [fw.sb([128, HM, 512], BF16, "qm") for _ in range(1)]
        GS = fw.sb([128, 4, HQ * 3], F32, "GS")
        IMP = fw.sb([128, 4, NGN, 64], F32, "IMP")
        SC = fw.sb([128, 64], F32, "SC")
        SC2 = fw.sb([128, 64], F32, "SC2")
        M8 = fw.sb([128, 16], F32, "M8")
        SEL = fw.sb([128, 64], BF16, "SEL")
        SELT = [fw.sb([64, 512], BF16, "SELT") for _ in range(NGN)]
        acc = [fw.sb([128, 4, 128], F32, "acc") for _ in range(2)]
        mixtm = fw.sb([128, 4, FWD], BF16, "mixtm")
        stgs = [fw.sb([128, 512], BF16, "stg") for _ in range(3)]
        cnt = [0]
        qv = qT_d.rearrange("(a p) s -> p a s", p=128)
        qmv = qmT_d.rearrange("(a p) s -> p a s", p=128)
        for qt in range(NQ):
            t0 = qt * 512
            qtb, qmb = qts[0], qms[0]
            fw.dma("sp", qtb[:, 0:HQ // 2, :], qv[:, 0:HQ // 2, t0:t0 + 512], writes=[qtb])
            fw.dma("sp", qtb[:, HQ // 2:HQ, :], qv[:, HQ // 2:HQ, t0:t0 + 512], writes=[qtb], cont=True)
            fw.dma("sp", qmb[:, :, :], qmv[:, :, t0:t0 + 512], writes=[qmb])
            fw.dma("sp", GS[:, :, :], g_d[t0:t0 + 512, :].rearrange("(qb p) e -> p qb e", p=128), writes=[GS])
            fw.op("act", lambda e: e.activation(out=GS[:, :, :], in_=GS[:, :, :], func=AF.Sigmoid), reads=[GS], writes=[GS])
            ckt = []
            for nt in range(NCT):
                sz = min(128, NCMP - nt * 128)
                first_t = 16 * (nt * 128) + 31
                last_t = 16 * (nt * 128 + sz - 1) + 31
                if t0 + 511 < first_t:
                    continue
                if t0 >= last_t:
                    ckt.append((nt, sz, None))
                else:
                    cm = cms[nt % 2]
                    fw.op("pool", lambda e, cm=cm, nt=nt: e.affine_select(out=cm[:], in_=ones5[:], pattern=[[1, 512]], compare_op=ALU.is_ge, fill=0.0,
                                                                          base=t0 - 31 - 2048 * nt, channel_multiplier=-16), reads=[ones5], writes=[cm])
                    ckt.append((nt, sz, cm))

            def cmp_tiles(g, nv):
                return [(KC[g], KC[g][:, nt * 128:nt * 128 + sz], VC[g], VC[g][0:sz, nt, 0:nv], cm, (cm[0:sz, :] if cm is not None else None), 0, 4, sz)
                        for (nt, sz, cm) in ckt]


            def imp_acc(outs, hh, g):
                for qb, (bank, col) in enumerate(outs):
                    rc = A.rc[A.ri % 4]
                    A.ri += 1
                    fw.op("dve", lambda e, rc=rc, bank=bank, col=col: e.tensor_scalar(out=rc[:, 0:1], in0=bank[:, col + 128:col + 129], scalar1=1e-30, scalar2=None, op0=ALU.max),
                          reads=[bank], writes=[rc])
                    fw.op("dve", lambda e, rc=rc: e.reciprocal(out=rc[:, 1:2], in_=rc[:, 0:1]), reads=[rc], writes=[rc])
                    if hh % 6 == 0:
                        fw.op("dve", lambda e, rc=rc, bank=bank, col=col, qb=qb, g=g: e.tensor_scalar(out=IMP[:, qb, g, :], in0=bank[:, col + 129:col + 193], scalar1=rc[:, 1:2],
                                                                                                      scalar2=None, op0=ALU.mult), reads=[bank, rc], writes=[IMP])
                    else:
                        fw.op("dve", lambda e, rc=rc, bank=bank, col=col, qb=qb, g=g: e.scalar_tensor_tensor(out=IMP[:, qb, g, :], in0=bank[:, col + 129:col + 193], scalar=rc[:, 1:2],
                                                                                                             in1=IMP[:, qb, g, :], op0=ALU.mult, op1=ALU.add),
                              reads=[bank, rc, IMP], writes=[IMP])
            for hh in range(HQ):
                g = hh // 6
                outs = attn_head(fw, A, qtb, lambda lo, hi, hh=hh: qtb[:, hh, lo * 128:hi * 128], cmp_tiles(g, 193), 193, scale)
                A.q.append(lambda outs=outs, hh=hh, g=g: imp_acc(outs, hh, g))
            A.flush()
            for g in range(NGN):
                for qb in range(4):
                    qa = 4 * qt + qb
                    fw.op("dve", lambda e, qb=qb, g=g: e.tensor_copy(out=SC[:, :], in_=IMP[:, qb, g, :]), reads=[IMP], writes=[SC])
                    for hf in range(2):
                        cur = 2 * qa + hf
                        rs = slice(hf * 64, hf * 64 + 64)
                        fw.op("dve", lambda e, rs=rs: e.memset(SC[rs, 0:1], 1e6), writes=[SC])
                        fw.op("dve", lambda e, rs=rs, cur=cur: e.memset(SC[rs, max(cur - 1, 0):cur + 1], 1e6), writes=[SC])
                        if cur < 63:
                            fw.op("dve", lambda e, rs=rs, cur=cur: e.memset(SC[rs, cur + 1:64], -1e9), writes=[SC])
                    fw.op("dve", lambda e: e.max(out=M8[:, 0:8], in_=SC[:, :]), reads=[SC], writes=[M8])
                    fw.op("dve", lambda e: e.match_replace(out=SC2[:, :], in_to_replace=M8[:, 0:8], in_values=SC[:, :], imm_value=-3e9), reads=[SC, M8], writes=[SC2])
                    fw.op("dve", lambda e: e.max(out=M8[:, 8:16], in_=SC2[:, :]), reads=[SC2], writes=[M8])
                    fw.op("dve", lambda e: e.tensor_scalar(out=SEL[:, :], in0=SC[:, :], scalar1=M8[:, 15:16], scalar2=None, op0=ALU.is_ge), reads=[SC, M8], writes=[SEL])
                    for hf in range(2):
                        cur = 2 * qa + hf
                        if cur < 63:
                            fw.op("dve", lambda e, hf=hf, cur=cur: e.memset(SEL[hf * 64:hf * 64 + 64, cur + 1:64], 0.0), writes=[SEL])
                    ps = pst[0]
                    fw.op("pe", lambda e, ps=ps: e.transpose(out=ps[0:64, 0:128], in_=SEL[:, :], identity=c.ident[:]), reads=[SEL, c.ident], writes=[ps])
                    fw.op("act", lambda e, ps=ps, g=g, qb=qb: e.copy(out=SELT[g][:, qb * 128:(qb + 1) * 128], in_=ps[0:64, 0:128]), reads=[ps], writes=[SELT[g]])
                nkt = 4 * qt + 4
                for kt in range(nkt):
                    i = kt - 4 * qt
                    mm(fw, psm, psm[:, :], EXP, EXP[:, kt * 128:(kt + 1) * 128], SELT[g], SELT[g][:, :], True, True)
                    mk = mks[kt]
                    if i >= 0:
                        fw.op("dve", lambda e, mk=mk, i=i: e.tensor_tensor(out=mk[:, :], in0=psm[:, :], in1=CM[i][:, :], op=ALU.mult), reads=[psm, CM[i]], writes=[mk])
                    elif kt % 2 == 0:
                        fw.op("act", lambda e, mk=mk: e.copy(out=mk[:, :], in_=psm[:, :]), reads=[psm], writes=[mk])
                    else:
                        fw.op("dve", lambda e, mk=mk: e.tensor_copy(out=mk[:, :], in_=psm[:, :]), reads=[psm], writes=[mk])
                for hp in range(6):
                    hh = g * 6 + hp
                    fw.cast_tick((qt * HQ + hh + 1) / float(NQ * HQ))
                    ac = acc[hh % 2]
                    qf = lambda lo, hi, hh=hh: qtb[:, hh, lo * 128:hi * 128]
                    outs = attn_head(fw, A, qtb, qf, cmp_tiles(g, 129), 129, scale)
                    attn_finish_gated(fw, A, outs, 128, GS, lambda qb, hh=hh: GS[:, qb, hh * 3:hh * 3 + 1], ac, lambda qb, ac=ac: ac[:, qb, :], "set")
                    kts = []
                    for kt in range(nkt):
                        i = kt - 4 * qt
                        kts.append((KS[g], KS[g][:, kt * 128:(kt + 1) * 128], VS[g], VS[g][:, kt, :], mks[kt], mks[kt][:, max(i, 0) * 128:512], max(i, 0), 4))
                    outs = attn_head(fw, A, qtb, qf, kts, 129, scale, mask_eng="alt")
                    attn_finish_gated(fw, A, outs, 128, GS, lambda qb, hh=hh: GS[:, qb, hh * 3 + 1:hh * 3 + 2], ac, lambda qb, ac=ac: ac[:, qb, :], "add")
                    kts = []
                    for i in range(-4, 4):
                        kt = 4 * qt + i
                        if kt < 0:
                            continue
                        if i >= 0:
                            lo, hi, mb = i, 4, CM[i]
                        else:
                            lo, hi, mb = 0, 5 + i, WM[i]
                        kts.append((KW[g], KW[g][:, kt * 128:(kt + 1) * 128], VW[g], VW[g][:, kt, :], mb, mb[:, lo * 128:hi * 128], lo, hi))
                    outs = attn_head(fw, A, qtb, qf, kts, 129, scale)
                    attn_finish_gated(fw, A, outs, 128, GS, lambda qb, hh=hh: GS[:, qb, hh * 3 + 2:hh * 3 + 3], ac, lambda qb, ac=ac: ac[:, qb, :], "final",
                                      dst_b=mixtm, dst_ap=lambda qb, hh=hh: mixtm[:, qb, hh * 128:(hh + 1) * 128])
            mem_attention_qtile(fw, A, MK, qmb, mixtm)
            A.flush()
            mixer_store(fw, c, mixtm, mixT_d, t0, pst, stgs, cnt, fcs=range(NFC))
        fw.barrier()


W_IN = {0: 3620, 1: 5120, 2: 6664, 3: 2432}
W_MY = {0: MW + 6 * NGN * 128 + HQ * 3 + MEMW, 1: 3 * MW + MEMW, 2: 4 * MW + 2 * HM + MEMW, 3: MW + 2 * NGS * 64 + MEMW}
DEBUG = set()
LAST = {}
GROUPS = [[0, 1], [2, 3], [4, 5], [6, 7]]


def cc_rows(SH):
    return max(128, min(D, (1 << 20) // SH))


def gl(v):
    return np.ascontiguousarray(np.asarray(v, np.float32).reshape(NCH, 128).T)


def build(S, layers):
    SH = S // NR
    nc = bass.Bass("TRN2", target_bir_lowering=False)
    dr = {}

    def din(name, shape, dt=F32):
        dr[name] = nc.dram_tensor(name, list(shape), dt, kind="ExternalInput").ap()
        return dr[name]

    def dsc(name, shape, dt):
        dr[name] = nc.dram_tensor(name, list(shape), dt, kind=("ExternalOutput" if name in DEBUG else "Internal")).ap()
        return dr[name]

    din("xT", [D, SH])
    din("memT", [D, MEM])
    din("mem_norm", [128, NCH])
    din("mem_w_kv", [D, 2 * HM * 128])
    din("final_norm", [128, NCH])
    casts = [("mem_w_kv", [D, 2 * HM * 128])]
    lcasts = {}
    for l in layers:
        ncast0 = len(casts)
        p = "l%d_" % l
        din(p + "norm_mix", [128, NCH])
        din(p + "norm_ffn", [128, NCH])
        din(p + "w_in", [D, W_MY[l]])
        din(p + "w_out", [FWD, D])
        din(p + "w_gate", [D, DFF])
        din(p + "w_up", [D, DFF])
        din(p + "w_down", [DFF, D])
        casts += [(p + "w_in", [D, W_MY[l]]), (p + "w_out", [FWD, D]), (p + "w_gate", [D, DFF]), (p + "w_up", [D, DFF]),
                  (p + "w_down", [DFF, D])]
        if l == 3:
            din(p + "sinks", [HS])
        if l == 0:
            for kv_ in ("k", "v"):
                din(p + "cmp_pe_" + kv_, [128, 32])
                din(p + "cmp_w1_" + kv_, [4096, 256])
                din(p + "cmp_w2_" + kv_, [256, 128])
                casts += [(p + "cmp_w1_" + kv_, [4096, 256]), (p + "cmp_w2_" + kv_, [256, 128])]
        if l == 2:
            din(p + "conv_w", [128, 2 * MW // 128, 4])
            din(p + "conv_b", [128, 2 * MW // 128])
            din(p + "gate_b", [2 * HM])
            din(p + "head_norm", [MW])
        lcasts[l] = [n for n, _ in casts[ncast0:]]
    for name, shp in casts:
        dsc(name + "_bf", shp, BF16)
    dsc("xres", [D, SH], F32)
    dsc("hT", [D, SH], BF16)
    dsc("Gh", [NR * D, SH], BF16)
    dsc("ypart", [NR * D, SH], BF16)
    dsc("ymy", [D, SH], BF16)
    dsc("mixT", [FWD, S], BF16)
    dsc("qT", [MW, S], BF16)
    dsc("qmT", [MEMW, S], BF16)
    dsc("kT", [MW, S], BF16)
    dsc("vtm", [S, MW], BF16)
    dsc("otm", [S, MW], BF16)
    dsc("ktm", [S, MW], BF16)
    dsc("gtm", [S, 2 * HM], F32)
    dsc("g36", [S, HQ * 3], F32)
    out = nc.dram_tensor("outT", [D, SH], F32, kind="ExternalOutput").ap()

    with ExitStack() as stack:
        fw = FW(nc, stack)
        c = make_consts(fw)
        if "dbg_pt" in DEBUG:
            fw.dbg = {"pt": dsc("dbg_pt", [128, 512], BF16), "o": dsc("dbg_o", [128, 2, 512], F32), "mix": dsc("dbg_mix", [128, 4, FWD], BF16)}
            fw.dbg_sb = fw.sbg([128, 2, 512], F32, "dbg_sb")
        l0 = layers[0]
        ffn_names = ["l%d_w_gate" % l0, "l%d_w_up" % l0, "l%d_w_down" % l0]
        phase_cast_weights(fw, [(dr[n], dr[n + "_bf"]) for n in (["mem_w_kv"] + [n for n in lcasts[l0] if n not in ffn_names])])
        issue_casts(fw, [(dr[n], dr[n + "_bf"]) for n in ffn_names])
        MK = phase_mem(fw, c, dr["memT"], dr["mem_norm"], dr["mem_w_kv_bf"])
        x_cur = dr["xT"]
        for li, l in enumerate(layers):
            p = "l%d_" % l
            last = li == len(layers) - 1
            phase_n1(fw, c, SH, x_cur, dr[p + "norm_mix"], dr["hT"])
            if not last:
                fw.set_casts([(dr[n], dr[n + "_bf"]) for n in lcasts[layers[li + 1]]])
            RC = cc_rows(SH)
            for ci in range(D // RC):
                fw.collective("AllGather", dr["hT"][ci * RC:(ci + 1) * RC, :], dr["Gh"][ci * NR * RC:(ci + 1) * NR * RC, :], GROUPS)
            wbf = dr[p + "w_in_bf"]
            if li > 0:
                fw.barrier(wait_casts=True)
            if l == 3:
                kw = NGS * 64
                segs = [(0, MW, "FM", dr["qT"], BF16), (MW, kw, "FM", dr["kT"][0:kw, :], BF16),
                        (MW + kw, kw, "TM", dr["vtm"][:, 0:kw], BF16), (MW + 2 * kw, MEMW, "FM", dr["qmT"], BF16)]
                phase_in(fw, c, S, SH, dr["Gh"], wbf, segs)
                phase_mix_swa(fw, c, S, MK, dr["qT"], dr["kT"][0:kw, :], dr["vtm"][:, 0:kw], dr["qmT"], dr[p + "sinks"], dr["mixT"])
            elif l == 1:
                segs = [(0, MW, "FM", dr["qT"], BF16), (MW, MW, "FM", dr["kT"], BF16),
                        (2 * MW, MW, "TM", dr["vtm"], BF16), (3 * MW, MEMW, "FM", dr["qmT"], BF16)]
                phase_in(fw, c, S, SH, dr["Gh"], wbf, segs)
                phase_mix_sb(fw, c, S, MK, dr["qT"], dr["kT"], dr["vtm"], dr["qmT"], dr["mixT"])
            elif l == 0:
                gb_ = NGN * 128
                o = MW
                segs = [(0, MW, "FM", dr["qT"], BF16), (o, 3 * gb_, "FM", dr["kT"][0:3 * gb_, :], BF16),
                        (o + 3 * gb_, gb_, "TM", dr["vtm"][:, 0:gb_], BF16), (o + 4 * gb_, gb_, "FM", dr["kT"][3 * gb_:4 * gb_, :], BF16),
                        (o + 5 * gb_, gb_, "TM", dr["vtm"][:, gb_:2 * gb_], BF16), (o + 6 * gb_, HQ * 3, "TM", dr["g36"], F32),
                        (o + 6 * gb_ + HQ * 3, MEMW, "FM", dr["qmT"], BF16)]
                phase_in(fw, c, S, SH, dr["Gh"], wbf, segs)
                phase_mix_nsa(fw, c, S, MK, dr["qT"], dr["kT"], dr["vtm"], dr["g36"], dr["qmT"], dr["mixT"],
                              [dr[p + "cmp_pe_k"], dr[p + "cmp_pe_v"]], [dr[p + "cmp_w1_k_bf"], dr[p + "cmp_w1_v_bf"]],
                              [dr[p + "cmp_w2_k_bf"], dr[p + "cmp_w2_v_bf"]])
            elif l == 2:
                segs = [(0, MW, "FM", dr["qT"], BF16), (MW, MW, "FM", dr["kT"], BF16),
                        (2 * MW, MW, "TM", dr["vtm"], BF16), (3 * MW, MW, "TM", dr["otm"], BF16),
                        (4 * MW, 2 * HM, "TM", dr["gtm"], F32), (4 * MW + 2 * HM, MEMW, "FM", dr["qmT"], BF16)]
                phase_in(fw, c, S, SH, dr["Gh"], wbf, segs)
                phase_mix_mlstm(fw, c, S, MK, dr["qT"], dr["kT"], dr["ktm"], dr["vtm"], dr["otm"], dr["gtm"], dr["qmT"], dr["mixT"],
                                dr[p + "conv_w"], dr[p + "conv_b"], dr[p + "gate_b"], dr[p + "head_norm"])
            phase_op(fw, c, S, SH, dr["mixT"], dr[p + "w_out_bf"], dr["ypart"])
            for ci in range(D // RC):
                fw.collective("ReduceScatter", dr["ypart"][ci * NR * RC:(ci + 1) * NR * RC, :], dr["ymy"][ci * RC:(ci + 1) * RC, :], GROUPS)
            if li == 0:
                fw.barrier(wait_casts=True)
            phase_ffn(fw, c, SH, x_cur, dr["xres"], dr["ymy"], dr[p + "norm_ffn"],
                      dr[p + "w_gate_bf"], dr[p + "w_up_bf"], dr[p + "w_down_bf"],
                      final_g=(dr["final_norm"] if last else None), out_dram=(out if last else None))
            x_cur = dr["xres"]
    return nc


def f32c(a):
    return np.ascontiguousarray(np.asarray(a, np.float32))


def host_inputs(inp, b, r, layers, S):
    SH = S // NR
    m = {}
    m["xT"] = f32c(np.asarray(inp["x"][b], np.float32)[r * SH:(r + 1) * SH].T)
    m["memT"] = f32c(np.asarray(inp["mem"][b], np.float32).T)
    m["mem_norm"] = gl(inp["mem_norm"])
    wkv = np.asarray(inp["mem_w_kv"], np.float32)
    mh = np.arange(r * HM * 128, (r + 1) * HM * 128)
    m["mem_w_kv"] = f32c(wkv[:, np.concatenate([mh, 512 + mh])])
    m["final_norm"] = gl(inp["final_norm"])
    memq = np.arange(r * MEMW, (r + 1) * MEMW)
    mixf = np.arange(r * MW, (r + 1) * MW)
    for l in layers:
        p = "l%d_" % l
        m[p + "norm_mix"] = gl(inp[p + "norm_mix"])
        m[p + "norm_ffn"] = gl(inp[p + "norm_ffn"])
        for w in ("w_gate", "w_up", "w_down"):
            m[p + w] = f32c(inp[p + w])
        w_in = np.asarray(inp[p + "w_in"], np.float32)
        w_out = np.asarray(inp[p + "w_out"], np.float32)
        feats = mixf
        if l == 0:
            gs = np.arange(r * NGN, (r + 1) * NGN)
            kvc = [1536 + j * 256 + np.concatenate([g * 128 + np.arange(128) for g in gs]) for j in range(6)]
            cols = np.concatenate([mixf] + kvc + [3072 + np.arange(r * HQ * 3, (r + 1) * HQ * 3), 3108 + memq])
            for kv_ in ("k", "v"):
                m[p + "cmp_pe_" + kv_] = f32c(np.asarray(inp[p + "cmp_pe_" + kv_], np.float32).T)
                m[p + "cmp_w1_" + kv_] = f32c(inp[p + "cmp_w1_" + kv_])
                m[p + "cmp_w2_" + kv_] = f32c(inp[p + "cmp_w2_" + kv_])
        elif l == 1:
            cols = np.concatenate([mixf, 1536 + mixf, 3072 + mixf, 4608 + memq])
        elif l == 2:
            hsel = np.arange(r * HM, (r + 1) * HM)
            cols = np.concatenate([mixf, 1536 + mixf, 3072 + mixf, 4608 + mixf, 6144 + hsel, 6148 + hsel, 6152 + memq])
            cwf = np.asarray(inp[p + "conv_w"], np.float32)[:, np.concatenate([mixf, 1536 + mixf])]
            m[p + "conv_w"] = f32c(cwf.T.reshape(2 * MW // 128, 128, 4).transpose(1, 0, 2))
            cbf = np.asarray(inp[p + "conv_b"], np.float32)[np.concatenate([mixf, 1536 + mixf])]
            m[p + "conv_b"] = f32c(cbf.reshape(2 * MW // 128, 128).T)
            m[p + "gate_b"] = f32c(np.asarray(inp[p + "gate_b"], np.float32)[np.concatenate([hsel, 4 + hsel])])
            m[p + "head_norm"] = f32c(np.asarray(inp[p + "head_norm"], np.float32)[mixf])
        else:
            if NR == 1:
                heads = np.arange(24)
                grp = [0, 1, 2]
            elif r == 0:
                heads = np.arange(12)
                grp = [0, 1]
            else:
                heads = np.concatenate([np.arange(16, 24), np.arange(12, 16)])
                grp = [2, 1]
            feats = np.concatenate([h * 64 + np.arange(64) for h in heads])
            kc = np.concatenate([g * 64 + np.arange(64) for g in grp])
            cols = np.concatenate([feats, 1536 + kc, 1728 + kc, 1920 + memq])
            m[p + "sinks"] = f32c(np.asarray(inp[p + "sinks"], np.float32)[heads])
        m[p + "w_in"] = f32c(w_in[:, cols])
        m[p + "w_out"] = f32c(w_out[np.concatenate([feats, 1536 + memq]), :])
    return m


def run(inp, S, layers, n_cores=8):
    nc = build(S, layers)
    B = inp["x"].shape[0]
    SH = S // NR
    in_maps = []
    for i in range(n_cores):
        b, r = (i // NR) % B, i % NR
        in_maps.append(host_inputs(inp, b, r, layers, S))
    res = run_bass_kernel_spmd(nc, in_maps, core_ids=list(range(n_cores)))
    LAST["res"] = res.results
    outs = []
    for b in range(B):
        outs.append(np.concatenate([np.asarray(res.results[b * NR + r]["outT"]).T for r in range(NR)], axis=0))
    return np.stack(outs, axis=0).astype(np.float32)


def kernel(x, mem, mem_norm, mem_w_kv,
           l0_norm_mix, l0_w_in, l0_cmp_pe_k, l0_cmp_w1_k, l0_cmp_w2_k, l0_cmp_pe_v, l0_cmp_w1_v,
           l0_cmp_w2_v, l0_w_out, l0_norm_ffn, l0_w_gate, l0_w_up, l0_w_down,
           l1_norm_mix, l1_w_in, l1_w_out, l1_norm_ffn, l1_w_gate, l1_w_up, l1_w_down,
           l2_norm_mix, l2_w_in, l2_conv_w, l2_conv_b, l2_gate_b, l2_head_norm, l2_w_out,
           l2_norm_ffn, l2_w_gate, l2_w_up, l2_w_down,
           l3_norm_mix, l3_w_in, l3_sinks, l3_w_out, l3_norm_ffn, l3_w_gate, l3_w_up, l3_w_down,
           final_norm):
    inputs = dict(
        x=x, mem=mem, mem_norm=mem_norm, mem_w_kv=mem_w_kv,
        l0_norm_mix=l0_norm_mix, l0_w_in=l0_w_in, l0_cmp_pe_k=l0_cmp_pe_k, l0_cmp_w1_k=l0_cmp_w1_k, l0_cmp_w2_k=l0_cmp_w2_k,
        l0_cmp_pe_v=l0_cmp_pe_v, l0_cmp_w1_v=l0_cmp_w1_v, l0_cmp_w2_v=l0_cmp_w2_v, l0_w_out=l0_w_out, l0_norm_ffn=l0_norm_ffn,
        l0_w_gate=l0_w_gate, l0_w_up=l0_w_up, l0_w_down=l0_w_down,
        l1_norm_mix=l1_norm_mix, l1_w_in=l1_w_in, l1_w_out=l1_w_out, l1_norm_ffn=l1_norm_ffn, l1_w_gate=l1_w_gate, l1_w_up=l1_w_up,
        l1_w_down=l1_w_down,
        l2_norm_mix=l2_norm_mix, l2_w_in=l2_w_in, l2_conv_w=l2_conv_w, l2_conv_b=l2_conv_b, l2_gate_b=l2_gate_b,
        l2_head_norm=l2_head_norm, l2_w_out=l2_w_out, l2_norm_ffn=l2_norm_ffn, l2_w_gate=l2_w_gate, l2_w_up=l2_w_up, l2_w_down=l2_w_down,
        l3_norm_mix=l3_norm_mix, l3_w_in=l3_w_in, l3_sinks=l3_sinks, l3_w_out=l3_w_out, l3_norm_ffn=l3_norm_ffn, l3_w_gate=l3_w_gate,
        l3_w_up=l3_w_up, l3_w_down=l3_w_down, final_norm=final_norm)
    return run(inputs, 4096, [0, 1, 2, 3])
```
